# Optimizing a Trainium2 kernel written in Bass

```python
import math
import jax
import jax.numpy as jnp
from jax import lax
import numpy as np

D_MODEL = 1024
BATCH = 8
SEQ = 8192
DEPTH = 2
DEC_BATCH = 8
DEC_SEQ = 2048
PAST_LEN = 128

N_MIXERS = 2
N_A_LAYERS = (DEPTH + N_MIXERS - 1) // N_MIXERS
N_B_LAYERS = DEPTH // N_MIXERS

A_WINDOWS = (128, 512, 2048)
A_DILATIONS = (1, 4, 16)
A_GROUPS = len(A_WINDOWS)
A_HEADS = 8
A_HEAD_DIM = D_MODEL // A_HEADS
A_WIDTH = A_HEADS * A_HEAD_DIM

NUM_BUCKETS = 32
MAX_DISTANCE = max(A_WINDOWS) // 2

B_HEADS = 16
Q_LORA = 384
KV_LORA = 256
QK_NOPE = 64
QK_ROPE = 32
V_HEAD = 64
ROPE_THETA = 10000.0
Q_BLOCK = 128

D_FF = 2816
CONV_W = 3

ALPHA = (2.0 * DEPTH) ** 0.25
BETA = (8.0 * DEPTH) ** -0.25

LN_EPS = 1e-5
RMS_EPS = 1e-6
NEG_INF = -1e30

kernel_name = "hybrid_dilated_mla_encoder"


def layer_norm(x, g, b):
    xf = x.astype(jnp.float32)
    mu = jnp.mean(xf, axis=-1, keepdims=True)
    xc = xf - mu
    var = jnp.mean(xc * xc, axis=-1, keepdims=True)
    return (xc * lax.rsqrt(var + LN_EPS) * g.astype(jnp.float32) + b.astype(jnp.float32)).astype(x.dtype)


def rms_norm(x, g):
    xf = x.astype(jnp.float32)
    ms = jnp.mean(xf * xf, axis=-1, keepdims=True)
    return (xf * lax.rsqrt(ms + RMS_EPS) * g.astype(jnp.float32)).astype(x.dtype)


def t5_bucket(rel):
    nb = NUM_BUCKETS // 2
    max_exact = nb // 2
    ret = jnp.where(rel > 0, nb, 0)
    n = jnp.abs(rel)
    large = max_exact + (jnp.log(jnp.maximum(n, 1).astype(jnp.float32) / max_exact)
                         / math.log(MAX_DISTANCE / max_exact) * (nb - max_exact)).astype(jnp.int32)
    large = jnp.minimum(large, nb - 1)
    return ret + jnp.where(n < max_exact, n, large)


def dilated_group_attention(q, k, v, bias_tab, dilation, half):
    B, S, H, C = q.shape
    L = S // dilation
    nblk = -(-L // half)
    Lp = nblk * half

    def to_sub(t):
        t = t.reshape(B, L, dilation, H, C).transpose(0, 2, 1, 3, 4)
        return jnp.pad(t, ((0, 0), (0, 0), (0, Lp - L), (0, 0), (0, 0)))

    def neighbours(t):
        t = jnp.pad(to_sub(t), ((0, 0), (0, 0), (half, half), (0, 0), (0, 0)))
        t = t.reshape(B, dilation, nblk + 2, half, H, C)
        return jnp.concatenate([t[:, :, :-2], t[:, :, 1:-1], t[:, :, 2:]], axis=3)

    qs = to_sub(q).reshape(B, dilation, nblk, half, H, C)
    ks = neighbours(k)
    vs = neighbours(v)

    qi = jnp.arange(half)[:, None]
    kj = jnp.arange(3 * half)[None, :]
    rel = kj - half - qi
    bias = bias_tab[t5_bucket(rel * dilation)].transpose(2, 0, 1).astype(jnp.float32)
    key_pos = (jnp.arange(nblk)[:, None] - 1) * half + kj
    valid = (jnp.abs(rel) <= half)[None] & ((key_pos >= 0) & (key_pos < L))[:, None, :]

    s = jnp.einsum('brnqhc,brnkhc->brnhqk', qs, ks).astype(jnp.float32) * (C ** -0.5) + bias
    s = jnp.where(valid[:, None], s, NEG_INF)
    m = jnp.max(s, axis=-1, keepdims=True)
    p = jnp.exp(s - m)
    den = jnp.sum(p, axis=-1, keepdims=True)
    lse = (m + jnp.log(den))[..., 0]
    o = jnp.einsum('brnhqk,brnkhc->brnqhc', (p / den).astype(v.dtype), vs)

    def from_sub(t):
        t = t.reshape((B, dilation, Lp) + t.shape[4:])[:, :, :L]
        t = jnp.moveaxis(t, 1, 2)
        return t.reshape((B, S) + t.shape[3:])

    return from_sub(o), from_sub(lse.transpose(0, 1, 2, 4, 3))


def dilated_mixer(x, w_qkv, w_o, rel_bias):
    B, S, _ = x.shape
    qkv = (x @ w_qkv).reshape(B, S, A_GROUPS, 3, A_HEADS, A_HEAD_DIM)
    outs, lses = [], []
    for g in range(A_GROUPS):
        dil = A_DILATIONS[g]
        half = A_WINDOWS[g] // (2 * dil)
        o, lse = dilated_group_attention(qkv[:, :, g, 0], qkv[:, :, g, 1], qkv[:, :, g, 2],
                                         rel_bias[:, g * A_HEADS:(g + 1) * A_HEADS], dil, half)
        outs.append(o)
        lses.append(lse)
    wts = jax.nn.softmax(jnp.stack(lses, axis=0), axis=0)
    o = jnp.einsum('gbsh,gbshc->bshc', wts.astype(x.dtype), jnp.stack(outs, axis=0))
    return o.reshape(B, S, A_WIDTH) @ w_o


def rope_tables(S):
    inv = 1.0 / (ROPE_THETA ** (jnp.arange(0, QK_ROPE, 2, dtype=jnp.float32) / QK_ROPE))
    ang = jnp.arange(S, dtype=jnp.float32)[:, None] * inv[None, :]
    return jnp.cos(ang), jnp.sin(ang)


def apply_rope(t, cos, sin):
    t1, t2 = jnp.split(t, 2, axis=-1)
    cos = cos.astype(t.dtype)
    sin = sin.astype(t.dtype)
    return jnp.concatenate([t1 * cos - t2 * sin, t1 * sin + t2 * cos], axis=-1)


def mla_mixer(x, w_dkv, g_q, g_kv, w_uq, w_ukv, w_o):
    B, S, _ = x.shape
    c = x @ w_dkv
    cq = c[..., :Q_LORA]
    ckv = c[..., Q_LORA:Q_LORA + KV_LORA]
    kr = c[..., Q_LORA + KV_LORA:]
    q = (rms_norm(cq, g_q) @ w_uq).reshape(B, S, B_HEADS, QK_NOPE + QK_ROPE)
    kv = (rms_norm(ckv, g_kv) @ w_ukv).reshape(B, S, B_HEADS, QK_NOPE + V_HEAD)
    qn, qr = q[..., :QK_NOPE], q[..., QK_NOPE:]
    kn, v = kv[..., :QK_NOPE], kv[..., QK_NOPE:]
    cos, sin = rope_tables(S)
    qr = apply_rope(qr, cos[:, None], sin[:, None])
    kr = apply_rope(kr, cos, sin)
    scale = (QK_NOPE + QK_ROPE) ** -0.5
    nqb = S // Q_BLOCK
    qn_b = qn.reshape(B, nqb, Q_BLOCK, B_HEADS, QK_NOPE).transpose(1, 0, 2, 3, 4)
    qr_b = qr.reshape(B, nqb, Q_BLOCK, B_HEADS, QK_ROPE).transpose(1, 0, 2, 3, 4)

    def attend(blk):
        qn_i, qr_i = blk
        s = (jnp.einsum('bqhc,bkhc->bhqk', qn_i, kn)
             + jnp.einsum('bqhr,bkr->bhqk', qr_i, kr)).astype(jnp.float32) * scale
        p = jax.nn.softmax(s, axis=-1)
        return jnp.einsum('bhqk,bkhc->bqhc', p.astype(v.dtype), v)

    o = lax.map(attend, (qn_b, qr_b))
    o = o.transpose(1, 0, 2, 3, 4).reshape(B, S, B_HEADS * V_HEAD)
    return o @ w_o


def conv_ffn(x, w_in, conv_w, conv_b, w_out):
    h = x @ w_in
    hp = jnp.pad(h, ((0, 0), (1, 1), (0, 0)))
    h = hp[:, :-2] * conv_w[0] + hp[:, 1:-1] * conv_w[1] + hp[:, 2:] * conv_w[2] + conv_b
    a, g = jnp.split(h, 2, axis=-1)
    return (a * jax.nn.gelu(g, approximate=False)) @ w_out


def encoder(x, rel_bias, w_qkv_a, w_o_a, w_dkv_b, g_q_b, g_kv_b, w_uq_b, w_ukv_b, w_o_b,
            ffn_w_in, ffn_conv_w, ffn_conv_b, ffn_w_out, ln_g, ln_b):
    for i in range(DEPTH):
        j = i // N_MIXERS
        if i % N_MIXERS == 0:
            mix = dilated_mixer(x, w_qkv_a[j], w_o_a[j], rel_bias)
        else:
            mix = mla_mixer(x, w_dkv_b[j], g_q_b[j], g_kv_b[j], w_uq_b[j], w_ukv_b[j], w_o_b[j])
        x = layer_norm(ALPHA * x + mix, ln_g[i, 0], ln_b[i, 0])
        x = layer_norm(ALPHA * x + conv_ffn(x, ffn_w_in[i], ffn_conv_w[i], ffn_conv_b[i], ffn_w_out[i]),
                       ln_g[i, 1], ln_b[i, 1])
    return x


def setup_inputs(seed: int = 0) -> dict:
    key = jax.random.key(seed)
    ks = jax.random.split(key, 20)
    f32 = jnp.float32

    def nrm(k, shape, fan_in, gain=1.0):
        return jax.random.normal(k, shape, f32) * (gain * fan_in ** -0.5)

    x_prompt = jax.random.normal(ks[0], (BATCH, SEQ, D_MODEL), f32)
    x_sample = jax.random.normal(ks[1], (DEC_BATCH, DEC_SEQ, D_MODEL), f32)
    rel_bias = jax.random.normal(ks[2], (NUM_BUCKETS, A_GROUPS * A_HEADS), f32) * 0.5
    w_qkv_a = nrm(ks[3], (N_A_LAYERS, D_MODEL, A_GROUPS * 3 * A_HEADS * A_HEAD_DIM), D_MODEL)
    w_o_a = nrm(ks[4], (N_A_LAYERS, A_WIDTH, D_MODEL), A_WIDTH, BETA)
    w_dkv_b = nrm(ks[5], (N_B_LAYERS, D_MODEL, Q_LORA + KV_LORA + QK_ROPE), D_MODEL)
    g_q_b = 1.0 + 0.05 * jax.random.normal(ks[6], (N_B_LAYERS, Q_LORA), f32)
    g_kv_b = 1.0 + 0.05 * jax.random.normal(ks[7], (N_B_LAYERS, KV_LORA), f32)
    w_uq_b = nrm(ks[8], (N_B_LAYERS, Q_LORA, B_HEADS * (QK_NOPE + QK_ROPE)), Q_LORA)
    w_ukv_b = nrm(ks[9], (N_B_LAYERS, KV_LORA, B_HEADS * (QK_NOPE + V_HEAD)), KV_LORA)
    w_o_b = nrm(ks[10], (N_B_LAYERS, B_HEADS * V_HEAD, D_MODEL), B_HEADS * V_HEAD, BETA)
    ffn_w_in = nrm(ks[11], (DEPTH, D_MODEL, 2 * D_FF), D_MODEL)
    ffn_conv_w = nrm(ks[12], (DEPTH, CONV_W, 2 * D_FF), CONV_W)
    ffn_conv_b = 0.02 * jax.random.normal(ks[13], (DEPTH, 2 * D_FF), f32)
    ffn_w_out = nrm(ks[14], (DEPTH, D_FF, D_MODEL), D_FF, BETA)
    ln_g = 1.0 + 0.05 * jax.random.normal(ks[15], (DEPTH, 2, D_MODEL), f32)
    ln_b = 0.02 * jax.random.normal(ks[16], (DEPTH, 2, D_MODEL), f32)
    return {"x_prompt": x_prompt, "x_sample": x_sample, "rel_bias": rel_bias,
            "w_qkv_a": w_qkv_a, "w_o_a": w_o_a, "w_dkv_b": w_dkv_b, "g_q_b": g_q_b,
            "g_kv_b": g_kv_b, "w_uq_b": w_uq_b, "w_ukv_b": w_ukv_b, "w_o_b": w_o_b,
            "ffn_w_in": ffn_w_in, "ffn_conv_w": ffn_conv_w, "ffn_conv_b": ffn_conv_b,
            "ffn_w_out": ffn_w_out, "ln_g": ln_g, "ln_b": ln_b}


def reference(x_prompt, x_sample, rel_bias, w_qkv_a, w_o_a, w_dkv_b, g_q_b, g_kv_b, w_uq_b,
              w_ukv_b, w_o_b, ffn_w_in, ffn_conv_w, ffn_conv_b, ffn_w_out, ln_g, ln_b):
    y_prompt = encoder(x_prompt, rel_bias, w_qkv_a, w_o_a, w_dkv_b, g_q_b, g_kv_b, w_uq_b, w_ukv_b,
                       w_o_b, ffn_w_in, ffn_conv_w, ffn_conv_b, ffn_w_out, ln_g, ln_b)
    y_sample = encoder(x_sample, rel_bias, w_qkv_a, w_o_a, w_dkv_b, g_q_b, g_kv_b, w_uq_b, w_ukv_b,
                       w_o_b, ffn_w_in, ffn_conv_w, ffn_conv_b, ffn_w_out, ln_g, ln_b)
    return (y_prompt, y_sample)
```

```python
import numpy as np
import concourse.bass as bass
import concourse.mybir as mybir
from concourse.bass_utils import run_bass_kernel_spmd

F32 = mybir.dt.float32
BF16 = mybir.dt.bfloat16
AF = mybir.ActivationFunctionType
ALU = mybir.AluOpType
AX = mybir.AxisListType


class Buf:
    __slots__ = ("name", "writers", "readers", "prev_readers", "psum")

    def __init__(self, name, psum=False):
        self.name = name
        self.psum = psum
        self.writers = []
        self.readers = []
        self.prev_readers = []


class Op:
    __slots__ = ("eng", "fn", "deps", "is_dma", "need_inc", "sem", "val", "grp", "idx", "semkey", "cost", "done", "cons", "barrier")

    def __init__(self, eng, fn, is_dma, semkey):
        self.cost = 0.0
        self.barrier = False
        self.done = None
        self.cons = None
        self.eng = eng
        self.fn = fn
        self.deps = []
        self.is_dma = is_dma
        self.need_inc = False
        self.sem = None
        self.val = 0
        self.grp = None
        self.semkey = semkey


EPOCH = 30000


class Sched:
    ENG = ("pe", "act", "dve", "pool", "sp")

    def __init__(self, nc, same_engine_sync=True):
        self.nc = nc
        self.ops = []
        self.last = {e: None for e in self.ENG}
        self.dma_since_barrier = []
        self.bounds = []
        self.same_engine_sync = same_engine_sync
        self.engobj = {"pe": nc.tensor, "act": nc.scalar, "dve": nc.vector, "pool": nc.gpsimd, "sp": nc.sync}

    DEFCOST = {"pe": 0.3, "act": 0.5, "dve": 0.5, "pool": 1.0, "sp": 0.05}

    class _Probe:
        def __init__(self, eng):
            self.eng = eng
            self.cost = 0.0

        def __getattr__(self, name):
            def call(*a, **k):
                F = 1
                for v in list(a) + list(k.values()):
                    sh = getattr(v, "shape", None)
                    if sh is not None and len(sh) >= 1:
                        f = 1
                        for d in sh[1:]:
                            f *= d
                        if f > F:
                            F = f
                e = self.eng
                if e == "pe":
                    self.cost += max(F, 64) / 1950.0 + 0.02
                elif e == "dve":
                    self.cost += 0.08 + F / 960.0
                elif e == "act":
                    self.cost += 0.13 + F / 1400.0
                elif e == "pool":
                    self.cost += 0.1 + F / 430.0
                else:
                    self.cost += 0.05
                return None
            return call

    def probe_cost(self, eng, fn):
        try:
            p = Sched._Probe(eng)
            fn(p)
            return p.cost if p.cost > 0 else self.DEFCOST[eng]
        except Exception:
            return self.DEFCOST[eng]

    def op(self, eng, fn, reads=(), writes=(), appends=(), dma=False, semkey=None, cost=None):
        o = Op(eng, fn, dma, semkey)
        o.cost = (0.05 if dma else self.probe_cost(eng, fn)) if cost is None else cost
        deps = o.deps
        for b in reads:
            for w in b.writers:
                deps.append((w, "raw"))
            if b.psum:
                for r in b.readers:
                    if r.eng != eng:
                        deps.append((r, "raw"))
        for b in writes:
            for w in b.writers:
                deps.append((w, "waw"))
            for r in b.readers:
                deps.append((r, "war"))
            for r in b.prev_readers:
                deps.append((r, "war"))
        for b in appends:
            for r in b.readers:
                deps.append((r, "war"))
            for r in b.prev_readers:
                deps.append((r, "war"))
            for w in b.writers:
                if w.eng == eng:
                    deps.append((w, "ord"))
        for b in reads:
            b.readers.append(o)
        for b in writes:
            b.prev_readers = list(b.readers)
            b.readers = []
            b.writers = [o]
        for b in appends:
            b.writers.append(o)
        if dma:
            if semkey is None:
                raise ValueError("dma op needs semkey")
            self.dma_since_barrier.append(o)
        self.ops.append(o)
        self.last[eng] = o
        return o

    def fence(self, buf, eng="sp"):
        o = Op(eng, lambda e: e.nop(), False, None)
        for w in buf.writers:
            o.deps.append((w, "raw"))
        buf.writers = [o]
        self.ops.append(o)
        self.last[eng] = o
        return o

    def barrier(self):
        prods = [o for o in self.last.values() if o is not None] + list(self.dma_since_barrier)
        self.dma_since_barrier = []
        self.bounds.append(len(self.ops))
        for e in self.ENG:
            o = Op(e, lambda en: en.nop(), False, None)
            o.barrier = True
            o.cost = 0.05
            for p in prods:
                o.deps.append((p, "raw"))
            self.ops.append(o)
            self.last[e] = o

    def reorder(self, window=64):
        import heapq
        DLAT = 3.0
        out = []
        bounds = [0] + list(self.bounds) + [len(self.ops)]
        last_by_eng = {}
        for a, b in zip(bounds, bounds[1:]):
            seg = self.ops[a:b]
            if not seg:
                continue
            for o in seg:
                if o.barrier:
                    o.deps = [(p, k) for (p, k) in o.deps if p.is_dma] + [(p, "raw") for p in last_by_eng.values()]
            inseg = set(id(o) for o in seg)
            q = {e: [] for e in self.ENG}
            for o in seg:
                o.done = None
                o.cons = set()
                q[o.eng].append(o)
            for o in seg:
                for (p, kind) in o.deps:
                    if id(p) in inseg:
                        p.cons.add(o.eng)
            ptr = {e: 0 for e in self.ENG}
            sched = set()
            free_at = {e: 0.0 for e in self.ENG}
            best = {e: None for e in self.ENG}

            def find(e):
                lst = q[e]
                i = ptr[e]
                n = len(lst)
                while i < n and id(lst[i]) in sched:
                    i += 1
                ptr[e] = i
                bs, bo = None, None
                cnt = 0
                j = i
                while j < n and cnt < window:
                    o = lst[j]
                    j += 1
                    if id(o) in sched:
                        continue
                    cnt += 1
                    st = free_at[e]
                    ok = True
                    for (p, kind) in o.deps:
                        if id(p) not in inseg:
                            continue
                        if kind == "ord" and p.eng != e:
                            continue
                        if p.done is None:
                            ok = False
                            break
                        t = p.done if p.eng != e or p.is_dma else p.done - 0.0
                        if p.eng == e and not p.is_dma:
                            t = 0.0
                        if t > st:
                            st = t
                    if ok and (bs is None or st < bs - 1e-9):
                        bs, bo = st, o
                        if st <= free_at[e] + 1e-9:
                            break
                best[e] = (bs, bo) if bo is not None else None
            for e in self.ENG:
                find(e)
            nleft = len(seg)
            newseg = []
            while nleft:
                ce, cs = None, None
                for e in self.ENG:
                    if best[e] is not None and (cs is None or best[e][0] < cs):
                        ce, cs = e, best[e][0]
                if ce is None:
                    raise RuntimeError("scheduler stuck (dependency cycle or window too small)")
                o = best[ce][1]
                sched.add(id(o))
                free_at[ce] = cs + o.cost
                o.done = cs + (DLAT if o.is_dma else o.cost + 1.2)
                newseg.append((cs, len(newseg), o))
                nleft -= 1
                find(ce)
                for e in o.cons:
                    if e != ce:
                        find(e)
            newseg.sort(key=lambda x: (x[0], x[1]))
            for (_, _, o) in newseg:
                out.append(o)
                if not o.is_dma:
                    last_by_eng[o.eng] = o
        return out

    def emit(self):
        nc = self.nc
        import os as _os
        if _os.environ.get("MK_NOREORDER", "") != "1":
            self.ops = self.reorder()
        for o in self.ops:
            for (p, kind) in o.deps:
                if kind == "ord":
                    continue
                if p.is_dma:
                    continue
                if p.eng == o.eng and not o.is_dma and not (kind == "raw" and self.same_engine_sync and o.eng != "pe"):
                    continue
                p.need_inc = True
        cnt = {e: 0 for e in self.ENG}
        sems = {}

        def get_sem(key):
            if key not in sems:
                sems[key] = nc.alloc_semaphore(name="s_%s" % (str(key).replace(" ", "_"),))
            return sems[key]

        dma_cnt = {}
        dma_grp = {}
        for o in self.ops:
            if o.is_dma:
                k = ("dma", o.semkey)
                dma_cnt[k] = dma_cnt.get(k, 0) + 16
                o.sem = k
                o.val = dma_cnt[k]
            elif o.need_inc:
                c = cnt[o.eng]
                cnt[o.eng] = c + 1
                o.sem = (o.eng, c // EPOCH)
                o.val = c % EPOCH + 1
        self.stats = dict(cnt)
        self.stats["n_ops"] = len(self.ops)
        waited = {e: {} for e in self.ENG}
        n_wait = 0
        for o in self.ops:
            E = self.engobj[o.eng]
            wd = waited[o.eng]
            need = {}
            for (p, kind) in o.deps:
                if kind == "ord":
                    continue
                if not p.is_dma:
                    if p.eng == o.eng and not o.is_dma and not (kind == "raw" and self.same_engine_sync and o.eng != "pe"):
                        continue
                k, v = p.sem, p.val
                if v > wd.get(k, 0) and v > need.get(k, 0):
                    need[k] = v
            for k, v in need.items():
                E.wait_ge(get_sem(k), v)
                wd[k] = v
                n_wait += 1
            ins = o.fn(E)
            if o.is_dma:
                ins.then_inc(get_sem(o.sem), 16)
            elif o.need_inc:
                ins.then_inc(get_sem(o.sem), 1)
        self.stats["n_wait"] = n_wait
        self.stats["n_sems"] = len(sems)
        return self.stats


from contextlib import ExitStack
import math

DM = 1024
DFF = 2816
DILS = (1, 4, 16)
ALPHA = (2.0 * 2) ** 0.25
LN_EPS = 1e-5
RMS_EPS = 1e-6
NEGV = -30000.0
SMAX = 8192


class Ring:
    def __init__(self, slots):
        self.slots = slots
        self.i = 0

    def next(self):
        s = self.slots[self.i % len(self.slots)]
        self.i += 1
        return s


def bcast_last(ap, n):
    return bass.AP(tensor=ap.tensor, offset=ap.offset, ap=[list(x) for x in ap.ap] + [[0, n]])


def bcast_mid(ap, n):
    a = [list(x) for x in ap.ap]
    return bass.AP(tensor=ap.tensor, offset=ap.offset, ap=[a[0], [0, n]] + a[1:])


class MK:
    def __init__(self, seqs, debug=False):
        self.seqs = seqs
        self.debug = debug
        nc = self.nc = bass.Bass("TRN2", target_bir_lowering=False)
        self.S = Sched(nc)
        self.es = None
        D = lambda n, s, dt=F32: nc.dram_tensor(n, s, dt, kind="ExternalInput").ap()
        self.x = [D("x%d" % i, [s, DM]) for i, s in enumerate(seqs)]
        self.y = [nc.dram_tensor("y%d" % i, [s, DM], F32, kind="ExternalOutput").ap() for i, s in enumerate(seqs)]
        self.rel_bias = D("rel_bias", [32, 24])
        self.w_qkv = D("w_qkv_a", [1, DM, 9216])
        self.w_o_a = D("w_o_a", [1, DM, DM])
        self.w_dkv = D("w_dkv_b", [1, DM, 672])
        self.g_q = D("g_q_b", [1, 384])
        self.g_kv = D("g_kv_b", [1, 256])
        self.w_uq = D("w_uq_b", [1, 384, 1536])
        self.w_ukv = D("w_ukv_b", [1, 256, 2048])
        self.w_o_b = D("w_o_b", [1, DM, DM])
        self.ffn_w_in = D("ffn_w_in", [2, DM, 2 * DFF])
        self.ffn_conv_w = D("ffn_conv_w", [2, 3, 2 * DFF])
        self.ffn_conv_b = D("ffn_conv_b", [2, 2 * DFF])
        self.ffn_w_out = D("ffn_w_out", [2, DFF, DM])
        self.ln_g = D("ln_g", [2, 2, DM])
        self.ln_b = D("ln_b", [2, 2, DM])
        self.c_onehot = D("c_onehot", [3, 32, 384])
        self.c_negv = D("c_negv", [3, 384])
        self.c_rope = D("c_rope", [2, 32, SMAX])
        T = lambda n, s, dt: (nc.dram_tensor(n, s, dt, kind="ExternalOutput").ap() if (debug and n[:2] in ("X1", "X2", "X3", "OG", "BT", "OM", "UT", "QT", "KT", "V2")) else nc.dram_tensor(n, s, dt).ap())
        self.PADS = T("PADS", [24, 384], F32)
        self.bPADS = Buf("PADS")
        self.BT = T("BT", [24, 128, 256], F32)
        self.bBT = Buf("BT")
        self.OG = [[T("OG%d_%d" % (g, i), [s, 1032], F32) for i, s in enumerate(seqs)] for g in range(3)]
        self.bOG = Buf("OG")
        self.X1 = [T("X1_%d" % i, [s, DM], F32) for i, s in enumerate(seqs)]
        self.X1T = [T("X1T_%d" % i, [8, 128, s], BF16) for i, s in enumerate(seqs)]
        self.X2 = [T("X2_%d" % i, [s, DM], F32) for i, s in enumerate(seqs)]
        self.X2T = [T("X2T_%d" % i, [8, 128, s], BF16) for i, s in enumerate(seqs)]
        self.X3 = [T("X3_%d" % i, [s, DM], F32) for i, s in enumerate(seqs)]
        self.X3T = [T("X3T_%d" % i, [8, 128, s], BF16) for i, s in enumerate(seqs)]
        self.UT = [T("UT_%d" % i, [22, 128, s], BF16) for i, s in enumerate(seqs)]
        self.QTd = [T("QT_%d" % i, [16, 96, s], BF16) for i, s in enumerate(seqs)]
        self.KTd = [T("KT_%d" % i, [16, 96, s], BF16) for i, s in enumerate(seqs)]
        self.V2 = [T("V2_%d" % i, [16, 128, s // 128, 65], BF16) for i, s in enumerate(seqs)]
        self.OM = [T("OM_%d" % i, [s, DM], BF16) for i, s in enumerate(seqs)]
        self.bX1 = Buf("X1"); self.bX1T = Buf("X1T"); self.bX2 = Buf("X2"); self.bX2T = Buf("X2T")
        self.bX3 = Buf("X3"); self.bX3T = Buf("X3T"); self.bUT = Buf("UT"); self.bQKV = Buf("QKV"); self.bOM = Buf("OM")
        self.bY = Buf("Y")
        self.uid = 0

    def sb(self, name, shape, dt):
        self.uid += 1
        t = self.es.enter_context(self.nc.sbuf_tensor("%s_%d" % (name, self.uid), shape, dt))
        return t, Buf(name)

    def ps(self, name, shape, dt=F32):
        self.uid += 1
        t = self.es.enter_context(self.nc.psum_tensor("%s_%d" % (name, self.uid), shape, dt))
        return t, Buf(name, psum=True)

    def dma(self, out, in_, reads=(), writes=(), appends=(), key=None, eng="sp", slow=False):
        if slow:
            fn = lambda e: e.dma_start(out=out, in_=in_, allow_slow_non_contiguous=True)
        else:
            fn = lambda e: e.dma_start(out=out, in_=in_)
        return self.S.op(eng, fn, reads=reads, writes=writes, appends=appends, dma=True, semkey=key)

    def load_w(self, dst_t, dst_b, key, pieces):
        for i, (o, s) in enumerate(pieces):
            if i == 0:
                self.dma(o, s, writes=[dst_b], key=key, eng="pool")
            else:
                self.dma(o, s, appends=[dst_b], key=key, eng="pool")

    def consts(self):
        S = self.S
        self.ident, self.bident = self.sb("ident", [128, 128], BF16)
        self.identf, self.bidentf = self.sb("identf", [128, 128], F32)
        self.Jf, self.bJf = self.sb("Jf", [128, 128], F32)
        self.ones, self.bones = self.sb("ones", [128, 2], BF16)
        identf, ident, Jf, ones = self.identf, self.ident, self.Jf, self.ones
        S.op("pool", lambda e: e.memset(identf[:], 1.0), writes=[self.bidentf])
        S.op("pool", lambda e: e.affine_select(out=identf[:], in_=identf[:], pattern=[[-1, 128]], compare_op=ALU.is_equal,
                                               fill=0.0, base=0, channel_multiplier=1), reads=[self.bidentf], writes=[self.bidentf])
        S.op("dve", lambda e: e.tensor_copy(out=ident[:], in_=identf[:]), reads=[self.bidentf], writes=[self.bident])
        S.op("pool", lambda e: e.memset(Jf[:], 1.0), writes=[self.bJf])
        S.op("pool", lambda e: e.affine_select(out=Jf[:], in_=Jf[:], pattern=[[1, 128]], compare_op=ALU.is_equal,
                                               fill=0.0, base=-127, channel_multiplier=1), reads=[self.bJf], writes=[self.bJf])
        S.op("pool", lambda e: e.memset(ones[:], 1.0), writes=[self.bones])

    def ln_consts(self, li, lj):
        g, bg = self.sb("lng", [128, DM], F32)
        b, bb = self.sb("lnb", [128, DM], F32)
        gs = bass.AP(tensor=self.ln_g.tensor, offset=(li * 2 + lj) * DM, ap=[[0, 128], [1, DM]])
        bs = bass.AP(tensor=self.ln_b.tensor, offset=(li * 2 + lj) * DM, ap=[[0, 128], [1, DM]])
        self.dma(g[:], gs, writes=[bg], key="lng")
        self.dma(b[:], bs, writes=[bb], key="lnb")
        return (g, bg, b, bb)

    def phase_bias(self):
        S, nc = self.S, self.nc
        with ExitStack() as es:
            self.es = es
            rb, brb = self.sb("rb", [32, 24], F32)
            oh, boh = self.sb("oh", [32, 3, 384], F32)
            ng, bng = self.sb("ng", [8, 3, 384], F32)
            pads, bpads = self.sb("pads", [8, 3, 384], F32)
            pp, bpp = self.ps("pp", [128, 512])
            self.dma(rb[:], self.rel_bias[:, :], writes=[brb], key="rb")
            self.dma(oh[:], self.c_onehot.rearrange("g b n -> b g n"), writes=[boh], key="oh")
            self.dma(ng[:], bass.AP(tensor=self.c_negv.tensor, offset=0, ap=[[0, 8], [384, 3], [1, 384]]), writes=[bng], key="ng")
            for g in range(3):
                S.op("pe", lambda e, g=g: e.matmul(pp[0:8, 0:384], rb[:, g * 8:(g + 1) * 8], oh[:, g, :], start=True, stop=True),
                     reads=[brb, boh], writes=[bpp])
                S.op("dve", lambda e, g=g: e.tensor_tensor(out=pads[:, g, :], in0=pp[0:8, 0:384], in1=ng[:, g, :], op=ALU.add),
                     reads=[bpp, bng], **(dict(writes=[bpads]) if g == 0 else dict(appends=[bpads])))
            self.dma(self.PADS.rearrange("(g h) n -> h g n", g=3), pads[:], reads=[bpads], writes=[self.bPADS], key="pads_o")
            hk = Ring([self.sb("hk%d" % i, [128, 256], F32) for i in range(2)])
            tb = Ring([self.sb("tb%d" % i, [128, 256], F32) for i in range(2)])
            pt = Ring([self.ps("ptz%d" % i, [128, 512]) for i in range(2)])
            for hh in range(24):
                h_t, h_b = hk.next()
                t_t, t_b = tb.next()
                p_t, p_b = pt.next()
                src = bass.AP(tensor=self.PADS.tensor, offset=hh * 384, ap=[[1, 128], [1, 256]])
                self.dma(h_t[:], src, reads=[self.bPADS], writes=[h_b], key="hk%d" % (hh % 2))
                S.op("pe", lambda e, p_t=p_t, h_t=h_t: e.matmul(p_t[:, 0:256], self.Jf[:], h_t[:], start=True, stop=True),
                     reads=[self.bJf, h_b], writes=[p_b])
                S.op("dve", lambda e, p_t=p_t, t_t=t_t: e.tensor_copy(out=t_t[:], in_=p_t[:, 0:256]), reads=[p_b], writes=[t_b])
                self.dma(self.BT[hh], t_t[:], reads=[t_b], appends=[self.bBT], key="tb%d" % (hh % 2))
            S.fence(self.bBT)
        S.barrier()

    def phase_A(self, g):
        S, nc = self.S, self.nc
        d = DILS[g]
        scale = 128 ** -0.5
        with ExitStack() as es:
            self.es = es
            Wg, bWg = self.sb("Wg", [128, 8, 3072], BF16)
            self.load_w(Wg, bWg, "Wg", [(Wg[:, k, :], self.w_qkv[0, k * 128:(k + 1) * 128, g * 3072:(g + 1) * 3072]) for k in range(8)])
            bt, bbt = self.sb("biasT", [128, 8, 256], F32)
            self.dma(bt[:], self.BT[g * 8:(g + 1) * 8].rearrange("h k q -> k h q"), reads=[self.bBT], writes=[bbt], key="biasT")
            xin = Ring([self.sb("xin%d" % i, [128, DM], BF16) for i in range(3)])
            for (t, b) in xin.slots:
                S.op("pool", lambda e, t=t: e.memset(t[:], 0.0), writes=[b])
            xT = Ring([self.sb("xT%d" % i, [128, 8, 640], BF16) for i in range(2)])
            QT = Ring([self.sb("QT%d" % i, [128, 8, 512], BF16) for i in range(2)])
            KT = Ring([self.sb("KT%d" % i, [128, 8, 640], BF16) for i in range(2)])
            VA = Ring([self.sb("VA%d" % i, [128, 5, 8, 129], BF16) for i in range(2)])
            for (t, b) in VA.slots:
                S.op("pool", lambda e, t=t: e.memset(t[:], 1.0), writes=[b])
            PT = Ring([self.sb("PT%d" % i, [128, 256], BF16) for i in range(4)])
            SC = Ring([self.sb("SC%d" % i, [128, 256], F32) for i in range(4)])
            OS = Ring([self.sb("OS%d" % i, [128, 4, 1032], F32) for i in range(2)])
            RC = Ring([self.sb("RC%d" % i, [128, 4], F32) for i in range(4)])
            tp = Ring([self.ps("tp%d" % i, [128, 8, 128], BF16) for i in range(2)])
            pj = Ring([self.ps("pj%d" % i, [128, 512]) for i in range(2)])
            pO = Ring([self.ps("pO%d" % i, [128, 2, 512]) for i in range(2)])
            DN = Ring([self.sb("DN%d" % i, [128, 4], F32) for i in range(4)])
            ident, bident, ones, bones = self.ident, self.bident, self.ones, self.bones
            for si, Sq in enumerate(self.seqs):
                L = Sq // d
                nq = min(512, L)
                nsub = nq // 128
                nch = nsub + 1
                Wd = nq + 128
                xsub = self.x[si].rearrange("(l d) f -> d l f", d=d)
                ogsub = self.OG[g][si].rearrange("(l d) f -> d l f", d=d)
                prev = {}

                def do_tile(r, T, L=L, nq=nq, nsub=nsub, nch=nch, Wd=Wd, xsub=xsub, ogsub=ogsub, prev=prev):
                    if True:
                        l0 = T * nq
                        xT_t, xT_b = xT.next()
                        carry = T > 0
                        if carry:
                            pxT_t, pxT_b, pKT_t, pKT_b, pVA_t, pVA_b = prev["t"]
                            S.op("pool", lambda e: e.tensor_copy(out=xT_t[:, :, 0:128], in_=pxT_t[:, :, nq:nq + 128]), reads=[pxT_b], writes=[xT_b])
                        vparts = []
                        for c in range(nch):
                            lo = l0 - 64 + 128 * c
                            vlo, vhi = max(lo, 0), min(lo + 128, L)
                            p0, p1 = vlo - lo, vhi - lo
                            vparts.append((p0, p1))
                            if carry and c == 0:
                                continue
                            xi_t, xi_b = xin.next()
                            self.dma(xi_t[p0:p1, :], xsub[r, vlo:vhi, :], writes=[xi_b], key=xi_b.name, eng="pool")
                            tp_t, tp_b = tp.next()

                            def f_tr(e, tp_t=tp_t, xi_t=xi_t):
                                for k in range(8):
                                    ins = e.transpose(tp_t[:, k, :], xi_t[:, k * 128:(k + 1) * 128], ident[:])
                                return ins
                            S.op("pe", f_tr, reads=[xi_b, bident], writes=[tp_b])
                            kw = dict(writes=[xT_b]) if (c == 0 and not carry) else dict(appends=[xT_b])
                            S.op("act" if c % 2 else "dve",
                                 (lambda e, tp_t=tp_t, c=c: e.copy(out=xT_t[:, :, c * 128:(c + 1) * 128], in_=tp_t[:])) if c % 2 else
                                 (lambda e, tp_t=tp_t, c=c: e.tensor_copy(out=xT_t[:, :, c * 128:(c + 1) * 128], in_=tp_t[:])),
                                 reads=[tp_b], **kw)
                        QT_t, QT_b = QT.next()
                        KT_t, KT_b = KT.next()
                        VA_t, VA_b = VA.next()
                        ev = 0
                        for h in range(8):
                            pj_t, pj_b = pj.next()

                            def f_q(e, pj_t=pj_t, h=h):
                                for k in range(8):
                                    ins = e.matmul(pj_t[:, 0:nq], Wg[:, k, h * 128:(h + 1) * 128], xT_t[:, k, 64:64 + nq], start=(k == 0), stop=(k == 7))
                                return ins
                            S.op("pe", f_q, reads=[bWg, xT_b], writes=[pj_b])
                            kw = dict(writes=[QT_b]) if h == 0 else dict(appends=[QT_b])
                            S.op("act", lambda e, pj_t=pj_t, h=h: e.activation(out=QT_t[:, h, 0:nq], in_=pj_t[:, 0:nq], func=AF.Copy, scale=scale),
                                 reads=[pj_b], **kw)
                        if carry:
                            S.op("pool", lambda e: e.tensor_copy(out=KT_t[:, :, 0:128], in_=pKT_t[:, :, nq:nq + 128]), reads=[pKT_b], writes=[KT_b])
                            S.op("pool", lambda e: e.tensor_copy(out=VA_t[:, 0, :, :], in_=pVA_t[:, nch - 1, :, :]), reads=[pVA_b], writes=[VA_b])
                            kspans = [(128, nq)]
                        else:
                            kspans = [(0, Wd // 2), (Wd // 2, Wd // 2)]
                        for h in range(8):
                            for (k0, hw) in kspans:
                                pj_t, pj_b = pj.next()

                                def f_k(e, pj_t=pj_t, h=h, k0=k0, hw=hw):
                                    for k in range(8):
                                        ins = e.matmul(pj_t[:, 0:hw], Wg[:, k, 1024 + h * 128:1024 + (h + 1) * 128],
                                                       xT_t[:, k, k0:k0 + hw], start=(k == 0), stop=(k == 7))
                                    return ins
                                S.op("pe", f_k, reads=[bWg, xT_b], writes=[pj_b])
                                kw = dict(writes=[KT_b]) if (h == 0 and k0 == 0) else dict(appends=[KT_b])
                                S.op("dve", lambda e, pj_t=pj_t, h=h, k0=k0, hw=hw: e.tensor_copy(out=KT_t[:, h, k0:k0 + hw], in_=pj_t[:, 0:hw]),
                                     reads=[pj_b], **kw)
                        prev["t"] = (xT_t, xT_b, KT_t, KT_b, VA_t, VA_b)
                        for c in range(1 if carry else 0, nch):
                            for j in range(2):
                                pj_t, pj_b = pj.next()

                                def f_v(e, pj_t=pj_t, c=c, j=j):
                                    for k in range(8):
                                        ins = e.matmul(pj_t[:, :], xT_t[:, k, c * 128:(c + 1) * 128], Wg[:, k, 2048 + j * 512:2048 + (j + 1) * 512],
                                                       start=(k == 0), stop=(k == 7))
                                    return ins
                                S.op("pe", f_v, reads=[bWg, xT_b], writes=[pj_b])
                                kw = dict(writes=[VA_b]) if (c == 0 and j == 0) else dict(appends=[VA_b])
                                if (c * 2 + j) % 2:
                                    S.op("act", lambda e, pj_t=pj_t, c=c, j=j: e.copy(out=VA_t[:, c, 4 * j:4 * j + 4, 0:128], in_=pj_t[:, :].rearrange("p (h f) -> p h f", h=4)),
                                         reads=[pj_b], **kw)
                                else:
                                    S.op("dve", lambda e, pj_t=pj_t, c=c, j=j: e.tensor_copy(out=VA_t[:, c, 4 * j:4 * j + 4, 0:128], in_=pj_t[:, :].rearrange("p (h f) -> p h f", h=4)),
                                         reads=[pj_b], **kw)
                        OS_t, OS_b = OS.next()
                        items = [(h, c) for h in range(8) for c in range(nch)]
                        LAG = 2
                        pend = {}
                        pOcur = {}
                        first_os = [True]

                        def emit_qk(h, c):
                            pT_t, pT_b = PT.next()
                            sc_t, sc_b = SC.next()
                            ps_t, ps_b = pj.next()
                            if c == 0:
                                q0, q1, b0, b1 = 0, 128, 128, 256
                            elif c == nch - 1:
                                q0, q1, b0, b1 = (c - 1) * 128, c * 128, 0, 128
                            else:
                                q0, q1, b0, b1 = (c - 1) * 128, (c + 1) * 128, 0, 256
                            S.op("pe", lambda e: e.matmul(ps_t[:, b0:b1], KT_t[:, h, c * 128:(c + 1) * 128], QT_t[:, h, q0:q1], start=True, stop=True),
                                 reads=[KT_b, QT_b], writes=[ps_b])
                            S.op("dve", lambda e: e.tensor_tensor(out=sc_t[:, b0:b1], in0=ps_t[:, b0:b1], in1=bt[:, h, b0:b1], op=ALU.add),
                                 reads=[ps_b, bbt], writes=[sc_b])
                            S.op("act", lambda e: e.activation(out=pT_t[:, b0:b1], in_=sc_t[:, b0:b1], func=AF.Exp), reads=[sc_b], writes=[pT_b])
                            p0, p1 = vparts[c]
                            if (p0, p1) != (0, 128):
                                i0, i1 = (0, p0) if p0 > 0 else (p1, 128)
                                S.op("pool", lambda e: e.memset(pT_t[i0:i1, b0:b1], 0.0), reads=[pT_b], writes=[pT_b], cost=0.2)
                            pend[(h, c)] = (pT_t, pT_b)

                        def emit_pv(h, c):
                            pT_t, pT_b = pend.pop((h, c))
                            if c == 0:
                                pOcur[h] = pO.next()
                            pO_t, pO_b = pOcur[h]
                            reg = lambda i: pO_t[:, i // 2, (i % 2) * 129:(i % 2) * 129 + 129]
                            p0, p1 = 0, 128

                            def f_pv(e):
                                ins = None
                                if c >= 1:
                                    i = c - 1
                                    ins = e.matmul(reg(i), pT_t[p0:p1, 0:128], VA_t[p0:p1, c, h, :], start=False, stop=True)
                                if c <= nch - 2:
                                    i = c
                                    ins = e.matmul(reg(i), pT_t[p0:p1, 128:256], VA_t[p0:p1, c, h, :], start=True, stop=False)
                                return ins
                            kw = dict(writes=[pO_b]) if c == 0 else dict(appends=[pO_b])
                            S.op("pe", f_pv, reads=[pT_b, VA_b, bones], **kw)
                            if c == nch - 1:
                                rc_t, rc_b = RC.next()
                                dn_t, dn_b = DN.next()
                                nbk, nj = (nsub + 1) // 2, min(nsub, 2)
                                pv4 = pO_t[:, 0:nbk, 0:258].rearrange("p b (j f) -> p b j f", f=129)[:, :, 0:nj, :]
                                v3 = lambda ap: ap.rearrange("p (b j) -> p b j", j=nj)
                                S.op("dve", lambda e: e.tensor_copy(out=v3(dn_t[:, 0:nsub]), in_=pv4[:, :, :, 128]),
                                     reads=[pO_b], writes=[dn_b])
                                S.op("dve", lambda e: e.reciprocal(out=rc_t[:, 0:nsub], in_=dn_t[:, 0:nsub]), reads=[dn_b], writes=[rc_b])
                                kw2 = dict(writes=[OS_b]) if first_os[0] else dict(appends=[OS_b])
                                first_os[0] = False
                                S.op("act", lambda e: e.activation(out=OS_t[:, 0:nsub, 1024 + h], in_=dn_t[:, 0:nsub], func=AF.Ln),
                                     reads=[dn_b], **kw2)
                                S.op("dve", lambda e: e.tensor_tensor(out=OS_t[:, 0:nsub, h * 128:(h + 1) * 128].rearrange("p (b j) f -> p b j f", j=nj),
                                                                      in0=pv4[:, :, :, 0:128],
                                                                      in1=bcast_last(v3(rc_t[:, 0:nsub]), 128), op=ALU.mult),
                                     reads=[pO_b, rc_b], appends=[OS_b])
                        for j in range(len(items) + LAG):
                            if j < len(items):
                                emit_qk(*items[j])
                            if j >= LAG:
                                emit_pv(*items[j - LAG])
                        for i in range(nsub):
                            self.dma(ogsub[r, l0 + 128 * i:l0 + 128 * (i + 1), :], OS_t[:, i, :], reads=[OS_b], appends=[self.bOG], key=OS_b.name)
                for r in range(d):
                    for T in range(L // nq):
                        do_tile(r, T)
            S.fence(self.bOG)
        S.barrier()

    def emit_ln(self, z_t, z_b, lnc, o_t, o_b, tmp):
        S = self.S
        g_t, g_b, b_t, b_b = lnc
        st_t, st_b, mv_t, mv_b, rs_t, rs_b, xn_t, xn_b = tmp

        def f_st(e):
            e.bn_stats(st_t[:, 0, :], z_t[:, 0:512])
            return e.bn_stats(st_t[:, 1, :], z_t[:, 512:1024])
        S.op("dve", f_st, reads=[z_b], writes=[st_b])
        S.op("dve", lambda e: e.bn_aggr(mv_t[:], st_t[:]), reads=[st_b], writes=[mv_b])
        S.op("dve", lambda e: e.tensor_scalar_add(out=rs_t[:], in0=mv_t[:, 1:2], scalar1=LN_EPS), reads=[mv_b], writes=[rs_b])
        S.op("act", lambda e: e.sqrt(out=rs_t[:], in_=rs_t[:]), reads=[rs_b], writes=[rs_b])
        S.op("dve", lambda e: e.reciprocal(out=rs_t[:], in_=rs_t[:]), reads=[rs_b], writes=[rs_b])
        S.op("dve", lambda e: e.tensor_scalar(out=xn_t[:], in0=z_t[:], scalar1=mv_t[:, 0:1], scalar2=rs_t[:, 0:1], op0=ALU.subtract, op1=ALU.mult),
             reads=[z_b, mv_b, rs_b], writes=[xn_b])
        S.op("pool", lambda e: e.tensor_tensor(out=xn_t[:], in0=xn_t[:], in1=g_t[:], op=ALU.mult), reads=[xn_b, g_b], writes=[xn_b])
        S.op("pool", lambda e: e.tensor_tensor(out=o_t[:], in0=xn_t[:], in1=b_t[:], op=ALU.add), reads=[xn_b, b_b], writes=[o_b])

    def ln_tmp(self):
        a = self.sb("st", [128, 2, 6], F32) + self.sb("mv", [128, 2], F32) + self.sb("rs", [128, 1], F32) + self.sb("xn", [128, DM], F32)
        return a

    def proj_ln_tile(self, W_t, W_b, nk, lhs_fn, lhs_bufs, res_t, res_b, lnc, tmps, pz, zr, o_t, o_b):
        S = self.S
        pz_t, pz_b = pz.next()

        def f_mm(e):
            for j in range(2):
                for k in range(nk):
                    ins = e.matmul(pz_t[:, j * 512:(j + 1) * 512], lhs_fn(k), W_t[:, k, j * 512:(j + 1) * 512], start=(k == 0), stop=(k == nk - 1))
            return ins
        S.op("pe", f_mm, reads=[W_b] + list(lhs_bufs), writes=[pz_b])
        z_t, z_b = zr.next()
        S.op("dve", lambda e: e.scalar_tensor_tensor(out=z_t[:], in0=res_t[:], scalar=ALPHA, in1=pz_t[:], op0=ALU.mult, op1=ALU.add),
             reads=[res_b, pz_b], writes=[z_b])
        self.emit_ln(z_t, z_b, lnc, o_t, o_b, tmps.next())

    def emit_xT(self, o_t, o_b, ob_ring, tp, xTg_t, xTg_b, col, first):
        S = self.S
        ob_t, ob_b = ob_ring.next()
        S.op("act", lambda e: e.copy(out=ob_t[:], in_=o_t[:]), reads=[o_b], writes=[ob_b])
        tp_t, tp_b = tp.next()
        ident = self.ident

        def f_tr(e):
            for k in range(8):
                ins = e.transpose(tp_t[:, k, :], ob_t[:, k * 128:(k + 1) * 128], ident[:])
            return ins
        S.op("pe", f_tr, reads=[ob_b, self.bident], writes=[tp_b])
        kw = dict(writes=[xTg_b]) if first else dict(appends=[xTg_b])
        S.op("act", lambda e: e.copy(out=xTg_t[:, :, col * 128:(col + 1) * 128], in_=tp_t[:]), reads=[tp_b], **kw)

    def phase_B(self):
        S = self.S
        with ExitStack() as es:
            self.es = es
            Wo, bWo = self.sb("Wo", [128, 8, DM], BF16)
            self.load_w(Wo, bWo, "Wo", [(Wo[:, k, :], self.w_o_a[0, k * 128:(k + 1) * 128, :]) for k in range(8)])
            lnc = self.ln_consts(0, 0)
            og = Ring([self.sb("og%d" % i, [128, 3, 1032], F32) for i in range(2)])
            xr = Ring([self.sb("xr%d" % i, [128, DM], F32) for i in range(2)])
            sm = Ring([self.sb("m%d" % i, [128, 8], F32) + self.sb("e%d" % i, [128, 3, 8], F32) + self.sb("ss%d" % i, [128, 8], F32) for i in range(2)])
            t1 = Ring([self.sb("t1_%d" % i, [128, DM], F32) for i in range(2)])
            t2 = Ring([self.sb("t2_%d" % i, [128, DM], F32) for i in range(2)])
            mg = Ring([self.sb("mg%d" % i, [128, DM], BF16) for i in range(2)])
            mT = Ring([self.sb("mT%d" % i, [128, 8, 128], BF16) for i in range(2)])
            zr = Ring([self.sb("z%d" % i, [128, DM], F32) for i in range(2)])
            o1 = Ring([self.sb("o1_%d" % i, [128, DM], F32) for i in range(2)])
            ob = Ring([self.sb("ob%d" % i, [128, DM], BF16) for i in range(2)])
            xTg = Ring([self.sb("xTg%d" % i, [128, 8, 512], BF16) for i in range(2)])
            tmps = Ring([self.ln_tmp() for i in range(2)])
            tp = Ring([self.ps("tp%d" % i, [128, 8, 128], BF16) for i in range(2)])
            pz = Ring([self.ps("pz%d" % i, [128, DM]) for i in range(2)])
            tiles = [(si, t) for si, Sq in enumerate(self.seqs) for t in range(Sq // 128)]

            def load(si, t):
                og_t, og_b = og.next()
                for g in range(3):
                    kw = dict(writes=[og_b]) if g == 0 else dict(appends=[og_b])
                    self.dma(og_t[:, g, :], self.OG[g][si][t * 128:(t + 1) * 128, :], reads=[self.bOG], key=og_b.name, **kw)
                xr_t, xr_b = xr.next()
                self.dma(xr_t[:], self.x[si][t * 128:(t + 1) * 128, :], writes=[xr_b], key=xr_b.name)
                return og_t, og_b, xr_t, xr_b
            st = {}

            def do_tile(si, t, og_t, og_b, xr_t, xr_b):
                m_t, m_b, e_t, e_b, ss_t, ss_b = sm.next()
                lse = og_t[:, :, 1024:1032]

                def f_m(e):
                    e.tensor_tensor(out=m_t[:], in0=og_t[:, 0, 1024:1032], in1=og_t[:, 1, 1024:1032], op=ALU.max)
                    return e.tensor_tensor(out=m_t[:], in0=m_t[:], in1=og_t[:, 2, 1024:1032], op=ALU.max)
                S.op("dve", f_m, reads=[og_b], writes=[m_b])
                S.op("dve", lambda e: e.tensor_tensor(out=e_t[:], in0=lse, in1=bcast_mid(m_t[:], 3), op=ALU.subtract), reads=[og_b, m_b], writes=[e_b])
                S.op("act", lambda e: e.activation(out=e_t[:], in_=e_t[:], func=AF.Exp), reads=[e_b], writes=[e_b])

                def f_s(e):
                    e.tensor_tensor(out=ss_t[:], in0=e_t[:, 0, :], in1=e_t[:, 1, :], op=ALU.add)
                    return e.tensor_tensor(out=ss_t[:], in0=ss_t[:], in1=e_t[:, 2, :], op=ALU.add)
                S.op("pool", f_s, reads=[e_b], writes=[ss_b])
                S.op("dve", lambda e: e.reciprocal(out=ss_t[:], in_=ss_t[:]), reads=[ss_b], writes=[ss_b])
                S.op("dve", lambda e: e.tensor_tensor(out=e_t[:], in0=e_t[:], in1=bcast_mid(ss_t[:], 3), op=ALU.mult), reads=[e_b, ss_b], writes=[e_b])
                t1_t, t1_b = t1.next()
                t2_t, t2_b = t2.next()
                mg_t, mg_b = mg.next()
                v3 = lambda ap: ap.rearrange("p (h f) -> p h f", h=8)
                def f_m0(e):
                    for h in range(8):
                        ins = e.activation(out=t1_t[:, h * 128:(h + 1) * 128], in_=og_t[:, 0, h * 128:(h + 1) * 128], func=AF.Copy, scale=e_t[:, 0, h:h + 1])
                    return ins
                S.op("act", f_m0, reads=[og_b, e_b], writes=[t1_b])

                def f_m1(e):
                    for h in range(8):
                        ins = e.scalar_tensor_tensor(out=t1_t[:, h * 128:(h + 1) * 128], in0=og_t[:, 1, h * 128:(h + 1) * 128], scalar=e_t[:, 1, h:h + 1],
                                                     in1=t1_t[:, h * 128:(h + 1) * 128], op0=ALU.mult, op1=ALU.add)
                    return ins
                S.op("dve", f_m1, reads=[og_b, e_b, t1_b], writes=[t1_b])

                def f_m2(e):
                    for h in range(8):
                        ins = e.scalar_tensor_tensor(out=mg_t[:, h * 128:(h + 1) * 128], in0=og_t[:, 2, h * 128:(h + 1) * 128], scalar=e_t[:, 2, h:h + 1],
                                                     in1=t1_t[:, h * 128:(h + 1) * 128], op0=ALU.mult, op1=ALU.add)
                    return ins
                S.op("dve", f_m2, reads=[og_b, e_b, t1_b], writes=[mg_b])
                tp_t, tp_b = tp.next()
                mT_t, mT_b = mT.next()

                def f_tr(e):
                    for k in range(8):
                        ins = e.transpose(tp_t[:, k, :], mg_t[:, k * 128:(k + 1) * 128], self.ident[:])
                    return ins
                S.op("pe", f_tr, reads=[mg_b, self.bident], writes=[tp_b])
                S.op("act", lambda e: e.copy(out=mT_t[:], in_=tp_t[:]), reads=[tp_b], writes=[mT_b])
                o_t, o_b = o1.next()
                self.proj_ln_tile(Wo, bWo, 8, lambda k: mT_t[:, k, :], [mT_b], xr_t, xr_b, lnc, tmps, pz, zr, o_t, o_b)
                self.dma(self.X1[si][t * 128:(t + 1) * 128, :], o_t[:], reads=[o_b], appends=[self.bX1], key=o_b.name)
                if t % 4 == 0:
                    st["x"] = xTg.next()
                xTg_t, xTg_b = st["x"]
                self.emit_xT(o_t, o_b, ob, tp, xTg_t, xTg_b, t % 4, t % 4 == 0)
                if t % 4 == 3:
                    self.dma(self.X1T[si].rearrange("k p s -> p k s")[:, :, (t - 3) * 128:(t + 1) * 128], xTg_t[:], reads=[xTg_b], appends=[self.bX1T], key=xTg_b.name)
            nxt = load(*tiles[0])
            for ti, (si, t) in enumerate(tiles):
                cur = nxt
                if ti + 1 < len(tiles):
                    nxt = load(*tiles[ti + 1])
                do_tile(si, t, *cur)
            S.fence(self.bX1)
            S.fence(self.bX1T)
        S.barrier()

    def phase_C1(self, li, XT, bXT):
        S = self.S
        with ExitStack() as es:
            self.es = es
            Wi, bWi = self.sb("Wi", [128, 8, 2 * DFF], BF16)
            self.load_w(Wi, bWi, "Wi", [(Wi[:, k, :], self.ffn_w_in[li, k * 128:(k + 1) * 128, :]) for k in range(8)])
            cp, bcp = self.sb("cp", [128, 44, 4], F32)
            for t in range(3):
                src = bass.AP(tensor=self.ffn_conv_w.tensor, offset=(li * 3 + t) * 2 * DFF, ap=[[1, 128], [128, 44], [1, 1]])
                kw = dict(writes=[bcp]) if t == 0 else dict(appends=[bcp])
                self.dma(cp[:, :, t:t + 1], src, key="cp", slow=True, **kw)
            src = bass.AP(tensor=self.ffn_conv_b.tensor, offset=li * 2 * DFF, ap=[[1, 128], [128, 44], [1, 1]])
            self.dma(cp[:, :, 3:4], src, appends=[bcp], key="cp", slow=True)
            xt = Ring([self.sb("xt%d" % i, [128, 8, 512], BF16) for i in range(2)])
            ca = Ring([self.sb("ca%d" % i, [128, 512], F32) for i in range(3)])
            cg = Ring([self.sb("cg%d" % i, [128, 512], F32) for i in range(3)])
            gg = Ring([self.sb("gg%d" % i, [128, 512], F32) for i in range(3)])
            uu = Ring([self.sb("uu%d" % i, [128, 512], BF16) for i in range(4)])
            pa = Ring([self.ps("pa%d" % i, [128, 512]) for i in range(3)])
            pg = Ring([self.ps("pg%d" % i, [128, 512]) for i in range(3)])
            tiles = []
            for si, Sq in enumerate(self.seqs):
                a = 0
                while a < Sq:
                    b = min(a + 510, Sq)
                    tiles.append((si, a, b))
                    a = b

            def load(si, a, b):
                Sq = self.seqs[si]
                xt_t, xt_b = xt.next()
                lo, hi = max(a - 1, 0), min(b + 1, Sq)
                c0 = lo - (a - 1)
                w = b - a + 2
                first = True
                if a == 0:
                    S.op("pool", lambda e: e.memset(xt_t[:, :, 0:2], 0.0), writes=[xt_b])
                    first = False
                if b == Sq:
                    kw = dict(writes=[xt_b]) if first else dict(appends=[xt_b])
                    e0 = (w - 2) if (w % 2 == 0) else (w - 1)
                    S.op("pool", lambda e: e.memset(xt_t[:, :, e0:e0 + 2], 0.0), **kw)
                    first = False
                self.dma(xt_t[:, :, c0:c0 + hi - lo], XT[si].rearrange("k p s -> p k s")[:, :, lo:hi], reads=[bXT], writes=[xt_b], key=xt_b.name)
                return xt_t, xt_b
            def do_tile(si, a, b, xt_t, xt_b):
                nb = b - a
                w = nb + 2
                for j in range(22):
                    pa_t, pa_b = pa.next()
                    pg_t, pg_b = pg.next()

                    def f_h(e, pa_t=pa_t, pg_t=pg_t, j=j):
                        for k in range(8):
                            e.matmul(pa_t[:, 0:w], Wi[:, k, j * 128:(j + 1) * 128], xt_t[:, k, 0:w], start=(k == 0), stop=(k == 7))
                        for k in range(8):
                            ins = e.matmul(pg_t[:, 0:w], Wi[:, k, DFF + j * 128:DFF + (j + 1) * 128], xt_t[:, k, 0:w], start=(k == 0), stop=(k == 7))
                        return ins
                    S.op("pe", f_h, reads=[bWi, xt_b], writes=[pa_b, pg_b])
                    ca_t, ca_b = ca.next()
                    cg_t, cg_b = cg.next()
                    gg_t, gg_b = gg.next()
                    uu_t, uu_b = uu.next()
                    for (p_t, p_b, c_t, c_b, jj) in ((pa_t, pa_b, ca_t, ca_b, j), (pg_t, pg_b, cg_t, cg_b, 22 + j)):
                        S.op("act", lambda e, p_t=p_t, c_t=c_t, jj=jj: e.activation(out=c_t[:, 0:nb], in_=p_t[:, 0:nb], func=AF.Identity,
                                                                                   scale=cp[:, jj, 0:1], bias=cp[:, jj, 3:4]),
                             reads=[p_b, bcp], writes=[c_b])
                        S.op("dve", lambda e, p_t=p_t, c_t=c_t, jj=jj: e.scalar_tensor_tensor(out=c_t[:, 0:nb], in0=p_t[:, 1:nb + 1], scalar=cp[:, jj, 1:2],
                                                                                             in1=c_t[:, 0:nb], op0=ALU.mult, op1=ALU.add),
                             reads=[p_b, bcp, c_b], writes=[c_b])
                        S.op("dve", lambda e, p_t=p_t, c_t=c_t, jj=jj: e.scalar_tensor_tensor(out=c_t[:, 0:nb], in0=p_t[:, 2:nb + 2], scalar=cp[:, jj, 2:3],
                                                                                             in1=c_t[:, 0:nb], op0=ALU.mult, op1=ALU.add),
                             reads=[p_b, bcp, c_b], writes=[c_b])
                    S.op("act", lambda e, cg_t=cg_t, gg_t=gg_t: e.activation(out=gg_t[:, 0:nb], in_=cg_t[:, 0:nb], func=AF.Gelu), reads=[cg_b], writes=[gg_b])
                    S.op("pool", lambda e, ca_t=ca_t, gg_t=gg_t, uu_t=uu_t: e.tensor_tensor(out=uu_t[:, 0:nb], in0=ca_t[:, 0:nb], in1=gg_t[:, 0:nb], op=ALU.mult),
                         reads=[ca_b, gg_b], writes=[uu_b])
                    self.dma(self.UT[si][j, :, a:b], uu_t[:, 0:nb], reads=[uu_b], appends=[self.bUT], key=uu_b.name)
            nxt = load(*tiles[0])
            for ti, (si, a, b) in enumerate(tiles):
                cur = nxt
                if ti + 1 < len(tiles):
                    nxt = load(*tiles[ti + 1])
                do_tile(si, a, b, *cur)
            S.fence(self.bUT)
        S.barrier()

    def phase_C2(self, li, RES, bRES, OUT, bOUT, OUTT, bOUTT):
        S = self.S
        with ExitStack() as es:
            self.es = es
            Wo, bWo = self.sb("Wout", [128, 22, DM], BF16)
            self.load_w(Wo, bWo, "Wout", [(Wo[:, k, :], self.ffn_w_out[li, k * 128:(k + 1) * 128, :]) for k in range(22)])
            lnc = self.ln_consts(li, 1)
            ut = Ring([self.sb("ut%d" % i, [128, 22, 512], BF16) for i in range(2)])
            xr = Ring([self.sb("xr%d" % i, [128, DM], F32) for i in range(3)])
            zr = Ring([self.sb("z%d" % i, [128, DM], F32) for i in range(2)])
            o1 = Ring([self.sb("o1_%d" % i, [128, DM], F32) for i in range(3)])
            ob = Ring([self.sb("ob%d" % i, [128, DM], BF16) for i in range(2)])
            xTg = Ring([self.sb("xTg%d" % i, [128, 8, 512], BF16) for i in range(2)])
            tmps = Ring([self.ln_tmp() for i in range(2)])
            tp = Ring([self.ps("tp%d" % i, [128, 8, 128], BF16) for i in range(2)])
            pz = Ring([self.ps("pz%d" % i, [128, DM]) for i in range(2)])
            tiles = [(si, t) for si, Sq in enumerate(self.seqs) for t in range(Sq // 512)]

            def load(si, t):
                ut_t, ut_b = ut.next()
                self.dma(ut_t[:], self.UT[si].rearrange("j p s -> p j s")[:, :, t * 512:(t + 1) * 512], reads=[self.bUT], writes=[ut_b], key=ut_b.name)
                return ut_t, ut_b
            def do_tile(si, t, ut_t, ut_b):
                if OUTT is not None:
                    xTg_t, xTg_b = xTg.next()
                for s4 in range(4):
                    r0 = t * 512 + s4 * 128
                    xr_t, xr_b = xr.next()
                    self.dma(xr_t[:], RES[si][r0:r0 + 128, :], reads=[bRES], writes=[xr_b], key=xr_b.name)
                    o_t, o_b = o1.next()
                    self.proj_ln_tile(Wo, bWo, 22, lambda k, s4=s4: ut_t[:, k, s4 * 128:(s4 + 1) * 128], [ut_b], xr_t, xr_b, lnc, tmps, pz, zr, o_t, o_b)
                    self.dma(OUT[si][r0:r0 + 128, :], o_t[:], reads=[o_b], appends=[bOUT], key=o_b.name)
                    if OUTT is not None:
                        self.emit_xT(o_t, o_b, ob, tp, xTg_t, xTg_b, s4, s4 == 0)
                if OUTT is not None:
                    self.dma(OUTT[si].rearrange("k p s -> p k s")[:, :, t * 512:(t + 1) * 512], xTg_t[:], reads=[xTg_b], appends=[bOUTT], key=xTg_b.name)
            nxt = load(*tiles[0])
            for ti, (si, t) in enumerate(tiles):
                cur = nxt
                if ti + 1 < len(tiles):
                    nxt = load(*tiles[ti + 1])
                do_tile(si, t, *cur)
            S.fence(bOUT)
            if OUTT is not None:
                S.fence(bOUTT)
        S.barrier()

    def phase_D(self):
        S = self.S
        with ExitStack() as es:
            self.es = es
            Wd, bWd = self.sb("Wd", [128, 8, 704], BF16)
            pieces = [(Wd[:, k, 0:672], self.w_dkv[0, k * 128:(k + 1) * 128, :]) for k in range(8)]
            self.load_w(Wd, bWd, "Wd", pieces)
            S.op("dve", lambda e: e.tensor_scalar(out=Wd[:, :, 672:688], in0=Wd[:, :, 656:672], scalar1=-1.0, scalar2=None, op0=ALU.mult), reads=[bWd], writes=[bWd])
            S.op("dve", lambda e: e.tensor_copy(out=Wd[:, :, 688:704], in_=Wd[:, :, 640:656]), reads=[bWd], writes=[bWd])
            wuq = self.w_uq[0].rearrange("(k p) (h c) -> k p h c", p=128, c=96)
            Wqn, bWqn = self.sb("Wqn", [128, 3, 16, 64], BF16)
            self.load_w(Wqn, bWqn, "Wqn", [(Wqn[:, k, :, :], wuq[k, :, :, 0:64]) for k in range(3)])
            Wqp, bWqp = self.sb("Wqp", [128, 3, 16, 32], BF16)
            self.load_w(Wqp, bWqp, "Wqp", [(Wqp[:, k, :, :], wuq[k, :, :, 64:96]) for k in range(3)])
            Wqr, bWqr = self.sb("Wqr", [128, 3, 16, 32], BF16)
            S.op("dve", lambda e: e.tensor_scalar(out=Wqr[:, :, :, 0:16], in0=Wqp[:, :, :, 16:32], scalar1=-1.0, scalar2=None, op0=ALU.mult), reads=[bWqp], writes=[bWqr])
            S.op("dve", lambda e: e.tensor_copy(out=Wqr[:, :, :, 16:32], in_=Wqp[:, :, :, 0:16]), reads=[bWqp, bWqr], writes=[bWqr])
            Wk, bWk = self.sb("Wk", [128, 2, 16, 64], BF16)
            wukv = self.w_ukv[0].rearrange("(k p) (h c) -> k p h c", p=128, c=128)
            self.load_w(Wk, bWk, "Wk", [(Wk[:, k, :, :], wukv[k, :, :, 0:64]) for k in range(2)])
            Wv, bWv = self.sb("Wv", [128, 2, 16, 64], BF16)
            self.load_w(Wv, bWv, "Wv", [(Wv[:, k, :, :], wukv[k, :, :, 64:128]) for k in range(2)])
            gq, bgq = self.sb("gq", [128, 384], F32)
            gkv, bgkv = self.sb("gkv", [128, 256], F32)
            self.dma(gq[:], bass.AP(tensor=self.g_q.tensor, offset=0, ap=[[0, 128], [1, 384]]), writes=[bgq], key="gq")
            self.dma(gkv[:], bass.AP(tensor=self.g_kv.tensor, offset=0, ap=[[0, 128], [1, 256]]), writes=[bgkv], key="gkv")
            xt = Ring([self.sb("xt%d" % i, [128, 8, 512], BF16) for i in range(2)])
            cs = Ring([self.sb("cs%d" % i, [128, 2, 512], F32) for i in range(2)])
            cn = Ring([self.sb("cn%d" % i, [128, 704], BF16) for i in range(2)])
            junk = Ring([self.sb("junk%d" % i, [128, 384], F32) for i in range(2)])
            ssr = Ring([self.sb("ssq%d" % i, [128, 4], F32) for i in range(3)])
            cT = Ring([self.sb("cT%d" % i, [128, 7, 512], BF16) for i in range(2)])
            qn = Ring([self.sb("qn%d" % i, [128, 8, 512], BF16) for i in range(2)])
            qr = Ring([self.sb("qr%d" % i, [128, 4, 512], BF16) for i in range(2)])
            kn = Ring([self.sb("kn%d" % i, [128, 8, 512], BF16) for i in range(2)])
            vo = Ring([self.sb("vo%d" % i, [128, 16, 65], BF16) for i in range(3)])
            for (t_, b_) in vo.slots:
                S.op("pool", lambda e, t_=t_: e.memset(t_[:], 1.0), writes=[b_])
            rt = Ring([self.sb("rt%d" % i, [128, 2, 512], F32) for i in range(3)])
            krf = Ring([self.sb("krf%d" % i, [96, 512], BF16) for i in range(2)])
            pc = Ring([self.ps("pc%d" % i, [128, DM]) for i in range(2)])
            tp = Ring([self.ps("tp%d" % i, [128, 8, 128], BF16) for i in range(1)])
            pq = Ring([self.ps("pq%d" % i, [128, 512]) for i in range(3)])
            tiles = [(si, t) for si, Sq in enumerate(self.seqs) for t in range(Sq // 512)]

            def load(si, t):
                xt_t, xt_b = xt.next()
                self.dma(xt_t[:], self.X2T[si].rearrange("k p s -> p k s")[:, :, t * 512:(t + 1) * 512], reads=[self.bX2T], writes=[xt_b], key=xt_b.name)
                cs_t, cs_b = cs.next()
                for q4 in range(4):
                    kw = dict(writes=[cs_b]) if q4 == 0 else dict(appends=[cs_b])
                    self.dma(cs_t[32 * q4:32 * q4 + 32, :, :], self.c_rope.rearrange("a r s -> r a s")[:, :, t * 512:(t + 1) * 512], key=cs_b.name, **kw)
                return xt_t, xt_b, cs_t, cs_b
            def do_tile(si, t, xt_t, xt_b, cs_t, cs_b):
                cT_t, cT_b = cT.next()
                for s4 in range(4):
                    pc_t, pc_b = pc.next()

                    def f_c(e, pc_t=pc_t, s4=s4):
                        for k in range(8):
                            e.matmul(pc_t[:, 0:512], xt_t[:, k, s4 * 128:(s4 + 1) * 128], Wd[:, k, 0:512], start=(k == 0), stop=(k == 7))
                        for k in range(8):
                            ins = e.matmul(pc_t[:, 512:704], xt_t[:, k, s4 * 128:(s4 + 1) * 128], Wd[:, k, 512:704], start=(k == 0), stop=(k == 7))
                        return ins
                    S.op("pe", f_c, reads=[bWd, xt_b], writes=[pc_b])
                    ss_t, ss_b = ssr.next()
                    jk_t, jk_b = junk.next()

                    def f_sq(e, pc_t=pc_t, ss_t=ss_t, jk_t=jk_t):
                        e.activation(out=jk_t[:, 0:384], in_=pc_t[:, 0:384], func=AF.Square, accum_out=ss_t[:, 0:1])
                        return e.activation(out=jk_t[:, 0:256], in_=pc_t[:, 384:640], func=AF.Square, accum_out=ss_t[:, 1:2])
                    S.op("act", f_sq, reads=[pc_b], writes=[jk_b, ss_b])

                    def f_rs(e, ss_t=ss_t):
                        e.tensor_scalar(out=ss_t[:, 2:3], in0=ss_t[:, 0:1], scalar1=1.0 / 384, scalar2=RMS_EPS, op0=ALU.mult, op1=ALU.add)
                        return e.tensor_scalar(out=ss_t[:, 3:4], in0=ss_t[:, 1:2], scalar1=1.0 / 256, scalar2=RMS_EPS, op0=ALU.mult, op1=ALU.add)
                    S.op("dve", f_rs, reads=[ss_b], writes=[ss_b])
                    S.op("act", lambda e, ss_t=ss_t: e.sqrt(out=ss_t[:, 2:4], in_=ss_t[:, 2:4]), reads=[ss_b], writes=[ss_b])
                    S.op("dve", lambda e, ss_t=ss_t: e.reciprocal(out=ss_t[:, 2:4], in_=ss_t[:, 2:4]), reads=[ss_b], writes=[ss_b])
                    cn_t, cn_b = cn.next()
                    S.op("dve", lambda e, pc_t=pc_t, ss_t=ss_t, cn_t=cn_t: e.scalar_tensor_tensor(out=cn_t[:, 0:384], in0=pc_t[:, 0:384], scalar=ss_t[:, 2:3], in1=gq[:],
                                                                                                   op0=ALU.mult, op1=ALU.mult), reads=[pc_b, ss_b, bgq], writes=[cn_b])
                    S.op("dve", lambda e, pc_t=pc_t, ss_t=ss_t, cn_t=cn_t: e.scalar_tensor_tensor(out=cn_t[:, 384:640], in0=pc_t[:, 384:640], scalar=ss_t[:, 3:4], in1=gkv[:],
                                                                                                   op0=ALU.mult, op1=ALU.mult), reads=[pc_b, ss_b, bgkv], appends=[cn_b])
                    S.op("dve", lambda e, pc_t=pc_t, cn_t=cn_t: e.tensor_copy(out=cn_t[:, 640:704], in_=pc_t[:, 640:704]), reads=[pc_b], appends=[cn_b])
                    tp_t, tp_b = tp.next()

                    def f_tr(e, tp_t=tp_t, cn_t=cn_t):
                        for k in range(5):
                            e.transpose(tp_t[:, k, :], cn_t[:, k * 128:(k + 1) * 128], self.ident[:])
                        e.transpose(tp_t[0:96, 5, :], cn_t[:, 576:672], self.ident[:])
                        return e.transpose(tp_t[0:96, 6, :], cn_t[:, 608:704], self.ident[:])
                    S.op("pe", f_tr, reads=[cn_b, self.bident], writes=[tp_b])
                    kw = dict(writes=[cT_b]) if s4 == 0 else dict(appends=[cT_b])

                    def f_cp(e, tp_t=tp_t, s4=s4):
                        e.copy(out=cT_t[:, 0:5, s4 * 128:(s4 + 1) * 128], in_=tp_t[:, 0:5, :])
                        return e.copy(out=cT_t[64:96, 5:7, s4 * 128:(s4 + 1) * 128], in_=tp_t[64:96, 5:7, :])
                    S.op("act", f_cp, reads=[tp_b], **kw)
                    pv_t, pv_b = pc.next()

                    def f_v(e, pv_t=pv_t, s4=s4):
                        for j in range(2):
                            for k in range(2):
                                ins = e.matmul(pv_t[:, j * 512:(j + 1) * 512], cT_t[:, 3 + k, s4 * 128:(s4 + 1) * 128], Wv[:, k, 8 * j:8 * j + 8, :],
                                               start=(k == 0), stop=(k == 1))
                        return ins
                    S.op("pe", f_v, reads=[cT_b, bWv], writes=[pv_b])
                    vo_t, vo_b = vo.next()
                    S.op("dve", lambda e, pv_t=pv_t, vo_t=vo_t: e.tensor_copy(out=vo_t[:, :, 0:64], in_=pv_t[:, :].rearrange("p (h c) -> p h c", c=64)),
                         reads=[pv_b], writes=[vo_b])
                    self.dma(self.V2[si][:, :, t * 4 + s4, :].rearrange("h p f -> p h f"), vo_t[:], reads=[vo_b], appends=[self.bQKV], key=vo_b.name)
                rt_t, rt_b = rt.next()
                kf_t, kf_b = krf.next()
                S.op("dve", lambda e: e.tensor_tensor(out=rt_t[64:96, 0, :], in0=cT_t[64:96, 5, :], in1=cs_t[64:96, 0, :], op=ALU.mult), reads=[cT_b, cs_b], writes=[rt_b])
                S.op("pool", lambda e: e.tensor_tensor(out=rt_t[64:96, 1, :], in0=cT_t[64:96, 6, :], in1=cs_t[64:96, 1, :], op=ALU.mult), reads=[cT_b, cs_b], appends=[rt_b])
                S.op("pool", lambda e: e.tensor_tensor(out=kf_t[64:96, :], in0=rt_t[64:96, 0, :], in1=rt_t[64:96, 1, :], op=ALU.add), reads=[rt_b], writes=[kf_b])
                ktd = self.KTd[si].rearrange("h p s -> p h s")
                qtd = self.QTd[si].rearrange("h p s -> p h s")
                cols = slice(t * 512, (t + 1) * 512)
                self.dma(ktd[64:96, :, cols], bcast_mid(kf_t[64:96, :], 16), reads=[kf_b], appends=[self.bQKV], key=kf_b.name)
                qn_t, qn_b = qn.next()
                kn_t, kn_b = kn.next()
                qr_t, qr_b = qr.next()
                for hp in range(8):
                    pq_t, pq_b = pq.next()

                    def f_qn(e, pq_t=pq_t, hp=hp):
                        for k in range(3):
                            ins = e.matmul(pq_t[:, :], Wqn[:, k, 2 * hp:2 * hp + 2, :].rearrange("p a b -> p (a b)"), cT_t[:, k, :], start=(k == 0), stop=(k == 2))
                        return ins
                    S.op("pe", f_qn, reads=[bWqn, cT_b], writes=[pq_b])
                    kw = dict(writes=[qn_b]) if hp == 0 else dict(appends=[qn_b])
                    S.op("act", lambda e, pq_t=pq_t, hp=hp: e.copy(out=qn_t[:, hp, :], in_=pq_t[:, :]), reads=[pq_b], **kw)
                    pk_t, pk_b = pq.next()

                    def f_kn(e, pk_t=pk_t, hp=hp):
                        for k in range(2):
                            ins = e.matmul(pk_t[:, :], Wk[:, k, 2 * hp:2 * hp + 2, :].rearrange("p a b -> p (a b)"), cT_t[:, 3 + k, :], start=(k == 0), stop=(k == 1))
                        return ins
                    S.op("pe", f_kn, reads=[bWk, cT_b], writes=[pk_b])
                    kw = dict(writes=[kn_b]) if hp == 0 else dict(appends=[kn_b])
                    if hp % 2:
                        S.op("dve", lambda e, pk_t=pk_t, hp=hp: e.tensor_copy(out=kn_t[:, hp, :], in_=pk_t[:, :]), reads=[pk_b], **kw)
                    else:
                        S.op("act", lambda e, pk_t=pk_t, hp=hp: e.copy(out=kn_t[:, hp, :], in_=pk_t[:, :]), reads=[pk_b], **kw)
                for hq in range(4):
                    pp_t, pp_b = pq.next()
                    pr_t, pr_b = pq.next()

                    def f_qp(e, pp_t=pp_t, pr_t=pr_t, hq=hq):
                        for k in range(3):
                            e.matmul(pp_t[:, :], Wqp[:, k, 4 * hq:4 * hq + 4, :].rearrange("p a b -> p (a b)"), cT_t[:, k, :], start=(k == 0), stop=(k == 2))
                        for k in range(3):
                            ins = e.matmul(pr_t[:, :], Wqr[:, k, 4 * hq:4 * hq + 4, :].rearrange("p a b -> p (a b)"), cT_t[:, k, :], start=(k == 0), stop=(k == 2))
                        return ins
                    S.op("pe", f_qp, reads=[bWqp, bWqr, cT_b], writes=[pp_b, pr_b])
                    rq_t, rq_b = rt.next()
                    S.op("dve", lambda e, pp_t=pp_t, rq_t=rq_t: e.tensor_tensor(out=rq_t[:, 0, :], in0=pp_t[:, :], in1=cs_t[:, 0, :], op=ALU.mult),
                         reads=[pp_b, cs_b], writes=[rq_b])
                    S.op("dve", lambda e, pr_t=pr_t, rq_t=rq_t: e.tensor_tensor(out=rq_t[:, 1, :], in0=pr_t[:, :], in1=cs_t[:, 1, :], op=ALU.mult),
                         reads=[pr_b, cs_b], appends=[rq_b])
                    kw = dict(writes=[qr_b]) if hq == 0 else dict(appends=[qr_b])
                    S.op("pool", lambda e, rq_t=rq_t, hq=hq: e.tensor_tensor(out=qr_t[:, hq, :], in0=rq_t[:, 0, :], in1=rq_t[:, 1, :], op=ALU.add), reads=[rq_b], **kw)
                qtd5 = self.QTd[si].rearrange("(hp two) p s -> two p hp s", two=2)
                ktd5 = self.KTd[si].rearrange("(hp two) p s -> two p hp s", two=2)
                for two in range(2):
                    self.dma(qtd5[two, 0:64, :, cols], qn_t[64 * two:64 * two + 64, :, :], reads=[qn_b], appends=[self.bQKV], key=qn_b.name)
                    self.dma(ktd5[two, 0:64, :, cols], kn_t[64 * two:64 * two + 64, :, :], reads=[kn_b], appends=[self.bQKV], key=kn_b.name)
                qtd4 = self.QTd[si].rearrange("(hq four) p s -> four p hq s", four=4)
                for j in range(4):
                    self.dma(qtd4[j, 64:96, :, cols], qr_t[32 * j:32 * j + 32, :, :], reads=[qr_b], appends=[self.bQKV], key=qr_b.name)
            nxt = load(*tiles[0])
            for ti, (si, t) in enumerate(tiles):
                cur = nxt
                if ti + 1 < len(tiles):
                    nxt = load(*tiles[ti + 1])
                do_tile(si, t, *cur)
            S.fence(self.bQKV)
        S.barrier()

    def phase_E(self):
        S = self.S
        scale = 96 ** -0.5
        with ExitStack() as es:
            self.es = es
            Smax = max(self.seqs)
            kt = Ring([self.sb("kt%d" % i, [96, Smax], BF16) for i in range(2)])
            vt = Ring([self.sb("vt%d" % i, [128, Smax // 128, 65], BF16) for i in range(2)])
            qt = Ring([self.sb("qt%d" % i, [96, 512], BF16) for i in range(3)])
            PT = Ring([self.sb("PT%d" % i, [128, 512], BF16) for i in range(4)])
            oT = Ring([self.sb("oT%d" % i, [65, 512], F32) for i in range(2)])
            rc = Ring([self.sb("rc%d" % i, [128, 4], F32) for i in range(2)])
            om = Ring([self.sb("om%d" % i, [128, 4, 64], BF16) for i in range(3)])
            psc = Ring([self.ps("psc%d" % i, [128, 512]) for i in range(3)])
            pO = Ring([self.ps("pO%d" % i, [128, 512]) for i in range(2)])
            ptr = Ring([self.ps("ptr%d" % i, [128, 4, 128]) for i in range(2)])
            for si, Sq in enumerate(self.seqs):
                nchk = Sq // 128
                heads = list(range(16))

                def loadh(h, si=si, Sq=Sq, nchk=nchk):
                    kt_t, kt_b = kt.next()
                    vt_t, vt_b = vt.next()
                    self.dma(kt_t[:, 0:Sq], self.KTd[si][h], reads=[self.bQKV], writes=[kt_b], key=kt_b.name)
                    self.dma(vt_t[:, 0:nchk, :], self.V2[si][h], reads=[self.bQKV], writes=[vt_b], key=vt_b.name)
                    return kt_t, kt_b, vt_t, vt_b
                nxh = loadh(0)
                for h in heads:
                    kt_t, kt_b, vt_t, vt_b = nxh
                    if h + 1 < 16:
                        nxh = loadh(h + 1)
                    def do_q(qi, si=si, h=h, kt_t=kt_t, kt_b=kt_b, vt_t=vt_t, vt_b=vt_b, nchk=nchk):
                        q_t, q_b = qt.next()
                        self.dma(q_t[:], self.QTd[si][h, :, qi * 512:(qi + 1) * 512], reads=[self.bQKV], writes=[q_b], key=q_b.name)
                        pO_t, pO_b = pO.next()
                        pend = {}
                        LAG = 2

                        def e_qk(c):
                            ps_t, ps_b = psc.next()
                            pT_t, pT_b = PT.next()
                            S.op("pe", lambda e: e.matmul(ps_t[:, :], kt_t[:, c * 128:(c + 1) * 128], q_t[:, :], start=True, stop=True), reads=[kt_b, q_b], writes=[ps_b])
                            S.op("act", lambda e: e.activation(out=pT_t[:], in_=ps_t[:], func=AF.Exp, scale=scale), reads=[ps_b], writes=[pT_b])
                            pend[c] = (pT_t, pT_b)

                        def e_pv(c):
                            pT_t, pT_b = pend.pop(c)
                            kw = dict(writes=[pO_b]) if c == 0 else dict(appends=[pO_b])
                            S.op("pe", lambda e: e.matmul(pO_t[0:65, :], vt_t[:, c, :], pT_t[:], start=(c == 0), stop=(c == nchk - 1)), reads=[vt_b, pT_b], **kw)
                        for j in range(nchk + LAG):
                            if j < nchk:
                                e_qk(j)
                            if j >= LAG:
                                e_pv(j - LAG)
                        oT_t, oT_b = oT.next()
                        S.op("dve", lambda e: e.tensor_copy(out=oT_t[:], in_=pO_t[0:65, :]), reads=[pO_b], writes=[oT_b])
                        ptr_t, ptr_b = ptr.next()

                        def f_tr(e):
                            for i in range(4):
                                ins = e.transpose(ptr_t[:, i, 0:65], oT_t[:, i * 128:(i + 1) * 128], self.identf[0:65, 0:65])
                            return ins
                        S.op("pe", f_tr, reads=[oT_b, self.bidentf], writes=[ptr_b])
                        rc_t, rc_b = rc.next()
                        om_t, om_b = om.next()
                        S.op("dve", lambda e: e.reciprocal(out=rc_t[:], in_=ptr_t[:, :, 64]), reads=[ptr_b], writes=[rc_b])
                        S.op("dve", lambda e: e.tensor_tensor(out=om_t[:], in0=ptr_t[:, :, 0:64], in1=bcast_last(rc_t[:], 64), op=ALU.mult),
                             reads=[ptr_b, rc_b], writes=[om_b])
                        self.dma(self.OM[si][qi * 512:(qi + 1) * 512, h * 64:(h + 1) * 64].rearrange("(i p) c -> p i c", p=128), om_t[:],
                                 reads=[om_b], appends=[self.bOM], key=om_b.name)
                    for qi in range(Sq // 512):
                        do_q(qi)
            S.fence(self.bOM)
        S.barrier()

    def phase_F(self):
        S = self.S
        with ExitStack() as es:
            self.es = es
            Wo, bWo = self.sb("Wob", [128, 8, DM], BF16)
            self.load_w(Wo, bWo, "Wob", [(Wo[:, k, :], self.w_o_b[0, k * 128:(k + 1) * 128, :]) for k in range(8)])
            lnc = self.ln_consts(1, 0)
            om = Ring([self.sb("om%d" % i, [128, DM], BF16) for i in range(2)])
            xr = Ring([self.sb("xr%d" % i, [128, DM], F32) for i in range(2)])
            mT = Ring([self.sb("mT%d" % i, [128, 8, 128], BF16) for i in range(2)])
            zr = Ring([self.sb("z%d" % i, [128, DM], F32) for i in range(2)])
            o1 = Ring([self.sb("o1_%d" % i, [128, DM], F32) for i in range(2)])
            ob = Ring([self.sb("ob%d" % i, [128, DM], BF16) for i in range(2)])
            xTg = Ring([self.sb("xTg%d" % i, [128, 8, 512], BF16) for i in range(2)])
            tmps = Ring([self.ln_tmp() for i in range(2)])
            tp = Ring([self.ps("tp%d" % i, [128, 8, 128], BF16) for i in range(2)])
            pz = Ring([self.ps("pz%d" % i, [128, DM]) for i in range(2)])
            tiles = [(si, t) for si, Sq in enumerate(self.seqs) for t in range(Sq // 128)]

            def load(si, t):
                om_t, om_b = om.next()
                self.dma(om_t[:], self.OM[si][t * 128:(t + 1) * 128, :], reads=[self.bOM], writes=[om_b], key=om_b.name)
                xr_t, xr_b = xr.next()
                self.dma(xr_t[:], self.X2[si][t * 128:(t + 1) * 128, :], reads=[self.bX2], writes=[xr_b], key=xr_b.name)
                return om_t, om_b, xr_t, xr_b
            st = {}

            def do_tile(si, t, om_t, om_b, xr_t, xr_b):
                tp_t, tp_b = tp.next()
                mT_t, mT_b = mT.next()

                def f_tr(e):
                    for k in range(8):
                        ins = e.transpose(tp_t[:, k, :], om_t[:, k * 128:(k + 1) * 128], self.ident[:])
                    return ins
                S.op("pe", f_tr, reads=[om_b, self.bident], writes=[tp_b])
                S.op("act", lambda e: e.copy(out=mT_t[:], in_=tp_t[:]), reads=[tp_b], writes=[mT_b])
                o_t, o_b = o1.next()
                self.proj_ln_tile(Wo, bWo, 8, lambda k: mT_t[:, k, :], [mT_b], xr_t, xr_b, lnc, tmps, pz, zr, o_t, o_b)
                self.dma(self.X3[si][t * 128:(t + 1) * 128, :], o_t[:], reads=[o_b], appends=[self.bX3], key=o_b.name)
                if t % 4 == 0:
                    st["x"] = xTg.next()
                xTg_t, xTg_b = st["x"]
                self.emit_xT(o_t, o_b, ob, tp, xTg_t, xTg_b, t % 4, t % 4 == 0)
                if t % 4 == 3:
                    self.dma(self.X3T[si].rearrange("k p s -> p k s")[:, :, (t - 3) * 128:(t + 1) * 128], xTg_t[:], reads=[xTg_b], appends=[self.bX3T], key=xTg_b.name)
            nxt = load(*tiles[0])
            for ti, (si, t) in enumerate(tiles):
                cur = nxt
                if ti + 1 < len(tiles):
                    nxt = load(*tiles[ti + 1])
                do_tile(si, t, *cur)
            S.fence(self.bX3)
            S.fence(self.bX3T)
        S.barrier()

    def build(self, stop_after=None):
        with ExitStack() as es0:
            self.es = es0
            self.consts()
            self.es0 = es0
            steps = [("bias", self.phase_bias), ("A0", lambda: self.phase_A(0)), ("A1", lambda: self.phase_A(1)), ("A2", lambda: self.phase_A(2)),
                     ("B", self.phase_B), ("C1a", lambda: self.phase_C1(0, self.X1T, self.bX1T)),
                     ("C2a", lambda: self.phase_C2(0, self.X1, self.bX1, self.X2, self.bX2, self.X2T, self.bX2T)),
                     ("D", self.phase_D), ("E", self.phase_E), ("F", self.phase_F),
                     ("C1b", lambda: self.phase_C1(1, self.X3T, self.bX3T)),
                     ("C2b", lambda: self.phase_C2(1, self.X3, self.bX3, self.y, self.bY, None, None))]
            for name, fn in steps:
                fn()
                if stop_after == name:
                    break
            self.S.barrier()
            self.stats = self.S.emit()
        return self.nc


def _t5_bucket_np(rel):
    nb = 16
    max_exact = 8
    rel = np.asarray(rel, np.int64)
    ret = np.where(rel > 0, nb, 0)
    n = np.abs(rel)
    nf = np.maximum(n, 1).astype(np.float32)
    large = max_exact + (np.log(nf / np.float32(max_exact)) / np.float32(math.log(1024 / max_exact)) * np.float32(nb - max_exact)).astype(np.int32)
    large = np.minimum(large, nb - 1)
    return ret + np.where(n < max_exact, n, large)


def _host_consts():
    oh = np.zeros((3, 32, 384), np.float32)
    ng = np.full((3, 384), NEGV, np.float32)
    n = np.arange(384)
    m = n - 127
    valid = (m >= 0) & (m <= 128)
    rel = 64 - m
    for g, d in enumerate(DILS):
        b = _t5_bucket_np(rel * d)
        for i in range(384):
            if valid[i]:
                oh[g, b[i], i] = 1.0
                ng[g, i] = 0.0
    inv = (1.0 / (np.float32(10000.0) ** (np.arange(0, 32, 2, dtype=np.float32) / np.float32(32)))).astype(np.float32)
    ang = (np.arange(SMAX, dtype=np.float32)[:, None] * inv[None, :]).astype(np.float32)
    cos = np.cos(ang.astype(np.float64)).astype(np.float32).T
    sin = np.sin(ang.astype(np.float64)).astype(np.float32).T
    rope = np.stack([np.concatenate([cos, cos], 0), np.concatenate([sin, sin], 0)], 0)
    return oh, ng, np.ascontiguousarray(rope)


_CACHE = {}


def run(inputs, seqs, n_cores, xs):
    key = tuple(seqs)
    if key not in _CACHE:
        _CACHE[key] = MK(list(seqs)).build()
    nc = _CACHE[key]
    oh, ng, rope = _host_consts()
    shared = {k: np.ascontiguousarray(np.asarray(inputs[k], np.float32)) for k in
              ("rel_bias", "w_qkv_a", "w_o_a", "w_dkv_b", "g_q_b", "g_kv_b", "w_uq_b", "w_ukv_b", "w_o_b",
               "ffn_w_in", "ffn_conv_w", "ffn_conv_b", "ffn_w_out", "ln_g", "ln_b")}
    shared["c_onehot"] = oh
    shared["c_negv"] = ng
    shared["c_rope"] = rope
    in_maps = []
    for c in range(n_cores):
        m = dict(shared)
        for i in range(len(seqs)):
            m["x%d" % i] = np.ascontiguousarray(xs[c][i])
        in_maps.append(m)
    res = run_bass_kernel_spmd(nc, in_maps, core_ids=list(range(n_cores)))
    return [[np.asarray(r["y%d" % i]) for i in range(len(seqs))] for r in res.results]


def kernel(**inputs):
    xp = np.asarray(inputs["x_prompt"], np.float32)
    xs_ = np.asarray(inputs["x_sample"], np.float32)
    n = 8
    outs = run(inputs, (xp.shape[1], xs_.shape[1]), n, [[xp[c], xs_[c]] for c in range(n)])
    yp = np.stack([outs[c][0] for c in range(n)], 0).astype(np.float32)
    ys = np.stack([outs[c][1] for c in range(n)], 0).astype(np.float32)
    return (yp, ys)
```

```python
import numpy as np
import concourse.bass as bass
import concourse.mybir as mybir
from concourse.bass_utils import run_bass_kernel_spmd

F32 = mybir.dt.float32
BF16 = mybir.dt.bfloat16
AF = mybir.ActivationFunctionType
ALU = mybir.AluOpType
AX = mybir.AxisListType


class Buf:
    __slots__ = ("name", "writers", "readers", "prev_readers", "psum")

    def __init__(self, name, psum=False):
        self.name = name
        self.psum = psum
        self.writers = []
        self.readers = []
        self.prev_readers = []


class Op:
    __slots__ = ("eng", "fn", "deps", "is_dma", "need_inc", "sem", "val", "grp", "idx", "semkey", "cost", "done", "cons", "barrier", "lat")

    def __init__(self, eng, fn, is_dma, semkey):
        self.cost = 0.0
        self.lat = 0.3
        self.barrier = False
        self.done = None
        self.cons = None
        self.eng = eng
        self.fn = fn
        self.deps = []
        self.is_dma = is_dma
        self.need_inc = False
        self.sem = None
        self.val = 0
        self.grp = None
        self.semkey = semkey


EPOCH = 30000


class Sched:
    ENG = ("pe", "act", "dve", "pool", "sp")

    def __init__(self, nc, same_engine_sync=True):
        self.nc = nc
        self.ops = []
        self.last = {e: None for e in self.ENG}
        self.dma_since_barrier = []
        self.bounds = []
        self.lat = 1.2
        self.same_engine_sync = same_engine_sync
        self.engobj = {"pe": nc.tensor, "act": nc.scalar, "dve": nc.vector, "pool": nc.gpsimd, "sp": nc.sync}

    DEFCOST = {"pe": 0.3, "act": 0.5, "dve": 0.5, "pool": 1.0, "sp": 0.05}

    class _Probe:
        def __init__(self, eng):
            self.eng = eng
            self.cost = 0.0

        def __getattr__(self, name):
            def call(*a, **k):
                F = 1
                for v in list(a) + list(k.values()):
                    sh = getattr(v, "shape", None)
                    if sh is not None and len(sh) >= 1:
                        f = 1
                        for d in sh[1:]:
                            f *= d
                        if f > F:
                            F = f
                e = self.eng
                if e == "pe":
                    self.cost += max(F, 64) / 1950.0 + 0.02
                elif e == "dve":
                    self.cost += 0.08 + F / 960.0
                elif e == "act":
                    self.cost += 0.13 + F / 1400.0
                elif e == "pool":
                    self.cost += 0.1 + F / 430.0
                else:
                    self.cost += 0.05
                return None
            return call

    def probe_cost(self, eng, fn):
        try:
            p = Sched._Probe(eng)
            fn(p)
            return p.cost if p.cost > 0 else self.DEFCOST[eng]
        except Exception:
            return self.DEFCOST[eng]

    def op(self, eng, fn, reads=(), writes=(), appends=(), dma=False, semkey=None, cost=None):
        o = Op(eng, fn, dma, semkey)
        o.cost = (0.05 if dma else self.probe_cost(eng, fn)) if cost is None else cost
        o.lat = self.lat
        deps = o.deps
        for b in reads:
            for w in b.writers:
                deps.append((w, "raw"))
            if b.psum:
                for r in b.readers:
                    if r.eng != eng:
                        deps.append((r, "raw"))
        for b in writes:
            for w in b.writers:
                deps.append((w, "waw"))
            for r in b.readers:
                deps.append((r, "war"))
            for r in b.prev_readers:
                deps.append((r, "war"))
        for b in appends:
            for r in b.readers:
                deps.append((r, "war"))
            for r in b.prev_readers:
                deps.append((r, "war"))
            for w in b.writers:
                if w.eng == eng:
                    deps.append((w, "ord"))
        for b in reads:
            b.readers.append(o)
        for b in writes:
            b.prev_readers = list(b.readers)
            b.readers = []
            b.writers = [o]
        for b in appends:
            b.writers.append(o)
        if dma:
            if semkey is None:
                raise ValueError("dma op needs semkey")
            self.dma_since_barrier.append(o)
        self.ops.append(o)
        self.last[eng] = o
        return o

    def fence(self, buf, eng="sp"):
        o = Op(eng, lambda e: e.nop(), False, None)
        for w in buf.writers:
            o.deps.append((w, "raw"))
        buf.writers = [o]
        self.ops.append(o)
        self.last[eng] = o
        return o

    def barrier(self):
        prods = [o for o in self.last.values() if o is not None] + list(self.dma_since_barrier)
        self.dma_since_barrier = []
        self.bounds.append(len(self.ops))
        for e in self.ENG:
            o = Op(e, lambda en: en.nop(), False, None)
            o.barrier = True
            o.cost = 0.05
            for p in prods:
                o.deps.append((p, "raw"))
            self.ops.append(o)
            self.last[e] = o

    def reorder(self, window=64):
        import heapq
        DLAT = 3.0
        out = []
        bounds = [0] + list(self.bounds) + [len(self.ops)]
        last_by_eng = {}
        for a, b in zip(bounds, bounds[1:]):
            seg = self.ops[a:b]
            if not seg:
                continue
            for o in seg:
                if o.barrier:
                    o.deps = [(p, k) for (p, k) in o.deps if p.is_dma] + [(p, "raw") for p in last_by_eng.values()]
            inseg = set(id(o) for o in seg)
            q = {e: [] for e in self.ENG}
            for o in seg:
                o.done = None
                o.cons = set()
                q[o.eng].append(o)
            for o in seg:
                for (p, kind) in o.deps:
                    if id(p) in inseg:
                        p.cons.add(o.eng)
            ptr = {e: 0 for e in self.ENG}
            sched = set()
            free_at = {e: 0.0 for e in self.ENG}
            best = {e: None for e in self.ENG}

            def find(e):
                lst = q[e]
                i = ptr[e]
                n = len(lst)
                while i < n and id(lst[i]) in sched:
                    i += 1
                ptr[e] = i
                bs, bo = None, None
                cnt = 0
                j = i
                while j < n and cnt < window:
                    o = lst[j]
                    j += 1
                    if id(o) in sched:
                        continue
                    cnt += 1
                    st = free_at[e]
                    ok = True
                    for (p, kind) in o.deps:
                        if id(p) not in inseg:
                            continue
                        if kind == "ord" and p.eng != e:
                            continue
                        if p.done is None:
                            ok = False
                            break
                        t = p.done if p.eng != e or p.is_dma else p.done - 0.0
                        if p.eng == e and not p.is_dma:
                            t = 0.0
                        if t > st:
                            st = t
                    if ok and (bs is None or st < bs - 1e-9):
                        bs, bo = st, o
                        if st <= free_at[e] + 1e-9:
                            break
                best[e] = (bs, bo) if bo is not None else None
            for e in self.ENG:
                find(e)
            nleft = len(seg)
            newseg = []
            while nleft:
                ce, cs = None, None
                for e in self.ENG:
                    if best[e] is not None and (cs is None or best[e][0] < cs):
                        ce, cs = e, best[e][0]
                if ce is None:
                    raise RuntimeError("scheduler stuck (dependency cycle or window too small)")
                o = best[ce][1]
                sched.add(id(o))
                free_at[ce] = cs + o.cost
                o.done = cs + (DLAT if o.is_dma else o.cost + o.lat)
                newseg.append((cs, len(newseg), o))
                nleft -= 1
                find(ce)
                for e in o.cons:
                    if e != ce:
                        find(e)
            newseg.sort(key=lambda x: (x[0], x[1]))
            for (_, _, o) in newseg:
                out.append(o)
                if not o.is_dma:
                    last_by_eng[o.eng] = o
        return out

    def emit(self):
        nc = self.nc
        import os as _os
        if _os.environ.get("MK_NOREORDER", "") != "1":
            self.ops = self.reorder()
        for o in self.ops:
            for (p, kind) in o.deps:
                if kind == "ord":
                    continue
                if p.is_dma:
                    continue
                if p.eng == o.eng and not o.is_dma and not (kind == "raw" and self.same_engine_sync and o.eng != "pe"):
                    continue
                p.need_inc = True
        cnt = {e: 0 for e in self.ENG}
        sems = {}

        def get_sem(key):
            if key not in sems:
                sems[key] = nc.alloc_semaphore(name="s_%s" % (str(key).replace(" ", "_"),))
            return sems[key]

        dma_cnt = {}
        dma_grp = {}
        for o in self.ops:
            if o.is_dma:
                k = ("dma", o.semkey)
                dma_cnt[k] = dma_cnt.get(k, 0) + 16
                o.sem = k
                o.val = dma_cnt[k]
            elif o.need_inc:
                c = cnt[o.eng]
                cnt[o.eng] = c + 1
                o.sem = (o.eng, c // EPOCH)
                o.val = c % EPOCH + 1
        self.stats = dict(cnt)
        self.stats["n_ops"] = len(self.ops)
        waited = {e: {} for e in self.ENG}
        n_wait = 0
        for o in self.ops:
            E = self.engobj[o.eng]
            wd = waited[o.eng]
            need = {}
            for (p, kind) in o.deps:
                if kind == "ord":
                    continue
                if not p.is_dma:
                    if p.eng == o.eng and not o.is_dma and not (kind == "raw" and self.same_engine_sync and o.eng != "pe"):
                        continue
                k, v = p.sem, p.val
                if v > wd.get(k, 0) and v > need.get(k, 0):
                    need[k] = v
            for k, v in need.items():
                E.wait_ge(get_sem(k), v)
                wd[k] = v
                n_wait += 1
            ins = o.fn(E)
            if o.is_dma:
                ins.then_inc(get_sem(o.sem), 16)
            elif o.need_inc:
                ins.then_inc(get_sem(o.sem), 1)
        self.stats["n_wait"] = n_wait
        self.stats["n_sems"] = len(sems)
        return self.stats


from contextlib import ExitStack
import math

DM = 1024
DFF = 2816
DILS = (1, 4, 16)
ALPHA = (2.0 * 2) ** 0.25
LN_EPS = 1e-5
RMS_EPS = 1e-6
NEGV = -30000.0
SMAX = 8192


class Ring:
    def __init__(self, slots):
        self.slots = slots
        self.i = 0

    def next(self):
        s = self.slots[self.i % len(self.slots)]
        self.i += 1
        return s


def bcast_last(ap, n):
    return bass.AP(tensor=ap.tensor, offset=ap.offset, ap=[list(x) for x in ap.ap] + [[0, n]])


def bcast_mid(ap, n):
    a = [list(x) for x in ap.ap]
    return bass.AP(tensor=ap.tensor, offset=ap.offset, ap=[a[0], [0, n]] + a[1:])


class MK:
    def __init__(self, seqs, debug=False):
        self.seqs = seqs
        self.debug = debug
        nc = self.nc = bass.Bass("TRN2", target_bir_lowering=False)
        self.S = Sched(nc)
        self.es = None
        D = lambda n, s, dt=F32: nc.dram_tensor(n, s, dt, kind="ExternalInput").ap()
        self.x = [D("x%d" % i, [s, DM]) for i, s in enumerate(seqs)]
        self.y = [nc.dram_tensor("y%d" % i, [s, DM], F32, kind="ExternalOutput").ap() for i, s in enumerate(seqs)]
        self.rel_bias = D("rel_bias", [32, 24])
        self.w_qkv = D("w_qkv_a", [1, DM, 9216])
        self.w_o_a = D("w_o_a", [1, DM, DM])
        self.w_dkv = D("w_dkv_b", [1, DM, 672])
        self.g_q = D("g_q_b", [1, 384])
        self.g_kv = D("g_kv_b", [1, 256])
        self.w_uq = D("w_uq_b", [1, 384, 1536])
        self.w_ukv = D("w_ukv_b", [1, 256, 2048])
        self.w_o_b = D("w_o_b", [1, DM, DM])
        self.ffn_w_in = D("ffn_w_in", [2, DM, 2 * DFF])
        self.ffn_conv_w = D("ffn_conv_w", [2, 3, 2 * DFF])
        self.ffn_conv_b = D("ffn_conv_b", [2, 2 * DFF])
        self.ffn_w_out = D("ffn_w_out", [2, DFF, DM])
        self.ln_g = D("ln_g", [2, 2, DM])
        self.ln_b = D("ln_b", [2, 2, DM])
        self.c_onehot = D("c_onehot", [3, 32, 384])
        self.c_negv = D("c_negv", [3, 384])
        self.c_rope = D("c_rope", [2, 32, SMAX])
        T = lambda n, s, dt: (nc.dram_tensor(n, s, dt, kind="ExternalOutput").ap() if (debug and n[:2] in ("X1", "X2", "X3", "OG", "BT", "OM", "UT", "QT", "KT", "V2")) else nc.dram_tensor(n, s, dt).ap())
        self.PADS = T("PADS", [24, 384], F32)
        self.bPADS = Buf("PADS")
        self.BT = T("BT", [24, 128, 256], F32)
        self.bBT = Buf("BT")
        self.OG = [[T("OG%d_%d" % (g, i), [s, 1032], F32) for i, s in enumerate(seqs)] for g in range(3)]
        self.bOG = Buf("OG")
        self.X1 = [T("X1_%d" % i, [s, DM], F32) for i, s in enumerate(seqs)]
        self.X1T = [T("X1T_%d" % i, [8, 128, s], BF16) for i, s in enumerate(seqs)]
        self.X2 = [T("X2_%d" % i, [s, DM], F32) for i, s in enumerate(seqs)]
        self.X2T = [T("X2T_%d" % i, [8, 128, s], BF16) for i, s in enumerate(seqs)]
        self.X3 = [T("X3_%d" % i, [s, DM], F32) for i, s in enumerate(seqs)]
        self.X3T = [T("X3T_%d" % i, [8, 128, s], BF16) for i, s in enumerate(seqs)]
        self.UT = [T("UT_%d" % i, [22, 128, s], BF16) for i, s in enumerate(seqs)]
        self.QTd = [T("QT_%d" % i, [16, 96, s], BF16) for i, s in enumerate(seqs)]
        self.KTd = [T("KT_%d" % i, [16, 96, s], BF16) for i, s in enumerate(seqs)]
        self.V2 = [T("V2_%d" % i, [16, 128, s // 128, 65], BF16) for i, s in enumerate(seqs)]
        self.OM = [T("OM_%d" % i, [s, DM], BF16) for i, s in enumerate(seqs)]
        self.bX1 = Buf("X1"); self.bX1T = Buf("X1T"); self.bX2 = Buf("X2"); self.bX2T = Buf("X2T")
        self.bX3 = Buf("X3"); self.bX3T = Buf("X3T"); self.bUT = Buf("UT"); self.bQKV = Buf("QKV"); self.bOM = Buf("OM")
        self.bY = Buf("Y")
        self.uid = 0

    def sb(self, name, shape, dt):
        self.uid += 1
        t = self.es.enter_context(self.nc.sbuf_tensor("%s_%d" % (name, self.uid), shape, dt))
        return t, Buf(name)

    def ps(self, name, shape, dt=F32):
        self.uid += 1
        t = self.es.enter_context(self.nc.psum_tensor("%s_%d" % (name, self.uid), shape, dt))
        return t, Buf(name, psum=True)

    def dma(self, out, in_, reads=(), writes=(), appends=(), key=None, eng="sp", slow=False):
        if slow:
            fn = lambda e: e.dma_start(out=out, in_=in_, allow_slow_non_contiguous=True)
        else:
            fn = lambda e: e.dma_start(out=out, in_=in_)
        return self.S.op(eng, fn, reads=reads, writes=writes, appends=appends, dma=True, semkey=key)

    def load_w(self, dst_t, dst_b, key, pieces):
        for i, (o, s) in enumerate(pieces):
            if i == 0:
                self.dma(o, s, writes=[dst_b], key=key, eng="pool")
            else:
                self.dma(o, s, appends=[dst_b], key=key, eng="pool")

    def consts(self):
        S = self.S
        self.ident, self.bident = self.sb("ident", [128, 128], BF16)
        self.identf, self.bidentf = self.sb("identf", [128, 128], F32)
        self.Jf, self.bJf = self.sb("Jf", [128, 128], F32)
        self.ones, self.bones = self.sb("ones", [128, 2], BF16)
        identf, ident, Jf, ones = self.identf, self.ident, self.Jf, self.ones
        S.op("pool", lambda e: e.memset(identf[:], 1.0), writes=[self.bidentf])
        S.op("pool", lambda e: e.affine_select(out=identf[:], in_=identf[:], pattern=[[-1, 128]], compare_op=ALU.is_equal,
                                               fill=0.0, base=0, channel_multiplier=1), reads=[self.bidentf], writes=[self.bidentf])
        S.op("dve", lambda e: e.tensor_copy(out=ident[:], in_=identf[:]), reads=[self.bidentf], writes=[self.bident])
        S.op("pool", lambda e: e.memset(Jf[:], 1.0), writes=[self.bJf])
        S.op("pool", lambda e: e.affine_select(out=Jf[:], in_=Jf[:], pattern=[[1, 128]], compare_op=ALU.is_equal,
                                               fill=0.0, base=-127, channel_multiplier=1), reads=[self.bJf], writes=[self.bJf])
        S.op("pool", lambda e: e.memset(ones[:], 1.0), writes=[self.bones])

    def ln_consts(self, li, lj):
        g, bg = self.sb("lng", [128, DM], F32)
        b, bb = self.sb("lnb", [128, DM], F32)
        gs = bass.AP(tensor=self.ln_g.tensor, offset=(li * 2 + lj) * DM, ap=[[0, 128], [1, DM]])
        bs = bass.AP(tensor=self.ln_b.tensor, offset=(li * 2 + lj) * DM, ap=[[0, 128], [1, DM]])
        self.dma(g[:], gs, writes=[bg], key="lng")
        self.dma(b[:], bs, writes=[bb], key="lnb")
        return (g, bg, b, bb)

    def phase_bias(self):
        S, nc = self.S, self.nc
        with ExitStack() as es:
            self.es = es
            rb, brb = self.sb("rb", [32, 24], F32)
            oh, boh = self.sb("oh", [32, 3, 384], F32)
            ng, bng = self.sb("ng", [8, 3, 384], F32)
            pads, bpads = self.sb("pads", [8, 3, 384], F32)
            pp, bpp = self.ps("pp", [128, 512])
            self.dma(rb[:], self.rel_bias[:, :], writes=[brb], key="rb")
            self.dma(oh[:], self.c_onehot.rearrange("g b n -> b g n"), writes=[boh], key="oh")
            self.dma(ng[:], bass.AP(tensor=self.c_negv.tensor, offset=0, ap=[[0, 8], [384, 3], [1, 384]]), writes=[bng], key="ng")
            for g in range(3):
                S.op("pe", lambda e, g=g: e.matmul(pp[0:8, 0:384], rb[:, g * 8:(g + 1) * 8], oh[:, g, :], start=True, stop=True),
                     reads=[brb, boh], writes=[bpp])
                S.op("dve", lambda e, g=g: e.tensor_tensor(out=pads[:, g, :], in0=pp[0:8, 0:384], in1=ng[:, g, :], op=ALU.add),
                     reads=[bpp, bng], **(dict(writes=[bpads]) if g == 0 else dict(appends=[bpads])))
            self.dma(self.PADS.rearrange("(g h) n -> h g n", g=3), pads[:], reads=[bpads], writes=[self.bPADS], key="pads_o")
            hk = Ring([self.sb("hk%d" % i, [128, 256], F32) for i in range(2)])
            tb = Ring([self.sb("tb%d" % i, [128, 256], F32) for i in range(2)])
            pt = Ring([self.ps("ptz%d" % i, [128, 512]) for i in range(2)])
            for hh in range(24):
                h_t, h_b = hk.next()
                t_t, t_b = tb.next()
                p_t, p_b = pt.next()
                src = bass.AP(tensor=self.PADS.tensor, offset=hh * 384, ap=[[1, 128], [1, 256]])
                self.dma(h_t[:], src, reads=[self.bPADS], writes=[h_b], key="hk%d" % (hh % 2))
                S.op("pe", lambda e, p_t=p_t, h_t=h_t: e.matmul(p_t[:, 0:256], self.Jf[:], h_t[:], start=True, stop=True),
                     reads=[self.bJf, h_b], writes=[p_b])
                S.op("dve", lambda e, p_t=p_t, t_t=t_t: e.tensor_copy(out=t_t[:], in_=p_t[:, 0:256]), reads=[p_b], writes=[t_b])
                self.dma(self.BT[hh], t_t[:], reads=[t_b], appends=[self.bBT], key="tb%d" % (hh % 2))
            S.fence(self.bBT)
        S.barrier()

    def phase_A(self, g):
        S, nc = self.S, self.nc
        d = DILS[g]
        scale = 128 ** -0.5
        with ExitStack() as es:
            self.es = es
            Wg, bWg = self.sb("Wg", [128, 8, 3072], BF16)
            self.load_w(Wg, bWg, "Wg", [(Wg[:, k, :], self.w_qkv[0, k * 128:(k + 1) * 128, g * 3072:(g + 1) * 3072]) for k in range(8)])
            bt, bbt = self.sb("biasT", [128, 8, 256], F32)
            self.dma(bt[:], self.BT[g * 8:(g + 1) * 8].rearrange("h k q -> k h q"), reads=[self.bBT], writes=[bbt], key="biasT")
            xin = Ring([self.sb("xin%d" % i, [128, DM], BF16) for i in range(3)])
            for (t, b) in xin.slots:
                S.op("pool", lambda e, t=t: e.memset(t[:], 0.0), writes=[b])
            xT = Ring([self.sb("xT%d" % i, [128, 8, 640], BF16) for i in range(2)])
            QT = Ring([self.sb("QT%d" % i, [128, 8, 512], BF16) for i in range(2)])
            KT = Ring([self.sb("KT%d" % i, [128, 8, 640], BF16) for i in range(2)])
            VA = Ring([self.sb("VA%d" % i, [128, 5, 8, 129], BF16) for i in range(2)])
            for (t, b) in VA.slots:
                S.op("pool", lambda e, t=t: e.memset(t[:], 1.0), writes=[b])
            PT = Ring([self.sb("PT%d" % i, [128, 256], BF16) for i in range(4)])
            SC = Ring([self.sb("SC%d" % i, [128, 256], F32) for i in range(4)])
            OS = Ring([self.sb("OS%d" % i, [128, 4, 1032], F32) for i in range(2)])
            RC = Ring([self.sb("RC%d" % i, [128, 4], F32) for i in range(4)])
            tp = Ring([self.ps("tp%d" % i, [128, 8, 128], BF16) for i in range(2)])
            pj = Ring([self.ps("pj%d" % i, [128, 512]) for i in range(2)])
            pO = Ring([self.ps("pO%d" % i, [128, 2, 512]) for i in range(2)])
            DN = Ring([self.sb("DN%d" % i, [128, 4], F32) for i in range(4)])
            ident, bident, ones, bones = self.ident, self.bident, self.ones, self.bones
            for si, Sq in enumerate(self.seqs):
                L = Sq // d
                nq = min(512, L)
                nsub = nq // 128
                nch = nsub + 1
                Wd = nq + 128
                xsub = self.x[si].rearrange("(l d) f -> d l f", d=d)
                ogsub = self.OG[g][si].rearrange("(l d) f -> d l f", d=d)
                prev = {}

                def do_tile(r, T, L=L, nq=nq, nsub=nsub, nch=nch, Wd=Wd, xsub=xsub, ogsub=ogsub, prev=prev):
                    if True:
                        l0 = T * nq
                        xT_t, xT_b = xT.next()
                        carry = T > 0
                        if carry:
                            pxT_t, pxT_b, pKT_t, pKT_b, pVA_t, pVA_b = prev["t"]
                            S.op("pool", lambda e: e.tensor_copy(out=xT_t[:, :, 0:128], in_=pxT_t[:, :, nq:nq + 128]), reads=[pxT_b], writes=[xT_b])
                        vparts = []
                        for c in range(nch):
                            lo = l0 - 64 + 128 * c
                            vlo, vhi = max(lo, 0), min(lo + 128, L)
                            p0, p1 = vlo - lo, vhi - lo
                            vparts.append((p0, p1))
                            if carry and c == 0:
                                continue
                            xi_t, xi_b = xin.next()
                            self.dma(xi_t[p0:p1, :], xsub[r, vlo:vhi, :], writes=[xi_b], key=xi_b.name, eng="pool")
                            tp_t, tp_b = tp.next()

                            def f_tr(e, tp_t=tp_t, xi_t=xi_t):
                                for k in range(8):
                                    ins = e.transpose(tp_t[:, k, :], xi_t[:, k * 128:(k + 1) * 128], ident[:])
                                return ins
                            S.op("pe", f_tr, reads=[xi_b, bident], writes=[tp_b])
                            kw = dict(writes=[xT_b]) if (c == 0 and not carry) else dict(appends=[xT_b])
                            S.op("act" if c % 2 else "dve",
                                 (lambda e, tp_t=tp_t, c=c: e.copy(out=xT_t[:, :, c * 128:(c + 1) * 128], in_=tp_t[:])) if c % 2 else
                                 (lambda e, tp_t=tp_t, c=c: e.tensor_copy(out=xT_t[:, :, c * 128:(c + 1) * 128], in_=tp_t[:])),
                                 reads=[tp_b], **kw)
                        QT_t, QT_b = QT.next()
                        KT_t, KT_b = KT.next()
                        VA_t, VA_b = VA.next()
                        ev = 0
                        for h in range(8):
                            pj_t, pj_b = pj.next()

                            def f_q(e, pj_t=pj_t, h=h):
                                for k in range(8):
                                    ins = e.matmul(pj_t[:, 0:nq], Wg[:, k, h * 128:(h + 1) * 128], xT_t[:, k, 64:64 + nq], start=(k == 0), stop=(k == 7))
                                return ins
                            S.op("pe", f_q, reads=[bWg, xT_b], writes=[pj_b])
                            kw = dict(writes=[QT_b]) if h == 0 else dict(appends=[QT_b])
                            S.op("act", lambda e, pj_t=pj_t, h=h: e.activation(out=QT_t[:, h, 0:nq], in_=pj_t[:, 0:nq], func=AF.Copy, scale=scale),
                                 reads=[pj_b], **kw)
                        if carry:
                            S.op("pool", lambda e: e.tensor_copy(out=KT_t[:, :, 0:128], in_=pKT_t[:, :, nq:nq + 128]), reads=[pKT_b], writes=[KT_b])
                            S.op("pool", lambda e: e.tensor_copy(out=VA_t[:, 0, :, :], in_=pVA_t[:, nch - 1, :, :]), reads=[pVA_b], writes=[VA_b])
                            kspans = [(128, nq)]
                        else:
                            kspans = [(0, Wd // 2), (Wd // 2, Wd // 2)]
                        for h in range(8):
                            for (k0, hw) in kspans:
                                pj_t, pj_b = pj.next()

                                def f_k(e, pj_t=pj_t, h=h, k0=k0, hw=hw):
                                    for k in range(8):
                                        ins = e.matmul(pj_t[:, 0:hw], Wg[:, k, 1024 + h * 128:1024 + (h + 1) * 128],
                                                       xT_t[:, k, k0:k0 + hw], start=(k == 0), stop=(k == 7))
                                    return ins
                                S.op("pe", f_k, reads=[bWg, xT_b], writes=[pj_b])
                                kw = dict(writes=[KT_b]) if (h == 0 and k0 == 0) else dict(appends=[KT_b])
                                S.op("dve", lambda e, pj_t=pj_t, h=h, k0=k0, hw=hw: e.tensor_copy(out=KT_t[:, h, k0:k0 + hw], in_=pj_t[:, 0:hw]),
                                     reads=[pj_b], **kw)
                        prev["t"] = (xT_t, xT_b, KT_t, KT_b, VA_t, VA_b)
                        for c in range(1 if carry else 0, nch):
                            for j in range(2):
                                pj_t, pj_b = pj.next()

                                def f_v(e, pj_t=pj_t, c=c, j=j):
                                    for k in range(8):
                                        ins = e.matmul(pj_t[:, :], xT_t[:, k, c * 128:(c + 1) * 128], Wg[:, k, 2048 + j * 512:2048 + (j + 1) * 512],
                                                       start=(k == 0), stop=(k == 7))
                                    return ins
                                S.op("pe", f_v, reads=[bWg, xT_b], writes=[pj_b])
                                kw = dict(writes=[VA_b]) if (c == 0 and j == 0) else dict(appends=[VA_b])
                                if (c * 2 + j) % 2:
                                    S.op("act", lambda e, pj_t=pj_t, c=c, j=j: e.copy(out=VA_t[:, c, 4 * j:4 * j + 4, 0:128], in_=pj_t[:, :].rearrange("p (h f) -> p h f", h=4)),
                                         reads=[pj_b], **kw)
                                else:
                                    S.op("dve", lambda e, pj_t=pj_t, c=c, j=j: e.tensor_copy(out=VA_t[:, c, 4 * j:4 * j + 4, 0:128], in_=pj_t[:, :].rearrange("p (h f) -> p h f", h=4)),
                                         reads=[pj_b], **kw)
                        OS_t, OS_b = OS.next()
                        items = [(h, c) for h in range(8) for c in range(nch)]
                        LAG = 2
                        pend = {}
                        pOcur = {}
                        first_os = [True]

                        def emit_qk(h, c):
                            pT_t, pT_b = PT.next()
                            sc_t, sc_b = SC.next()
                            ps_t, ps_b = pj.next()
                            if c == 0:
                                q0, q1, b0, b1 = 0, 128, 128, 256
                            elif c == nch - 1:
                                q0, q1, b0, b1 = (c - 1) * 128, c * 128, 0, 128
                            else:
                                q0, q1, b0, b1 = (c - 1) * 128, (c + 1) * 128, 0, 256
                            S.op("pe", lambda e: e.matmul(ps_t[:, b0:b1], KT_t[:, h, c * 128:(c + 1) * 128], QT_t[:, h, q0:q1], start=True, stop=True),
                                 reads=[KT_b, QT_b], writes=[ps_b])
                            S.op("dve", lambda e: e.tensor_tensor(out=sc_t[:, b0:b1], in0=ps_t[:, b0:b1], in1=bt[:, h, b0:b1], op=ALU.add),
                                 reads=[ps_b, bbt], writes=[sc_b])
                            S.op("act", lambda e: e.activation(out=pT_t[:, b0:b1], in_=sc_t[:, b0:b1], func=AF.Exp), reads=[sc_b], writes=[pT_b])
                            p0, p1 = vparts[c]
                            if (p0, p1) != (0, 128):
                                i0, i1 = (0, p0) if p0 > 0 else (p1, 128)
                                S.op("pool", lambda e: e.memset(pT_t[i0:i1, b0:b1], 0.0), reads=[pT_b], writes=[pT_b], cost=0.2)
                            pend[(h, c)] = (pT_t, pT_b)

                        def emit_pv(h, c):
                            pT_t, pT_b = pend.pop((h, c))
                            if c == 0:
                                pOcur[h] = pO.next()
                            pO_t, pO_b = pOcur[h]
                            reg = lambda i: pO_t[:, i // 2, (i % 2) * 129:(i % 2) * 129 + 129]
                            p0, p1 = 0, 128

                            def f_pv(e):
                                ins = None
                                if c >= 1:
                                    i = c - 1
                                    ins = e.matmul(reg(i), pT_t[p0:p1, 0:128], VA_t[p0:p1, c, h, :], start=False, stop=True)
                                if c <= nch - 2:
                                    i = c
                                    ins = e.matmul(reg(i), pT_t[p0:p1, 128:256], VA_t[p0:p1, c, h, :], start=True, stop=False)
                                return ins
                            kw = dict(writes=[pO_b]) if c == 0 else dict(appends=[pO_b])
                            S.op("pe", f_pv, reads=[pT_b, VA_b, bones], **kw)
                            if c == nch - 1:
                                rc_t, rc_b = RC.next()
                                dn_t, dn_b = DN.next()
                                nbk, nj = (nsub + 1) // 2, min(nsub, 2)
                                pv4 = pO_t[:, 0:nbk, 0:258].rearrange("p b (j f) -> p b j f", f=129)[:, :, 0:nj, :]
                                v3 = lambda ap: ap.rearrange("p (b j) -> p b j", j=nj)
                                S.op("dve", lambda e: e.tensor_copy(out=v3(dn_t[:, 0:nsub]), in_=pv4[:, :, :, 128]),
                                     reads=[pO_b], writes=[dn_b])
                                S.op("dve", lambda e: e.reciprocal(out=rc_t[:, 0:nsub], in_=dn_t[:, 0:nsub]), reads=[dn_b], writes=[rc_b])
                                kw2 = dict(writes=[OS_b]) if first_os[0] else dict(appends=[OS_b])
                                first_os[0] = False
                                S.op("act", lambda e: e.activation(out=OS_t[:, 0:nsub, 1024 + h], in_=dn_t[:, 0:nsub], func=AF.Ln),
                                     reads=[dn_b], **kw2)
                                S.op("dve", lambda e: e.tensor_tensor(out=OS_t[:, 0:nsub, h * 128:(h + 1) * 128].rearrange("p (b j) f -> p b j f", j=nj),
                                                                      in0=pv4[:, :, :, 0:128],
                                                                      in1=bcast_last(v3(rc_t[:, 0:nsub]), 128), op=ALU.mult),
                                     reads=[pO_b, rc_b], appends=[OS_b])
                        for j in range(len(items) + LAG):
                            if j < len(items):
                                emit_qk(*items[j])
                            if j >= LAG:
                                emit_pv(*items[j - LAG])
                        for i in range(nsub):
                            self.dma(ogsub[r, l0 + 128 * i:l0 + 128 * (i + 1), :], OS_t[:, i, :], reads=[OS_b], appends=[self.bOG], key=OS_b.name)
                for r in range(d):
                    for T in range(L // nq):
                        do_tile(r, T)
            S.fence(self.bOG)
        S.barrier()

    def emit_ln(self, z_t, z_b, lnc, o_t, o_b, tmp):
        S = self.S
        g_t, g_b, b_t, b_b = lnc
        st_t, st_b, mv_t, mv_b, rs_t, rs_b, xn_t, xn_b = tmp

        def f_st(e):
            e.bn_stats(st_t[:, 0, :], z_t[:, 0:512])
            return e.bn_stats(st_t[:, 1, :], z_t[:, 512:1024])
        S.op("dve", f_st, reads=[z_b], writes=[st_b])
        S.op("dve", lambda e: e.bn_aggr(mv_t[:], st_t[:]), reads=[st_b], writes=[mv_b])
        S.op("dve", lambda e: e.tensor_scalar_add(out=rs_t[:], in0=mv_t[:, 1:2], scalar1=LN_EPS), reads=[mv_b], writes=[rs_b])
        S.op("act", lambda e: e.sqrt(out=rs_t[:], in_=rs_t[:]), reads=[rs_b], writes=[rs_b])
        S.op("dve", lambda e: e.reciprocal(out=rs_t[:], in_=rs_t[:]), reads=[rs_b], writes=[rs_b])
        S.op("dve", lambda e: e.tensor_scalar(out=xn_t[:], in0=z_t[:], scalar1=mv_t[:, 0:1], scalar2=rs_t[:, 0:1], op0=ALU.subtract, op1=ALU.mult),
             reads=[z_b, mv_b, rs_b], writes=[xn_b])
        S.op("pool", lambda e: e.tensor_tensor(out=xn_t[:], in0=xn_t[:], in1=g_t[:], op=ALU.mult), reads=[xn_b, g_b], writes=[xn_b])
        S.op("pool", lambda e: e.tensor_tensor(out=o_t[:], in0=xn_t[:], in1=b_t[:], op=ALU.add), reads=[xn_b, b_b], writes=[o_b])

    def ln_tmp(self):
        a = self.sb("st", [128, 2, 6], F32) + self.sb("mv", [128, 2], F32) + self.sb("rs", [128, 1], F32) + self.sb("xn", [128, DM], F32)
        return a

    def proj_ln_tile(self, W_t, W_b, nk, lhs_fn, lhs_bufs, res_t, res_b, lnc, tmps, pz, zr, o_t, o_b):
        S = self.S
        pz_t, pz_b = pz.next()

        def f_mm(e):
            for j in range(2):
                for k in range(nk):
                    ins = e.matmul(pz_t[:, j * 512:(j + 1) * 512], lhs_fn(k), W_t[:, k, j * 512:(j + 1) * 512], start=(k == 0), stop=(k == nk - 1))
            return ins
        S.op("pe", f_mm, reads=[W_b] + list(lhs_bufs), writes=[pz_b])
        z_t, z_b = zr.next()
        S.op("dve", lambda e: e.scalar_tensor_tensor(out=z_t[:], in0=res_t[:], scalar=ALPHA, in1=pz_t[:], op0=ALU.mult, op1=ALU.add),
             reads=[res_b, pz_b], writes=[z_b])
        self.emit_ln(z_t, z_b, lnc, o_t, o_b, tmps.next())

    def emit_xT(self, o_t, o_b, ob_ring, tp, xTg_t, xTg_b, col, first):
        S = self.S
        ob_t, ob_b = ob_ring.next()
        S.op("act", lambda e: e.copy(out=ob_t[:], in_=o_t[:]), reads=[o_b], writes=[ob_b])
        tp_t, tp_b = tp.next()
        ident = self.ident

        def f_tr(e):
            for k in range(8):
                ins = e.transpose(tp_t[:, k, :], ob_t[:, k * 128:(k + 1) * 128], ident[:])
            return ins
        S.op("pe", f_tr, reads=[ob_b, self.bident], writes=[tp_b])
        kw = dict(writes=[xTg_b]) if first else dict(appends=[xTg_b])
        S.op("act", lambda e: e.copy(out=xTg_t[:, :, col * 128:(col + 1) * 128], in_=tp_t[:]), reads=[tp_b], **kw)

    def phase_B(self):
        S = self.S
        with ExitStack() as es:
            self.es = es
            Wo, bWo = self.sb("Wo", [128, 8, DM], BF16)
            self.load_w(Wo, bWo, "Wo", [(Wo[:, k, :], self.w_o_a[0, k * 128:(k + 1) * 128, :]) for k in range(8)])
            lnc = self.ln_consts(0, 0)
            og = Ring([self.sb("og%d" % i, [128, 3, 1032], F32) for i in range(2)])
            xr = Ring([self.sb("xr%d" % i, [128, DM], F32) for i in range(2)])
            sm = Ring([self.sb("m%d" % i, [128, 8], F32) + self.sb("e%d" % i, [128, 3, 8], F32) + self.sb("ss%d" % i, [128, 8], F32) for i in range(2)])
            t1 = Ring([self.sb("t1_%d" % i, [128, DM], F32) for i in range(2)])
            t2 = Ring([self.sb("t2_%d" % i, [128, DM], F32) for i in range(2)])
            mg = Ring([self.sb("mg%d" % i, [128, DM], BF16) for i in range(2)])
            mT = Ring([self.sb("mT%d" % i, [128, 8, 128], BF16) for i in range(2)])
            zr = Ring([self.sb("z%d" % i, [128, DM], F32) for i in range(2)])
            o1 = Ring([self.sb("o1_%d" % i, [128, DM], F32) for i in range(2)])
            ob = Ring([self.sb("ob%d" % i, [128, DM], BF16) for i in range(2)])
            xTg = Ring([self.sb("xTg%d" % i, [128, 8, 512], BF16) for i in range(2)])
            tmps = Ring([self.ln_tmp() for i in range(2)])
            tp = Ring([self.ps("tp%d" % i, [128, 8, 128], BF16) for i in range(2)])
            pz = Ring([self.ps("pz%d" % i, [128, DM]) for i in range(2)])
            tiles = [(si, t) for si, Sq in enumerate(self.seqs) for t in range(Sq // 128)]

            def load(si, t):
                og_t, og_b = og.next()
                for g in range(3):
                    kw = dict(writes=[og_b]) if g == 0 else dict(appends=[og_b])
                    self.dma(og_t[:, g, :], self.OG[g][si][t * 128:(t + 1) * 128, :], reads=[self.bOG], key=og_b.name, **kw)
                xr_t, xr_b = xr.next()
                self.dma(xr_t[:], self.x[si][t * 128:(t + 1) * 128, :], writes=[xr_b], key=xr_b.name)
                return og_t, og_b, xr_t, xr_b
            st = {}

            def do_tile(si, t, og_t, og_b, xr_t, xr_b):
                m_t, m_b, e_t, e_b, ss_t, ss_b = sm.next()
                lse = og_t[:, :, 1024:1032]

                def f_m(e):
                    e.tensor_tensor(out=m_t[:], in0=og_t[:, 0, 1024:1032], in1=og_t[:, 1, 1024:1032], op=ALU.max)
                    return e.tensor_tensor(out=m_t[:], in0=m_t[:], in1=og_t[:, 2, 1024:1032], op=ALU.max)
                S.op("dve", f_m, reads=[og_b], writes=[m_b])
                S.op("dve", lambda e: e.tensor_tensor(out=e_t[:], in0=lse, in1=bcast_mid(m_t[:], 3), op=ALU.subtract), reads=[og_b, m_b], writes=[e_b])
                S.op("act", lambda e: e.activation(out=e_t[:], in_=e_t[:], func=AF.Exp), reads=[e_b], writes=[e_b])

                def f_s(e):
                    e.tensor_tensor(out=ss_t[:], in0=e_t[:, 0, :], in1=e_t[:, 1, :], op=ALU.add)
                    return e.tensor_tensor(out=ss_t[:], in0=ss_t[:], in1=e_t[:, 2, :], op=ALU.add)
                S.op("pool", f_s, reads=[e_b], writes=[ss_b])
                S.op("dve", lambda e: e.reciprocal(out=ss_t[:], in_=ss_t[:]), reads=[ss_b], writes=[ss_b])
                S.op("dve", lambda e: e.tensor_tensor(out=e_t[:], in0=e_t[:], in1=bcast_mid(ss_t[:], 3), op=ALU.mult), reads=[e_b, ss_b], writes=[e_b])
                t1_t, t1_b = t1.next()
                t2_t, t2_b = t2.next()
                mg_t, mg_b = mg.next()
                v3 = lambda ap: ap.rearrange("p (h f) -> p h f", h=8)
                def f_m0(e):
                    for h in range(8):
                        ins = e.activation(out=t1_t[:, h * 128:(h + 1) * 128], in_=og_t[:, 0, h * 128:(h + 1) * 128], func=AF.Copy, scale=e_t[:, 0, h:h + 1])
                    return ins
                S.op("act", f_m0, reads=[og_b, e_b], writes=[t1_b])

                def f_m1(e):
                    for h in range(8):
                        ins = e.scalar_tensor_tensor(out=t1_t[:, h * 128:(h + 1) * 128], in0=og_t[:, 1, h * 128:(h + 1) * 128], scalar=e_t[:, 1, h:h + 1],
                                                     in1=t1_t[:, h * 128:(h + 1) * 128], op0=ALU.mult, op1=ALU.add)
                    return ins
                S.op("dve", f_m1, reads=[og_b, e_b, t1_b], writes=[t1_b])

                def f_m2(e):
                    for h in range(8):
                        ins = e.scalar_tensor_tensor(out=mg_t[:, h * 128:(h + 1) * 128], in0=og_t[:, 2, h * 128:(h + 1) * 128], scalar=e_t[:, 2, h:h + 1],
                                                     in1=t1_t[:, h * 128:(h + 1) * 128], op0=ALU.mult, op1=ALU.add)
                    return ins
                S.op("dve", f_m2, reads=[og_b, e_b, t1_b], writes=[mg_b])
                tp_t, tp_b = tp.next()
                mT_t, mT_b = mT.next()

                def f_tr(e):
                    for k in range(8):
                        ins = e.transpose(tp_t[:, k, :], mg_t[:, k * 128:(k + 1) * 128], self.ident[:])
                    return ins
                S.op("pe", f_tr, reads=[mg_b, self.bident], writes=[tp_b])
                S.op("act", lambda e: e.copy(out=mT_t[:], in_=tp_t[:]), reads=[tp_b], writes=[mT_b])
                o_t, o_b = o1.next()
                self.proj_ln_tile(Wo, bWo, 8, lambda k: mT_t[:, k, :], [mT_b], xr_t, xr_b, lnc, tmps, pz, zr, o_t, o_b)
                self.dma(self.X1[si][t * 128:(t + 1) * 128, :], o_t[:], reads=[o_b], appends=[self.bX1], key=o_b.name)
                if t % 4 == 0:
                    st["x"] = xTg.next()
                xTg_t, xTg_b = st["x"]
                self.emit_xT(o_t, o_b, ob, tp, xTg_t, xTg_b, t % 4, t % 4 == 0)
                if t % 4 == 3:
                    self.dma(self.X1T[si].rearrange("k p s -> p k s")[:, :, (t - 3) * 128:(t + 1) * 128], xTg_t[:], reads=[xTg_b], appends=[self.bX1T], key=xTg_b.name)
            nxt = load(*tiles[0])
            for ti, (si, t) in enumerate(tiles):
                cur = nxt
                if ti + 1 < len(tiles):
                    nxt = load(*tiles[ti + 1])
                do_tile(si, t, *cur)
            S.fence(self.bX1)
            S.fence(self.bX1T)
        S.barrier()

    def phase_C1(self, li, XT, bXT):
        S = self.S
        with ExitStack() as es:
            self.es = es
            Wi, bWi = self.sb("Wi", [128, 8, 2 * DFF], BF16)
            self.load_w(Wi, bWi, "Wi", [(Wi[:, k, :], self.ffn_w_in[li, k * 128:(k + 1) * 128, :]) for k in range(8)])
            cp, bcp = self.sb("cp", [128, 44, 4], F32)
            for t in range(3):
                src = bass.AP(tensor=self.ffn_conv_w.tensor, offset=(li * 3 + t) * 2 * DFF, ap=[[1, 128], [128, 44], [1, 1]])
                kw = dict(writes=[bcp]) if t == 0 else dict(appends=[bcp])
                self.dma(cp[:, :, t:t + 1], src, key="cp", slow=True, **kw)
            src = bass.AP(tensor=self.ffn_conv_b.tensor, offset=li * 2 * DFF, ap=[[1, 128], [128, 44], [1, 1]])
            self.dma(cp[:, :, 3:4], src, appends=[bcp], key="cp", slow=True)
            xt = Ring([self.sb("xt%d" % i, [128, 8, 512], BF16) for i in range(2)])
            ca = Ring([self.sb("ca%d" % i, [128, 512], F32) for i in range(3)])
            cg = Ring([self.sb("cg%d" % i, [128, 512], F32) for i in range(3)])
            gg = Ring([self.sb("gg%d" % i, [128, 512], F32) for i in range(3)])
            uu = Ring([self.sb("uu%d" % i, [128, 512], BF16) for i in range(4)])
            pa = Ring([self.ps("pa%d" % i, [128, 512]) for i in range(3)])
            pg = Ring([self.ps("pg%d" % i, [128, 512]) for i in range(3)])
            tiles = []
            for si, Sq in enumerate(self.seqs):
                a = 0
                while a < Sq:
                    b = min(a + 510, Sq)
                    tiles.append((si, a, b))
                    a = b

            def load(si, a, b):
                Sq = self.seqs[si]
                xt_t, xt_b = xt.next()
                lo, hi = max(a - 1, 0), min(b + 1, Sq)
                c0 = lo - (a - 1)
                w = b - a + 2
                first = True
                if a == 0:
                    S.op("pool", lambda e: e.memset(xt_t[:, :, 0:2], 0.0), writes=[xt_b])
                    first = False
                if b == Sq:
                    kw = dict(writes=[xt_b]) if first else dict(appends=[xt_b])
                    e0 = (w - 2) if (w % 2 == 0) else (w - 1)
                    S.op("pool", lambda e: e.memset(xt_t[:, :, e0:e0 + 2], 0.0), **kw)
                    first = False
                self.dma(xt_t[:, :, c0:c0 + hi - lo], XT[si].rearrange("k p s -> p k s")[:, :, lo:hi], reads=[bXT], writes=[xt_b], key=xt_b.name)
                return xt_t, xt_b
            def do_tile(si, a, b, xt_t, xt_b):
                nb = b - a
                w = nb + 2
                for j in range(22):
                    pa_t, pa_b = pa.next()
                    pg_t, pg_b = pg.next()

                    def f_h(e, pa_t=pa_t, pg_t=pg_t, j=j):
                        for k in range(8):
                            e.matmul(pa_t[:, 0:w], Wi[:, k, j * 128:(j + 1) * 128], xt_t[:, k, 0:w], start=(k == 0), stop=(k == 7))
                        for k in range(8):
                            ins = e.matmul(pg_t[:, 0:w], Wi[:, k, DFF + j * 128:DFF + (j + 1) * 128], xt_t[:, k, 0:w], start=(k == 0), stop=(k == 7))
                        return ins
                    S.op("pe", f_h, reads=[bWi, xt_b], writes=[pa_b, pg_b])
                    ca_t, ca_b = ca.next()
                    cg_t, cg_b = cg.next()
                    gg_t, gg_b = gg.next()
                    uu_t, uu_b = uu.next()
                    for (p_t, p_b, c_t, c_b, jj) in ((pa_t, pa_b, ca_t, ca_b, j), (pg_t, pg_b, cg_t, cg_b, 22 + j)):
                        S.op("act", lambda e, p_t=p_t, c_t=c_t, jj=jj: e.activation(out=c_t[:, 0:nb], in_=p_t[:, 0:nb], func=AF.Identity,
                                                                                   scale=cp[:, jj, 0:1], bias=cp[:, jj, 3:4]),
                             reads=[p_b, bcp], writes=[c_b])
                        S.op("dve", lambda e, p_t=p_t, c_t=c_t, jj=jj: e.scalar_tensor_tensor(out=c_t[:, 0:nb], in0=p_t[:, 1:nb + 1], scalar=cp[:, jj, 1:2],
                                                                                             in1=c_t[:, 0:nb], op0=ALU.mult, op1=ALU.add),
                             reads=[p_b, bcp, c_b], writes=[c_b])
                        S.op("dve", lambda e, p_t=p_t, c_t=c_t, jj=jj: e.scalar_tensor_tensor(out=c_t[:, 0:nb], in0=p_t[:, 2:nb + 2], scalar=cp[:, jj, 2:3],
                                                                                             in1=c_t[:, 0:nb], op0=ALU.mult, op1=ALU.add),
                             reads=[p_b, bcp, c_b], writes=[c_b])
                    S.op("act", lambda e, cg_t=cg_t, gg_t=gg_t: e.activation(out=gg_t[:, 0:nb], in_=cg_t[:, 0:nb], func=AF.Gelu), reads=[cg_b], writes=[gg_b])
                    S.op("pool", lambda e, ca_t=ca_t, gg_t=gg_t, uu_t=uu_t: e.tensor_tensor(out=uu_t[:, 0:nb], in0=ca_t[:, 0:nb], in1=gg_t[:, 0:nb], op=ALU.mult),
                         reads=[ca_b, gg_b], writes=[uu_b])
                    self.dma(self.UT[si][j, :, a:b], uu_t[:, 0:nb], reads=[uu_b], appends=[self.bUT], key=uu_b.name)
            nxt = load(*tiles[0])
            for ti, (si, a, b) in enumerate(tiles):
                cur = nxt
                if ti + 1 < len(tiles):
                    nxt = load(*tiles[ti + 1])
                do_tile(si, a, b, *cur)
            S.fence(self.bUT)
        S.barrier()

    def phase_C2(self, li, RES, bRES, OUT, bOUT, OUTT, bOUTT):
        S = self.S
        with ExitStack() as es:
            self.es = es
            Wo, bWo = self.sb("Wout", [128, 22, DM], BF16)
            self.load_w(Wo, bWo, "Wout", [(Wo[:, k, :], self.ffn_w_out[li, k * 128:(k + 1) * 128, :]) for k in range(22)])
            lnc = self.ln_consts(li, 1)
            ut = Ring([self.sb("ut%d" % i, [128, 22, 512], BF16) for i in range(2)])
            xr = Ring([self.sb("xr%d" % i, [128, DM], F32) for i in range(3)])
            zr = Ring([self.sb("z%d" % i, [128, DM], F32) for i in range(2)])
            o1 = Ring([self.sb("o1_%d" % i, [128, DM], F32) for i in range(3)])
            ob = Ring([self.sb("ob%d" % i, [128, DM], BF16) for i in range(2)])
            xTg = Ring([self.sb("xTg%d" % i, [128, 8, 512], BF16) for i in range(2)])
            tmps = Ring([self.ln_tmp() for i in range(2)])
            tp = Ring([self.ps("tp%d" % i, [128, 8, 128], BF16) for i in range(2)])
            pz = Ring([self.ps("pz%d" % i, [128, DM]) for i in range(2)])
            tiles = [(si, t) for si, Sq in enumerate(self.seqs) for t in range(Sq // 512)]

            def load(si, t):
                ut_t, ut_b = ut.next()
                self.dma(ut_t[:], self.UT[si].rearrange("j p s -> p j s")[:, :, t * 512:(t + 1) * 512], reads=[self.bUT], writes=[ut_b], key=ut_b.name)
                return ut_t, ut_b
            def do_tile(si, t, ut_t, ut_b):
                if OUTT is not None:
                    xTg_t, xTg_b = xTg.next()
                for s4 in range(4):
                    r0 = t * 512 + s4 * 128
                    xr_t, xr_b = xr.next()
                    self.dma(xr_t[:], RES[si][r0:r0 + 128, :], reads=[bRES], writes=[xr_b], key=xr_b.name)
                    o_t, o_b = o1.next()
                    self.proj_ln_tile(Wo, bWo, 22, lambda k, s4=s4: ut_t[:, k, s4 * 128:(s4 + 1) * 128], [ut_b], xr_t, xr_b, lnc, tmps, pz, zr, o_t, o_b)
                    self.dma(OUT[si][r0:r0 + 128, :], o_t[:], reads=[o_b], appends=[bOUT], key=o_b.name)
                    if OUTT is not None:
                        self.emit_xT(o_t, o_b, ob, tp, xTg_t, xTg_b, s4, s4 == 0)
                if OUTT is not None:
                    self.dma(OUTT[si].rearrange("k p s -> p k s")[:, :, t * 512:(t + 1) * 512], xTg_t[:], reads=[xTg_b], appends=[bOUTT], key=xTg_b.name)
            nxt = load(*tiles[0])
            for ti, (si, t) in enumerate(tiles):
                cur = nxt
                if ti + 1 < len(tiles):
                    nxt = load(*tiles[ti + 1])
                do_tile(si, t, *cur)
            S.fence(bOUT)
            if OUTT is not None:
                S.fence(bOUTT)
        S.barrier()

    def phase_D(self):
        S = self.S
        with ExitStack() as es:
            self.es = es
            Wd, bWd = self.sb("Wd", [128, 8, 704], BF16)
            pieces = [(Wd[:, k, 0:672], self.w_dkv[0, k * 128:(k + 1) * 128, :]) for k in range(8)]
            self.load_w(Wd, bWd, "Wd", pieces)
            S.op("dve", lambda e: e.tensor_scalar(out=Wd[:, :, 672:688], in0=Wd[:, :, 656:672], scalar1=-1.0, scalar2=None, op0=ALU.mult), reads=[bWd], writes=[bWd])
            S.op("dve", lambda e: e.tensor_copy(out=Wd[:, :, 688:704], in_=Wd[:, :, 640:656]), reads=[bWd], writes=[bWd])
            wuq = self.w_uq[0].rearrange("(k p) (h c) -> k p h c", p=128, c=96)
            Wqn, bWqn = self.sb("Wqn", [128, 3, 16, 64], BF16)
            self.load_w(Wqn, bWqn, "Wqn", [(Wqn[:, k, :, :], wuq[k, :, :, 0:64]) for k in range(3)])
            Wqp, bWqp = self.sb("Wqp", [128, 3, 16, 32], BF16)
            self.load_w(Wqp, bWqp, "Wqp", [(Wqp[:, k, :, :], wuq[k, :, :, 64:96]) for k in range(3)])
            Wqr, bWqr = self.sb("Wqr", [128, 3, 16, 32], BF16)
            S.op("dve", lambda e: e.tensor_scalar(out=Wqr[:, :, :, 0:16], in0=Wqp[:, :, :, 16:32], scalar1=-1.0, scalar2=None, op0=ALU.mult), reads=[bWqp], writes=[bWqr])
            S.op("dve", lambda e: e.tensor_copy(out=Wqr[:, :, :, 16:32], in_=Wqp[:, :, :, 0:16]), reads=[bWqp, bWqr], writes=[bWqr])
            Wk, bWk = self.sb("Wk", [128, 2, 16, 64], BF16)
            wukv = self.w_ukv[0].rearrange("(k p) (h c) -> k p h c", p=128, c=128)
            self.load_w(Wk, bWk, "Wk", [(Wk[:, k, :, :], wukv[k, :, :, 0:64]) for k in range(2)])
            Wv, bWv = self.sb("Wv", [128, 2, 16, 64], BF16)
            self.load_w(Wv, bWv, "Wv", [(Wv[:, k, :, :], wukv[k, :, :, 64:128]) for k in range(2)])
            gq, bgq = self.sb("gq", [128, 384], F32)
            gkv, bgkv = self.sb("gkv", [128, 256], F32)
            self.dma(gq[:], bass.AP(tensor=self.g_q.tensor, offset=0, ap=[[0, 128], [1, 384]]), writes=[bgq], key="gq")
            self.dma(gkv[:], bass.AP(tensor=self.g_kv.tensor, offset=0, ap=[[0, 128], [1, 256]]), writes=[bgkv], key="gkv")
            xt = Ring([self.sb("xt%d" % i, [128, 8, 512], BF16) for i in range(2)])
            cs = Ring([self.sb("cs%d" % i, [128, 2, 512], F32) for i in range(2)])
            cn = Ring([self.sb("cn%d" % i, [128, 704], BF16) for i in range(2)])
            junk = Ring([self.sb("junk%d" % i, [128, 384], F32) for i in range(2)])
            ssr = Ring([self.sb("ssq%d" % i, [128, 4], F32) for i in range(3)])
            cT = Ring([self.sb("cT%d" % i, [128, 7, 512], BF16) for i in range(2)])
            qn = Ring([self.sb("qn%d" % i, [128, 8, 512], BF16) for i in range(2)])
            qr = Ring([self.sb("qr%d" % i, [128, 4, 512], BF16) for i in range(2)])
            kn = Ring([self.sb("kn%d" % i, [128, 8, 512], BF16) for i in range(2)])
            vo = Ring([self.sb("vo%d" % i, [128, 16, 65], BF16) for i in range(3)])
            for (t_, b_) in vo.slots:
                S.op("pool", lambda e, t_=t_: e.memset(t_[:], 1.0), writes=[b_])
            rt = Ring([self.sb("rt%d" % i, [128, 2, 512], F32) for i in range(3)])
            krf = Ring([self.sb("krf%d" % i, [96, 512], BF16) for i in range(2)])
            pc = Ring([self.ps("pc%d" % i, [128, DM]) for i in range(2)])
            tp = Ring([self.ps("tp%d" % i, [128, 8, 128], BF16) for i in range(1)])
            pq = Ring([self.ps("pq%d" % i, [128, 512]) for i in range(3)])
            tiles = [(si, t) for si, Sq in enumerate(self.seqs) for t in range(Sq // 512)]

            def load(si, t):
                xt_t, xt_b = xt.next()
                self.dma(xt_t[:], self.X2T[si].rearrange("k p s -> p k s")[:, :, t * 512:(t + 1) * 512], reads=[self.bX2T], writes=[xt_b], key=xt_b.name)
                cs_t, cs_b = cs.next()
                for q4 in range(4):
                    kw = dict(writes=[cs_b]) if q4 == 0 else dict(appends=[cs_b])
                    self.dma(cs_t[32 * q4:32 * q4 + 32, :, :], self.c_rope.rearrange("a r s -> r a s")[:, :, t * 512:(t + 1) * 512], key=cs_b.name, **kw)
                return xt_t, xt_b, cs_t, cs_b
            def do_tile(si, t, xt_t, xt_b, cs_t, cs_b):
                cT_t, cT_b = cT.next()
                for s4 in range(4):
                    pc_t, pc_b = pc.next()

                    def f_c(e, pc_t=pc_t, s4=s4):
                        for k in range(8):
                            e.matmul(pc_t[:, 0:512], xt_t[:, k, s4 * 128:(s4 + 1) * 128], Wd[:, k, 0:512], start=(k == 0), stop=(k == 7))
                        for k in range(8):
                            ins = e.matmul(pc_t[:, 512:704], xt_t[:, k, s4 * 128:(s4 + 1) * 128], Wd[:, k, 512:704], start=(k == 0), stop=(k == 7))
                        return ins
                    S.op("pe", f_c, reads=[bWd, xt_b], writes=[pc_b])
                    ss_t, ss_b = ssr.next()
                    jk_t, jk_b = junk.next()

                    def f_sq(e, pc_t=pc_t, ss_t=ss_t, jk_t=jk_t):
                        e.activation(out=jk_t[:, 0:384], in_=pc_t[:, 0:384], func=AF.Square, accum_out=ss_t[:, 0:1])
                        return e.activation(out=jk_t[:, 0:256], in_=pc_t[:, 384:640], func=AF.Square, accum_out=ss_t[:, 1:2])
                    S.op("act", f_sq, reads=[pc_b], writes=[jk_b, ss_b])

                    def f_rs(e, ss_t=ss_t):
                        e.tensor_scalar(out=ss_t[:, 2:3], in0=ss_t[:, 0:1], scalar1=1.0 / 384, scalar2=RMS_EPS, op0=ALU.mult, op1=ALU.add)
                        return e.tensor_scalar(out=ss_t[:, 3:4], in0=ss_t[:, 1:2], scalar1=1.0 / 256, scalar2=RMS_EPS, op0=ALU.mult, op1=ALU.add)
                    S.op("dve", f_rs, reads=[ss_b], writes=[ss_b])
                    S.op("act", lambda e, ss_t=ss_t: e.sqrt(out=ss_t[:, 2:4], in_=ss_t[:, 2:4]), reads=[ss_b], writes=[ss_b])
                    S.op("dve", lambda e, ss_t=ss_t: e.reciprocal(out=ss_t[:, 2:4], in_=ss_t[:, 2:4]), reads=[ss_b], writes=[ss_b])
                    cn_t, cn_b = cn.next()
                    S.op("dve", lambda e, pc_t=pc_t, ss_t=ss_t, cn_t=cn_t: e.scalar_tensor_tensor(out=cn_t[:, 0:384], in0=pc_t[:, 0:384], scalar=ss_t[:, 2:3], in1=gq[:],
                                                                                                   op0=ALU.mult, op1=ALU.mult), reads=[pc_b, ss_b, bgq], writes=[cn_b])
                    S.op("dve", lambda e, pc_t=pc_t, ss_t=ss_t, cn_t=cn_t: e.scalar_tensor_tensor(out=cn_t[:, 384:640], in0=pc_t[:, 384:640], scalar=ss_t[:, 3:4], in1=gkv[:],
                                                                                                   op0=ALU.mult, op1=ALU.mult), reads=[pc_b, ss_b, bgkv], appends=[cn_b])
                    S.op("dve", lambda e, pc_t=pc_t, cn_t=cn_t: e.tensor_copy(out=cn_t[:, 640:704], in_=pc_t[:, 640:704]), reads=[pc_b], appends=[cn_b])
                    tp_t, tp_b = tp.next()

                    def f_tr(e, tp_t=tp_t, cn_t=cn_t):
                        for k in range(5):
                            e.transpose(tp_t[:, k, :], cn_t[:, k * 128:(k + 1) * 128], self.ident[:])
                        e.transpose(tp_t[0:96, 5, :], cn_t[:, 576:672], self.ident[:])
                        return e.transpose(tp_t[0:96, 6, :], cn_t[:, 608:704], self.ident[:])
                    S.op("pe", f_tr, reads=[cn_b, self.bident], writes=[tp_b])
                    kw = dict(writes=[cT_b]) if s4 == 0 else dict(appends=[cT_b])

                    def f_cp(e, tp_t=tp_t, s4=s4):
                        e.copy(out=cT_t[:, 0:5, s4 * 128:(s4 + 1) * 128], in_=tp_t[:, 0:5, :])
                        return e.copy(out=cT_t[64:96, 5:7, s4 * 128:(s4 + 1) * 128], in_=tp_t[64:96, 5:7, :])
                    S.op("act", f_cp, reads=[tp_b], **kw)
                    pv_t, pv_b = pc.next()

                    def f_v(e, pv_t=pv_t, s4=s4):
                        for j in range(2):
                            for k in range(2):
                                ins = e.matmul(pv_t[:, j * 512:(j + 1) * 512], cT_t[:, 3 + k, s4 * 128:(s4 + 1) * 128], Wv[:, k, 8 * j:8 * j + 8, :],
                                               start=(k == 0), stop=(k == 1))
                        return ins
                    S.op("pe", f_v, reads=[cT_b, bWv], writes=[pv_b])
                    vo_t, vo_b = vo.next()
                    S.op("dve", lambda e, pv_t=pv_t, vo_t=vo_t: e.tensor_copy(out=vo_t[:, :, 0:64], in_=pv_t[:, :].rearrange("p (h c) -> p h c", c=64)),
                         reads=[pv_b], writes=[vo_b])
                    self.dma(self.V2[si][:, :, t * 4 + s4, :].rearrange("h p f -> p h f"), vo_t[:], reads=[vo_b], appends=[self.bQKV], key=vo_b.name)
                rt_t, rt_b = rt.next()
                kf_t, kf_b = krf.next()
                S.op("dve", lambda e: e.tensor_tensor(out=rt_t[64:96, 0, :], in0=cT_t[64:96, 5, :], in1=cs_t[64:96, 0, :], op=ALU.mult), reads=[cT_b, cs_b], writes=[rt_b])
                S.op("pool", lambda e: e.tensor_tensor(out=rt_t[64:96, 1, :], in0=cT_t[64:96, 6, :], in1=cs_t[64:96, 1, :], op=ALU.mult), reads=[cT_b, cs_b], appends=[rt_b])
                S.op("pool", lambda e: e.tensor_tensor(out=kf_t[64:96, :], in0=rt_t[64:96, 0, :], in1=rt_t[64:96, 1, :], op=ALU.add), reads=[rt_b], writes=[kf_b])
                ktd = self.KTd[si].rearrange("h p s -> p h s")
                qtd = self.QTd[si].rearrange("h p s -> p h s")
                cols = slice(t * 512, (t + 1) * 512)
                self.dma(ktd[64:96, :, cols], bcast_mid(kf_t[64:96, :], 16), reads=[kf_b], appends=[self.bQKV], key=kf_b.name)
                qn_t, qn_b = qn.next()
                kn_t, kn_b = kn.next()
                qr_t, qr_b = qr.next()
                for hp in range(8):
                    pq_t, pq_b = pq.next()

                    def f_qn(e, pq_t=pq_t, hp=hp):
                        for k in range(3):
                            ins = e.matmul(pq_t[:, :], Wqn[:, k, 2 * hp:2 * hp + 2, :].rearrange("p a b -> p (a b)"), cT_t[:, k, :], start=(k == 0), stop=(k == 2))
                        return ins
                    S.op("pe", f_qn, reads=[bWqn, cT_b], writes=[pq_b])
                    kw = dict(writes=[qn_b]) if hp == 0 else dict(appends=[qn_b])
                    S.op("act", lambda e, pq_t=pq_t, hp=hp: e.copy(out=qn_t[:, hp, :], in_=pq_t[:, :]), reads=[pq_b], **kw)
                    pk_t, pk_b = pq.next()

                    def f_kn(e, pk_t=pk_t, hp=hp):
                        for k in range(2):
                            ins = e.matmul(pk_t[:, :], Wk[:, k, 2 * hp:2 * hp + 2, :].rearrange("p a b -> p (a b)"), cT_t[:, 3 + k, :], start=(k == 0), stop=(k == 1))
                        return ins
                    S.op("pe", f_kn, reads=[bWk, cT_b], writes=[pk_b])
                    kw = dict(writes=[kn_b]) if hp == 0 else dict(appends=[kn_b])
                    if hp % 2:
                        S.op("dve", lambda e, pk_t=pk_t, hp=hp: e.tensor_copy(out=kn_t[:, hp, :], in_=pk_t[:, :]), reads=[pk_b], **kw)
                    else:
                        S.op("act", lambda e, pk_t=pk_t, hp=hp: e.copy(out=kn_t[:, hp, :], in_=pk_t[:, :]), reads=[pk_b], **kw)
                for hq in range(4):
                    pp_t, pp_b = pq.next()
                    pr_t, pr_b = pq.next()

                    def f_qp(e, pp_t=pp_t, pr_t=pr_t, hq=hq):
                        for k in range(3):
                            e.matmul(pp_t[:, :], Wqp[:, k, 4 * hq:4 * hq + 4, :].rearrange("p a b -> p (a b)"), cT_t[:, k, :], start=(k == 0), stop=(k == 2))
                        for k in range(3):
                            ins = e.matmul(pr_t[:, :], Wqr[:, k, 4 * hq:4 * hq + 4, :].rearrange("p a b -> p (a b)"), cT_t[:, k, :], start=(k == 0), stop=(k == 2))
                        return ins
                    S.op("pe", f_qp, reads=[bWqp, bWqr, cT_b], writes=[pp_b, pr_b])
                    rq_t, rq_b = rt.next()
                    S.op("dve", lambda e, pp_t=pp_t, rq_t=rq_t: e.tensor_tensor(out=rq_t[:, 0, :], in0=pp_t[:, :], in1=cs_t[:, 0, :], op=ALU.mult),
                         reads=[pp_b, cs_b], writes=[rq_b])
                    S.op("dve", lambda e, pr_t=pr_t, rq_t=rq_t: e.tensor_tensor(out=rq_t[:, 1, :], in0=pr_t[:, :], in1=cs_t[:, 1, :], op=ALU.mult),
                         reads=[pr_b, cs_b], appends=[rq_b])
                    kw = dict(writes=[qr_b]) if hq == 0 else dict(appends=[qr_b])
                    S.op("pool", lambda e, rq_t=rq_t, hq=hq: e.tensor_tensor(out=qr_t[:, hq, :], in0=rq_t[:, 0, :], in1=rq_t[:, 1, :], op=ALU.add), reads=[rq_b], **kw)
                qtd5 = self.QTd[si].rearrange("(hp two) p s -> two p hp s", two=2)
                ktd5 = self.KTd[si].rearrange("(hp two) p s -> two p hp s", two=2)
                for two in range(2):
                    self.dma(qtd5[two, 0:64, :, cols], qn_t[64 * two:64 * two + 64, :, :], reads=[qn_b], appends=[self.bQKV], key=qn_b.name)
                    self.dma(ktd5[two, 0:64, :, cols], kn_t[64 * two:64 * two + 64, :, :], reads=[kn_b], appends=[self.bQKV], key=kn_b.name)
                qtd4 = self.QTd[si].rearrange("(hq four) p s -> four p hq s", four=4)
                for j in range(4):
                    self.dma(qtd4[j, 64:96, :, cols], qr_t[32 * j:32 * j + 32, :, :], reads=[qr_b], appends=[self.bQKV], key=qr_b.name)
            nxt = load(*tiles[0])
            for ti, (si, t) in enumerate(tiles):
                cur = nxt
                if ti + 1 < len(tiles):
                    nxt = load(*tiles[ti + 1])
                do_tile(si, t, *cur)
            S.fence(self.bQKV)
        S.barrier()

    def phase_E(self):
        S = self.S
        scale = 96 ** -0.5
        with ExitStack() as es:
            self.es = es
            Smax = max(self.seqs)
            kt = Ring([self.sb("kt%d" % i, [96, Smax], BF16) for i in range(2)])
            vt = Ring([self.sb("vt%d" % i, [128, Smax // 128, 65], BF16) for i in range(2)])
            qt = Ring([self.sb("qt%d" % i, [96, 512], BF16) for i in range(3)])
            PT = Ring([self.sb("PT%d" % i, [128, 512], BF16) for i in range(4)])
            oT = Ring([self.sb("oT%d" % i, [65, 512], F32) for i in range(2)])
            rc = Ring([self.sb("rc%d" % i, [128, 4], F32) for i in range(2)])
            om = Ring([self.sb("om%d" % i, [128, 4, 64], BF16) for i in range(3)])
            psc = Ring([self.ps("psc%d" % i, [128, 512]) for i in range(3)])
            pO = Ring([self.ps("pO%d" % i, [128, 512]) for i in range(2)])
            ptr = Ring([self.ps("ptr%d" % i, [128, 4, 128]) for i in range(2)])
            for si, Sq in enumerate(self.seqs):
                nchk = Sq // 128
                heads = list(range(16))

                def loadh(h, si=si, Sq=Sq, nchk=nchk):
                    kt_t, kt_b = kt.next()
                    vt_t, vt_b = vt.next()
                    self.dma(kt_t[:, 0:Sq], self.KTd[si][h], reads=[self.bQKV], writes=[kt_b], key=kt_b.name)
                    self.dma(vt_t[:, 0:nchk, :], self.V2[si][h], reads=[self.bQKV], writes=[vt_b], key=vt_b.name)
                    return kt_t, kt_b, vt_t, vt_b
                nxh = loadh(0)
                for h in heads:
                    kt_t, kt_b, vt_t, vt_b = nxh
                    if h + 1 < 16:
                        nxh = loadh(h + 1)
                    def do_q(qi, si=si, h=h, kt_t=kt_t, kt_b=kt_b, vt_t=vt_t, vt_b=vt_b, nchk=nchk):
                        q_t, q_b = qt.next()
                        self.dma(q_t[:], self.QTd[si][h, :, qi * 512:(qi + 1) * 512], reads=[self.bQKV], writes=[q_b], key=q_b.name)
                        pO_t, pO_b = pO.next()
                        pend = {}
                        LAG = 2

                        def e_qk(c):
                            ps_t, ps_b = psc.next()
                            pT_t, pT_b = PT.next()
                            S.op("pe", lambda e: e.matmul(ps_t[:, :], kt_t[:, c * 128:(c + 1) * 128], q_t[:, :], start=True, stop=True), reads=[kt_b, q_b], writes=[ps_b])
                            S.op("act", lambda e: e.activation(out=pT_t[:], in_=ps_t[:], func=AF.Exp, scale=scale), reads=[ps_b], writes=[pT_b])
                            pend[c] = (pT_t, pT_b)

                        def e_pv(c):
                            pT_t, pT_b = pend.pop(c)
                            kw = dict(writes=[pO_b]) if c == 0 else dict(appends=[pO_b])
                            S.op("pe", lambda e: e.matmul(pO_t[0:65, :], vt_t[:, c, :], pT_t[:], start=(c == 0), stop=(c == nchk - 1)), reads=[vt_b, pT_b], **kw)
                        for j in range(nchk + LAG):
                            if j < nchk:
                                e_qk(j)
                            if j >= LAG:
                                e_pv(j - LAG)
                        oT_t, oT_b = oT.next()
                        S.op("dve", lambda e: e.tensor_copy(out=oT_t[:], in_=pO_t[0:65, :]), reads=[pO_b], writes=[oT_b])
                        ptr_t, ptr_b = ptr.next()

                        def f_tr(e):
                            for i in range(4):
                                ins = e.transpose(ptr_t[:, i, 0:65], oT_t[:, i * 128:(i + 1) * 128], self.identf[0:65, 0:65])
                            return ins
                        S.op("pe", f_tr, reads=[oT_b, self.bidentf], writes=[ptr_b])
                        rc_t, rc_b = rc.next()
                        om_t, om_b = om.next()
                        S.op("dve", lambda e: e.reciprocal(out=rc_t[:], in_=ptr_t[:, :, 64]), reads=[ptr_b], writes=[rc_b])
                        S.op("dve", lambda e: e.tensor_tensor(out=om_t[:], in0=ptr_t[:, :, 0:64], in1=bcast_last(rc_t[:], 64), op=ALU.mult),
                             reads=[ptr_b, rc_b], writes=[om_b])
                        self.dma(self.OM[si][qi * 512:(qi + 1) * 512, h * 64:(h + 1) * 64].rearrange("(i p) c -> p i c", p=128), om_t[:],
                                 reads=[om_b], appends=[self.bOM], key=om_b.name)
                    for qi in range(Sq // 512):
                        do_q(qi)
            S.fence(self.bOM)
        S.barrier()

    def phase_F(self):
        S = self.S
        with ExitStack() as es:
            self.es = es
            Wo, bWo = self.sb("Wob", [128, 8, DM], BF16)
            self.load_w(Wo, bWo, "Wob", [(Wo[:, k, :], self.w_o_b[0, k * 128:(k + 1) * 128, :]) for k in range(8)])
            lnc = self.ln_consts(1, 0)
            om = Ring([self.sb("om%d" % i, [128, DM], BF16) for i in range(2)])
            xr = Ring([self.sb("xr%d" % i, [128, DM], F32) for i in range(2)])
            mT = Ring([self.sb("mT%d" % i, [128, 8, 128], BF16) for i in range(2)])
            zr = Ring([self.sb("z%d" % i, [128, DM], F32) for i in range(2)])
            o1 = Ring([self.sb("o1_%d" % i, [128, DM], F32) for i in range(2)])
            ob = Ring([self.sb("ob%d" % i, [128, DM], BF16) for i in range(2)])
            xTg = Ring([self.sb("xTg%d" % i, [128, 8, 512], BF16) for i in range(2)])
            tmps = Ring([self.ln_tmp() for i in range(2)])
            tp = Ring([self.ps("tp%d" % i, [128, 8, 128], BF16) for i in range(2)])
            pz = Ring([self.ps("pz%d" % i, [128, DM]) for i in range(2)])
            tiles = [(si, t) for si, Sq in enumerate(self.seqs) for t in range(Sq // 128)]

            def load(si, t):
                om_t, om_b = om.next()
                self.dma(om_t[:], self.OM[si][t * 128:(t + 1) * 128, :], reads=[self.bOM], writes=[om_b], key=om_b.name)
                xr_t, xr_b = xr.next()
                self.dma(xr_t[:], self.X2[si][t * 128:(t + 1) * 128, :], reads=[self.bX2], writes=[xr_b], key=xr_b.name)
                return om_t, om_b, xr_t, xr_b
            st = {}

            def do_tile(si, t, om_t, om_b, xr_t, xr_b):
                tp_t, tp_b = tp.next()
                mT_t, mT_b = mT.next()

                def f_tr(e):
                    for k in range(8):
                        ins = e.transpose(tp_t[:, k, :], om_t[:, k * 128:(k + 1) * 128], self.ident[:])
                    return ins
                S.op("pe", f_tr, reads=[om_b, self.bident], writes=[tp_b])
                S.op("act", lambda e: e.copy(out=mT_t[:], in_=tp_t[:]), reads=[tp_b], writes=[mT_b])
                o_t, o_b = o1.next()
                self.proj_ln_tile(Wo, bWo, 8, lambda k: mT_t[:, k, :], [mT_b], xr_t, xr_b, lnc, tmps, pz, zr, o_t, o_b)
                self.dma(self.X3[si][t * 128:(t + 1) * 128, :], o_t[:], reads=[o_b], appends=[self.bX3], key=o_b.name)
                if t % 4 == 0:
                    st["x"] = xTg.next()
                xTg_t, xTg_b = st["x"]
                self.emit_xT(o_t, o_b, ob, tp, xTg_t, xTg_b, t % 4, t % 4 == 0)
                if t % 4 == 3:
                    self.dma(self.X3T[si].rearrange("k p s -> p k s")[:, :, (t - 3) * 128:(t + 1) * 128], xTg_t[:], reads=[xTg_b], appends=[self.bX3T], key=xTg_b.name)
            nxt = load(*tiles[0])
            for ti, (si, t) in enumerate(tiles):
                cur = nxt
                if ti + 1 < len(tiles):
                    nxt = load(*tiles[ti + 1])
                do_tile(si, t, *cur)
            S.fence(self.bX3)
            S.fence(self.bX3T)
        S.barrier()

    def build(self, stop_after=None):
        with ExitStack() as es0:
            self.es = es0
            self.consts()
            self.es0 = es0
            steps = [("bias", self.phase_bias), ("A0", lambda: self.phase_A(0)), ("A1", lambda: self.phase_A(1)), ("A2", lambda: self.phase_A(2)),
                     ("B", self.phase_B), ("C1a", lambda: self.phase_C1(0, self.X1T, self.bX1T)),
                     ("C2a", lambda: self.phase_C2(0, self.X1, self.bX1, self.X2, self.bX2, self.X2T, self.bX2T)),
                     ("D", self.phase_D), ("E", self.phase_E), ("F", self.phase_F),
                     ("C1b", lambda: self.phase_C1(1, self.X3T, self.bX3T)),
                     ("C2b", lambda: self.phase_C2(1, self.X3, self.bX3, self.y, self.bY, None, None))]
            LAT = {"A0": 0.3, "A1": 0.3, "A2": 0.3}
            for name, fn in steps:
                self.S.lat = LAT.get(name, 2.0)
                fn()
                if stop_after == name:
                    break
            self.S.barrier()
            self.stats = self.S.emit()
        return self.nc


def _t5_bucket_np(rel):
    nb = 16
    max_exact = 8
    rel = np.asarray(rel, np.int64)
    ret = np.where(rel > 0, nb, 0)
    n = np.abs(rel)
    nf = np.maximum(n, 1).astype(np.float32)
    large = max_exact + (np.log(nf / np.float32(max_exact)) / np.float32(math.log(1024 / max_exact)) * np.float32(nb - max_exact)).astype(np.int32)
    large = np.minimum(large, nb - 1)
    return ret + np.where(n < max_exact, n, large)


def _host_consts():
    oh = np.zeros((3, 32, 384), np.float32)
    ng = np.full((3, 384), NEGV, np.float32)
    n = np.arange(384)
    m = n - 127
    valid = (m >= 0) & (m <= 128)
    rel = 64 - m
    for g, d in enumerate(DILS):
        b = _t5_bucket_np(rel * d)
        for i in range(384):
            if valid[i]:
                oh[g, b[i], i] = 1.0
                ng[g, i] = 0.0
    inv = (1.0 / (np.float32(10000.0) ** (np.arange(0, 32, 2, dtype=np.float32) / np.float32(32)))).astype(np.float32)
    ang = (np.arange(SMAX, dtype=np.float32)[:, None] * inv[None, :]).astype(np.float32)
    cos = np.cos(ang.astype(np.float64)).astype(np.float32).T
    sin = np.sin(ang.astype(np.float64)).astype(np.float32).T
    rope = np.stack([np.concatenate([cos, cos], 0), np.concatenate([sin, sin], 0)], 0)
    return oh, ng, np.ascontiguousarray(rope)


_CACHE = {}


def run(inputs, seqs, n_cores, xs):
    key = tuple(seqs)
    if key not in _CACHE:
        _CACHE[key] = MK(list(seqs)).build()
    nc = _CACHE[key]
    oh, ng, rope = _host_consts()
    shared = {k: np.ascontiguousarray(np.asarray(inputs[k], np.float32)) for k in
              ("rel_bias", "w_qkv_a", "w_o_a", "w_dkv_b", "g_q_b", "g_kv_b", "w_uq_b", "w_ukv_b", "w_o_b",
               "ffn_w_in", "ffn_conv_w", "ffn_conv_b", "ffn_w_out", "ln_g", "ln_b")}
    shared["c_onehot"] = oh
    shared["c_negv"] = ng
    shared["c_rope"] = rope
    in_maps = []
    for c in range(n_cores):
        m = dict(shared)
        for i in range(len(seqs)):
            m["x%d" % i] = np.ascontiguousarray(xs[c][i])
        in_maps.append(m)
    res = run_bass_kernel_spmd(nc, in_maps, core_ids=list(range(n_cores)))
    return [[np.asarray(r["y%d" % i]) for i in range(len(seqs))] for r in res.results]


def kernel(**inputs):
    xp = np.asarray(inputs["x_prompt"], np.float32)
    xs_ = np.asarray(inputs["x_sample"], np.float32)
    n = 8
    outs = run(inputs, (xp.shape[1], xs_.shape[1]), n, [[xp[c], xs_[c]] for c in range(n)])
    yp = np.stack([outs[c][0] for c in range(n)], 0).astype(np.float32)
    ys = np.stack([outs[c][1] for c in range(n)], 0).astype(np.float32)
    return (yp, ys)
```

```python
import numpy as np
import concourse.bass as bass
import concourse.mybir as mybir
from concourse.bass_utils import run_bass_kernel_spmd

F32 = mybir.dt.float32
BF16 = mybir.dt.bfloat16
AF = mybir.ActivationFunctionType
ALU = mybir.AluOpType
AX = mybir.AxisListType


class Buf:
    __slots__ = ("name", "writers", "readers", "prev_readers", "psum")

    def __init__(self, name, psum=False):
        self.name = name
        self.psum = psum
        self.writers = []
        self.readers = []
        self.prev_readers = []


class Op:
    __slots__ = ("eng", "fn", "deps", "is_dma", "need_inc", "sem", "val", "grp", "idx", "semkey", "cost", "done", "cons", "barrier", "lat", "win")

    def __init__(self, eng, fn, is_dma, semkey):
        self.cost = 0.0
        self.lat = 0.3
        self.win = 64
        self.barrier = False
        self.done = None
        self.cons = None
        self.eng = eng
        self.fn = fn
        self.deps = []
        self.is_dma = is_dma
        self.need_inc = False
        self.sem = None
        self.val = 0
        self.grp = None
        self.semkey = semkey


EPOCH = 30000


class Sched:
    ENG = ("pe", "act", "dve", "pool", "sp")

    def __init__(self, nc, same_engine_sync=True):
        self.nc = nc
        self.ops = []
        self.last = {e: None for e in self.ENG}
        self.dma_since_barrier = []
        self.bounds = []
        self.lat = 1.2
        self.win = 64
        self.same_engine_sync = same_engine_sync
        self.engobj = {"pe": nc.tensor, "act": nc.scalar, "dve": nc.vector, "pool": nc.gpsimd, "sp": nc.sync}

    DEFCOST = {"pe": 0.3, "act": 0.5, "dve": 0.5, "pool": 1.0, "sp": 0.05}

    class _Probe:
        def __init__(self, eng):
            self.eng = eng
            self.cost = 0.0

        def __getattr__(self, name):
            def call(*a, **k):
                F = 1
                for v in list(a) + list(k.values()):
                    sh = getattr(v, "shape", None)
                    if sh is not None and len(sh) >= 1:
                        f = 1
                        for d in sh[1:]:
                            f *= d
                        if f > F:
                            F = f
                e = self.eng
                if e == "pe":
                    self.cost += max(F, 64) / 1950.0 + 0.02
                elif e == "dve":
                    self.cost += 0.08 + F / 960.0
                elif e == "act":
                    self.cost += 0.13 + F / 1400.0
                elif e == "pool":
                    self.cost += 0.1 + F / 430.0
                else:
                    self.cost += 0.05
                return None
            return call

    def probe_cost(self, eng, fn):
        try:
            p = Sched._Probe(eng)
            fn(p)
            return p.cost if p.cost > 0 else self.DEFCOST[eng]
        except Exception:
            return self.DEFCOST[eng]

    def op(self, eng, fn, reads=(), writes=(), appends=(), dma=False, semkey=None, cost=None):
        o = Op(eng, fn, dma, semkey)
        o.cost = (0.05 if dma else self.probe_cost(eng, fn)) if cost is None else cost
        o.lat = self.lat
        o.win = self.win
        deps = o.deps
        for b in reads:
            for w in b.writers:
                deps.append((w, "raw"))
            if b.psum:
                for r in b.readers:
                    if r.eng != eng:
                        deps.append((r, "raw"))
        for b in writes:
            for w in b.writers:
                deps.append((w, "waw"))
            for r in b.readers:
                deps.append((r, "war"))
            for r in b.prev_readers:
                deps.append((r, "war"))
        for b in appends:
            for r in b.readers:
                deps.append((r, "war"))
            for r in b.prev_readers:
                deps.append((r, "war"))
            for w in b.writers:
                if w.eng == eng:
                    deps.append((w, "ord"))
        for b in reads:
            b.readers.append(o)
        for b in writes:
            b.prev_readers = list(b.readers)
            b.readers = []
            b.writers = [o]
        for b in appends:
            b.writers.append(o)
        if dma:
            if semkey is None:
                raise ValueError("dma op needs semkey")
            self.dma_since_barrier.append(o)
        self.ops.append(o)
        self.last[eng] = o
        return o

    def fence(self, buf, eng="sp"):
        o = Op(eng, lambda e: e.nop(), False, None)
        for w in buf.writers:
            o.deps.append((w, "raw"))
        buf.writers = [o]
        self.ops.append(o)
        self.last[eng] = o
        return o

    def barrier(self):
        prods = [o for o in self.last.values() if o is not None] + list(self.dma_since_barrier)
        self.dma_since_barrier = []
        self.bounds.append(len(self.ops))
        for e in self.ENG:
            o = Op(e, lambda en: en.nop(), False, None)
            o.barrier = True
            o.cost = 0.05
            for p in prods:
                o.deps.append((p, "raw"))
            self.ops.append(o)
            self.last[e] = o

    def reorder(self, window=64):
        import heapq
        DLAT = 3.0
        out = []
        bounds = [0] + list(self.bounds) + [len(self.ops)]
        last_by_eng = {}
        for a, b in zip(bounds, bounds[1:]):
            seg = self.ops[a:b]
            if not seg:
                continue
            for o in seg:
                if o.barrier:
                    o.deps = [(p, k) for (p, k) in o.deps if p.is_dma] + [(p, "raw") for p in last_by_eng.values()]
            inseg = set(id(o) for o in seg)
            q = {e: [] for e in self.ENG}
            for o in seg:
                o.done = None
                o.cons = set()
                q[o.eng].append(o)
            for o in seg:
                for (p, kind) in o.deps:
                    if id(p) in inseg:
                        p.cons.add(o.eng)
            window = max(o.win for o in seg)
            ptr = {e: 0 for e in self.ENG}
            sched = set()
            free_at = {e: 0.0 for e in self.ENG}
            best = {e: None for e in self.ENG}

            def find(e):
                lst = q[e]
                i = ptr[e]
                n = len(lst)
                while i < n and id(lst[i]) in sched:
                    i += 1
                ptr[e] = i
                bs, bo = None, None
                cnt = 0
                j = i
                while j < n and cnt < window:
                    o = lst[j]
                    j += 1
                    if id(o) in sched:
                        continue
                    cnt += 1
                    st = free_at[e]
                    ok = True
                    for (p, kind) in o.deps:
                        if id(p) not in inseg:
                            continue
                        if kind == "ord" and p.eng != e:
                            continue
                        if p.done is None:
                            ok = False
                            break
                        t = p.done if p.eng != e or p.is_dma else p.done - 0.0
                        if p.eng == e and not p.is_dma:
                            t = 0.0
                        if t > st:
                            st = t
                    if ok and (bs is None or st < bs - 1e-9):
                        bs, bo = st, o
                        if st <= free_at[e] + 1e-9:
                            break
                best[e] = (bs, bo) if bo is not None else None
            for e in self.ENG:
                find(e)
            nleft = len(seg)
            newseg = []
            while nleft:
                ce, cs = None, None
                for e in self.ENG:
                    if best[e] is not None and (cs is None or best[e][0] < cs):
                        ce, cs = e, best[e][0]
                if ce is None:
                    raise RuntimeError("scheduler stuck (dependency cycle or window too small)")
                o = best[ce][1]
                sched.add(id(o))
                free_at[ce] = cs + o.cost
                o.done = cs + (DLAT if o.is_dma else o.cost + o.lat)
                newseg.append((cs, len(newseg), o))
                nleft -= 1
                find(ce)
                for e in o.cons:
                    if e != ce:
                        find(e)
            newseg.sort(key=lambda x: (x[0], x[1]))
            for (_, _, o) in newseg:
                out.append(o)
                if not o.is_dma:
                    last_by_eng[o.eng] = o
        return out

    def emit(self):
        nc = self.nc
        import os as _os
        if _os.environ.get("MK_NOREORDER", "") != "1":
            self.ops = self.reorder()
        for o in self.ops:
            for (p, kind) in o.deps:
                if kind == "ord":
                    continue
                if p.is_dma:
                    continue
                if p.eng == o.eng and not o.is_dma and not (kind == "raw" and self.same_engine_sync and o.eng != "pe"):
                    continue
                p.need_inc = True
        cnt = {e: 0 for e in self.ENG}
        sems = {}

        def get_sem(key):
            if key not in sems:
                sems[key] = nc.alloc_semaphore(name="s_%s" % (str(key).replace(" ", "_"),))
            return sems[key]

        dma_cnt = {}
        dma_grp = {}
        for o in self.ops:
            if o.is_dma:
                k = ("dma", o.semkey)
                dma_cnt[k] = dma_cnt.get(k, 0) + 16
                o.sem = k
                o.val = dma_cnt[k]
            elif o.need_inc:
                c = cnt[o.eng]
                cnt[o.eng] = c + 1
                o.sem = (o.eng, c // EPOCH)
                o.val = c % EPOCH + 1
        self.stats = dict(cnt)
        self.stats["n_ops"] = len(self.ops)
        waited = {e: {} for e in self.ENG}
        n_wait = 0
        for o in self.ops:
            E = self.engobj[o.eng]
            wd = waited[o.eng]
            need = {}
            for (p, kind) in o.deps:
                if kind == "ord":
                    continue
                if not p.is_dma:
                    if p.eng == o.eng and not o.is_dma and not (kind == "raw" and self.same_engine_sync and o.eng != "pe"):
                        continue
                k, v = p.sem, p.val
                if v > wd.get(k, 0) and v > need.get(k, 0):
                    need[k] = v
            for k, v in need.items():
                E.wait_ge(get_sem(k), v)
                wd[k] = v
                n_wait += 1
            ins = o.fn(E)
            if o.is_dma:
                ins.then_inc(get_sem(o.sem), 16)
            elif o.need_inc:
                ins.then_inc(get_sem(o.sem), 1)
        self.stats["n_wait"] = n_wait
        self.stats["n_sems"] = len(sems)
        return self.stats


from contextlib import ExitStack
import math

DM = 1024
DFF = 2816
DILS = (1, 4, 16)
ALPHA = (2.0 * 2) ** 0.25
LN_EPS = 1e-5
RMS_EPS = 1e-6
NEGV = -30000.0
SMAX = 8192


class Ring:
    def __init__(self, slots):
        self.slots = slots
        self.i = 0

    def next(self):
        s = self.slots[self.i % len(self.slots)]
        self.i += 1
        return s


def bcast_last(ap, n):
    return bass.AP(tensor=ap.tensor, offset=ap.offset, ap=[list(x) for x in ap.ap] + [[0, n]])


def bcast_mid(ap, n):
    a = [list(x) for x in ap.ap]
    return bass.AP(tensor=ap.tensor, offset=ap.offset, ap=[a[0], [0, n]] + a[1:])


class MK:
    def __init__(self, seqs, debug=False):
        self.seqs = seqs
        self.debug = debug
        nc = self.nc = bass.Bass("TRN2", target_bir_lowering=False)
        self.S = Sched(nc)
        self.es = None
        D = lambda n, s, dt=F32: nc.dram_tensor(n, s, dt, kind="ExternalInput").ap()
        self.x = [D("x%d" % i, [s, DM]) for i, s in enumerate(seqs)]
        self.y = [nc.dram_tensor("y%d" % i, [s, DM], F32, kind="ExternalOutput").ap() for i, s in enumerate(seqs)]
        self.rel_bias = D("rel_bias", [32, 24])
        self.w_qkv = D("w_qkv_a", [1, DM, 9216])
        self.w_o_a = D("w_o_a", [1, DM, DM])
        self.w_dkv = D("w_dkv_b", [1, DM, 672])
        self.g_q = D("g_q_b", [1, 384])
        self.g_kv = D("g_kv_b", [1, 256])
        self.w_uq = D("w_uq_b", [1, 384, 1536])
        self.w_ukv = D("w_ukv_b", [1, 256, 2048])
        self.w_o_b = D("w_o_b", [1, DM, DM])
        self.ffn_w_in = D("ffn_w_in", [2, DM, 2 * DFF])
        self.ffn_conv_w = D("ffn_conv_w", [2, 3, 2 * DFF])
        self.ffn_conv_b = D("ffn_conv_b", [2, 2 * DFF])
        self.ffn_w_out = D("ffn_w_out", [2, DFF, DM])
        self.ln_g = D("ln_g", [2, 2, DM])
        self.ln_b = D("ln_b", [2, 2, DM])
        self.c_onehot = D("c_onehot", [3, 32, 384])
        self.c_negv = D("c_negv", [3, 384])
        self.c_rope = D("c_rope", [2, 32, SMAX])
        T = lambda n, s, dt: (nc.dram_tensor(n, s, dt, kind="ExternalOutput").ap() if (debug and n[:2] in ("X1", "X2", "X3", "OG", "BT", "OM", "UT", "QT", "KT", "V2")) else nc.dram_tensor(n, s, dt).ap())
        self.PADS = T("PADS", [24, 384], F32)
        self.bPADS = Buf("PADS")
        self.BT = T("BT", [24, 128, 256], F32)
        self.bBT = Buf("BT")
        self.OG = [[T("OG%d_%d" % (g, i), [s, 1032], F32) for i, s in enumerate(seqs)] for g in range(3)]
        self.bOG = Buf("OG")
        self.X1 = [T("X1_%d" % i, [s, DM], F32) for i, s in enumerate(seqs)]
        self.X1T = [T("X1T_%d" % i, [8, 128, s], BF16) for i, s in enumerate(seqs)]
        self.X2 = [T("X2_%d" % i, [s, DM], F32) for i, s in enumerate(seqs)]
        self.X2T = [T("X2T_%d" % i, [8, 128, s], BF16) for i, s in enumerate(seqs)]
        self.X3 = [T("X3_%d" % i, [s, DM], F32) for i, s in enumerate(seqs)]
        self.X3T = [T("X3T_%d" % i, [8, 128, s], BF16) for i, s in enumerate(seqs)]
        self.UT = [T("UT_%d" % i, [22, 128, s], BF16) for i, s in enumerate(seqs)]
        self.QTd = [T("QT_%d" % i, [16, 96, s], BF16) for i, s in enumerate(seqs)]
        self.KTd = [T("KT_%d" % i, [16, 96, s], BF16) for i, s in enumerate(seqs)]
        self.V2 = [T("V2_%d" % i, [16, 128, s // 128, 65], BF16) for i, s in enumerate(seqs)]
        self.OM = [T("OM_%d" % i, [s, DM], BF16) for i, s in enumerate(seqs)]
        self.bX1 = Buf("X1"); self.bX1T = Buf("X1T"); self.bX2 = Buf("X2"); self.bX2T = Buf("X2T")
        self.bX3 = Buf("X3"); self.bX3T = Buf("X3T"); self.bUT = Buf("UT"); self.bQKV = Buf("QKV"); self.bOM = Buf("OM")
        self.bY = Buf("Y")
        self.uid = 0

    def sb(self, name, shape, dt):
        self.uid += 1
        t = self.es.enter_context(self.nc.sbuf_tensor("%s_%d" % (name, self.uid), shape, dt))
        return t, Buf(name)

    def ps(self, name, shape, dt=F32):
        self.uid += 1
        t = self.es.enter_context(self.nc.psum_tensor("%s_%d" % (name, self.uid), shape, dt))
        return t, Buf(name, psum=True)

    def dma(self, out, in_, reads=(), writes=(), appends=(), key=None, eng="sp", slow=False):
        if slow:
            fn = lambda e: e.dma_start(out=out, in_=in_, allow_slow_non_contiguous=True)
        else:
            fn = lambda e: e.dma_start(out=out, in_=in_)
        return self.S.op(eng, fn, reads=reads, writes=writes, appends=appends, dma=True, semkey=key)

    def load_w(self, dst_t, dst_b, key, pieces):
        for i, (o, s) in enumerate(pieces):
            if i == 0:
                self.dma(o, s, writes=[dst_b], key=key, eng="pool")
            else:
                self.dma(o, s, appends=[dst_b], key=key, eng="pool")

    def consts(self):
        S = self.S
        self.ident, self.bident = self.sb("ident", [128, 128], BF16)
        self.identf, self.bidentf = self.sb("identf", [128, 128], F32)
        self.Jf, self.bJf = self.sb("Jf", [128, 128], F32)
        self.ones, self.bones = self.sb("ones", [128, 2], BF16)
        identf, ident, Jf, ones = self.identf, self.ident, self.Jf, self.ones
        S.op("pool", lambda e: e.memset(identf[:], 1.0), writes=[self.bidentf])
        S.op("pool", lambda e: e.affine_select(out=identf[:], in_=identf[:], pattern=[[-1, 128]], compare_op=ALU.is_equal,
                                               fill=0.0, base=0, channel_multiplier=1), reads=[self.bidentf], writes=[self.bidentf])
        S.op("dve", lambda e: e.tensor_copy(out=ident[:], in_=identf[:]), reads=[self.bidentf], writes=[self.bident])
        S.op("pool", lambda e: e.memset(Jf[:], 1.0), writes=[self.bJf])
        S.op("pool", lambda e: e.affine_select(out=Jf[:], in_=Jf[:], pattern=[[1, 128]], compare_op=ALU.is_equal,
                                               fill=0.0, base=-127, channel_multiplier=1), reads=[self.bJf], writes=[self.bJf])
        S.op("pool", lambda e: e.memset(ones[:], 1.0), writes=[self.bones])

    def ln_consts(self, li, lj):
        g, bg = self.sb("lng", [128, DM], F32)
        b, bb = self.sb("lnb", [128, DM], F32)
        gs = bass.AP(tensor=self.ln_g.tensor, offset=(li * 2 + lj) * DM, ap=[[0, 128], [1, DM]])
        bs = bass.AP(tensor=self.ln_b.tensor, offset=(li * 2 + lj) * DM, ap=[[0, 128], [1, DM]])
        self.dma(g[:], gs, writes=[bg], key="lng")
        self.dma(b[:], bs, writes=[bb], key="lnb")
        return (g, bg, b, bb)

    def phase_bias(self):
        S, nc = self.S, self.nc
        with ExitStack() as es:
            self.es = es
            rb, brb = self.sb("rb", [32, 24], F32)
            oh, boh = self.sb("oh", [32, 3, 384], F32)
            ng, bng = self.sb("ng", [8, 3, 384], F32)
            pads, bpads = self.sb("pads", [8, 3, 384], F32)
            pp, bpp = self.ps("pp", [128, 512])
            self.dma(rb[:], self.rel_bias[:, :], writes=[brb], key="rb")
            self.dma(oh[:], self.c_onehot.rearrange("g b n -> b g n"), writes=[boh], key="oh")
            self.dma(ng[:], bass.AP(tensor=self.c_negv.tensor, offset=0, ap=[[0, 8], [384, 3], [1, 384]]), writes=[bng], key="ng")
            for g in range(3):
                S.op("pe", lambda e, g=g: e.matmul(pp[0:8, 0:384], rb[:, g * 8:(g + 1) * 8], oh[:, g, :], start=True, stop=True),
                     reads=[brb, boh], writes=[bpp])
                S.op("dve", lambda e, g=g: e.tensor_tensor(out=pads[:, g, :], in0=pp[0:8, 0:384], in1=ng[:, g, :], op=ALU.add),
                     reads=[bpp, bng], **(dict(writes=[bpads]) if g == 0 else dict(appends=[bpads])))
            self.dma(self.PADS.rearrange("(g h) n -> h g n", g=3), pads[:], reads=[bpads], writes=[self.bPADS], key="pads_o")
            hk = Ring([self.sb("hk%d" % i, [128, 256], F32) for i in range(2)])
            tb = Ring([self.sb("tb%d" % i, [128, 256], F32) for i in range(2)])
            pt = Ring([self.ps("ptz%d" % i, [128, 512]) for i in range(2)])
            for hh in range(24):
                h_t, h_b = hk.next()
                t_t, t_b = tb.next()
                p_t, p_b = pt.next()
                src = bass.AP(tensor=self.PADS.tensor, offset=hh * 384, ap=[[1, 128], [1, 256]])
                self.dma(h_t[:], src, reads=[self.bPADS], writes=[h_b], key="hk%d" % (hh % 2))
                S.op("pe", lambda e, p_t=p_t, h_t=h_t: e.matmul(p_t[:, 0:256], self.Jf[:], h_t[:], start=True, stop=True),
                     reads=[self.bJf, h_b], writes=[p_b])
                S.op("dve", lambda e, p_t=p_t, t_t=t_t: e.tensor_copy(out=t_t[:], in_=p_t[:, 0:256]), reads=[p_b], writes=[t_b])
                self.dma(self.BT[hh], t_t[:], reads=[t_b], appends=[self.bBT], key="tb%d" % (hh % 2))
            S.fence(self.bBT)
        S.barrier()

    def phase_A(self, g):
        S, nc = self.S, self.nc
        d = DILS[g]
        scale = 128 ** -0.5
        with ExitStack() as es:
            self.es = es
            Wg, bWg = self.sb("Wg", [128, 8, 3072], BF16)
            self.load_w(Wg, bWg, "Wg", [(Wg[:, k, :], self.w_qkv[0, k * 128:(k + 1) * 128, g * 3072:(g + 1) * 3072]) for k in range(8)])
            bt, bbt = self.sb("biasT", [128, 8, 256], F32)
            self.dma(bt[:], self.BT[g * 8:(g + 1) * 8].rearrange("h k q -> k h q"), reads=[self.bBT], writes=[bbt], key="biasT")
            xin = Ring([self.sb("xin%d" % i, [128, DM], BF16) for i in range(3)])
            for (t, b) in xin.slots:
                S.op("pool", lambda e, t=t: e.memset(t[:], 0.0), writes=[b])
            xT = Ring([self.sb("xT%d" % i, [128, 8, 640], BF16) for i in range(2)])
            QT = Ring([self.sb("QT%d" % i, [128, 8, 512], BF16) for i in range(2)])
            KT = Ring([self.sb("KT%d" % i, [128, 8, 640], BF16) for i in range(2)])
            VA = Ring([self.sb("VA%d" % i, [128, 5, 8, 129], BF16) for i in range(2)])
            for (t, b) in VA.slots:
                S.op("pool", lambda e, t=t: e.memset(t[:], 1.0), writes=[b])
            PT = Ring([self.sb("PT%d" % i, [128, 256], BF16) for i in range(4)])
            SC = Ring([self.sb("SC%d" % i, [128, 256], F32) for i in range(4)])
            OS = Ring([self.sb("OS%d" % i, [128, 4, 1032], F32) for i in range(2)])
            RC = Ring([self.sb("RC%d" % i, [128, 4], F32) for i in range(4)])
            tp = Ring([self.ps("tp%d" % i, [128, 8, 128], BF16) for i in range(2)])
            pj = Ring([self.ps("pj%d" % i, [128, 512]) for i in range(2)])
            pO = Ring([self.ps("pO%d" % i, [128, 2, 512]) for i in range(2)])
            DN = Ring([self.sb("DN%d" % i, [128, 4], F32) for i in range(4)])
            ident, bident, ones, bones = self.ident, self.bident, self.ones, self.bones
            for si, Sq in enumerate(self.seqs):
                L = Sq // d
                nq = min(512, L)
                nsub = nq // 128
                nch = nsub + 1
                Wd = nq + 128
                xsub = self.x[si].rearrange("(l d) f -> d l f", d=d)
                ogsub = self.OG[g][si].rearrange("(l d) f -> d l f", d=d)
                prev = {}

                def do_tile(r, T, L=L, nq=nq, nsub=nsub, nch=nch, Wd=Wd, xsub=xsub, ogsub=ogsub, prev=prev):
                    if True:
                        l0 = T * nq
                        xT_t, xT_b = xT.next()
                        carry = T > 0
                        if carry:
                            pxT_t, pxT_b, pKT_t, pKT_b, pVA_t, pVA_b = prev["t"]
                            S.op("pool", lambda e: e.tensor_copy(out=xT_t[:, :, 0:128], in_=pxT_t[:, :, nq:nq + 128]), reads=[pxT_b], writes=[xT_b])
                        vparts = []
                        for c in range(nch):
                            lo = l0 - 64 + 128 * c
                            vlo, vhi = max(lo, 0), min(lo + 128, L)
                            p0, p1 = vlo - lo, vhi - lo
                            vparts.append((p0, p1))
                            if carry and c == 0:
                                continue
                            xi_t, xi_b = xin.next()
                            self.dma(xi_t[p0:p1, :], xsub[r, vlo:vhi, :], writes=[xi_b], key=xi_b.name, eng="pool")
                            tp_t, tp_b = tp.next()

                            def f_tr(e, tp_t=tp_t, xi_t=xi_t):
                                for k in range(8):
                                    ins = e.transpose(tp_t[:, k, :], xi_t[:, k * 128:(k + 1) * 128], ident[:])
                                return ins
                            S.op("pe", f_tr, reads=[xi_b, bident], writes=[tp_b])
                            kw = dict(writes=[xT_b]) if (c == 0 and not carry) else dict(appends=[xT_b])
                            S.op("act" if c % 2 else "dve",
                                 (lambda e, tp_t=tp_t, c=c: e.copy(out=xT_t[:, :, c * 128:(c + 1) * 128], in_=tp_t[:])) if c % 2 else
                                 (lambda e, tp_t=tp_t, c=c: e.tensor_copy(out=xT_t[:, :, c * 128:(c + 1) * 128], in_=tp_t[:])),
                                 reads=[tp_b], **kw)
                        QT_t, QT_b = QT.next()
                        KT_t, KT_b = KT.next()
                        VA_t, VA_b = VA.next()
                        ev = 0
                        for h in range(8):
                            pj_t, pj_b = pj.next()

                            def f_q(e, pj_t=pj_t, h=h):
                                for k in range(8):
                                    ins = e.matmul(pj_t[:, 0:nq], Wg[:, k, h * 128:(h + 1) * 128], xT_t[:, k, 64:64 + nq], start=(k == 0), stop=(k == 7))
                                return ins
                            S.op("pe", f_q, reads=[bWg, xT_b], writes=[pj_b])
                            kw = dict(writes=[QT_b]) if h == 0 else dict(appends=[QT_b])
                            S.op("act", lambda e, pj_t=pj_t, h=h: e.activation(out=QT_t[:, h, 0:nq], in_=pj_t[:, 0:nq], func=AF.Copy, scale=scale),
                                 reads=[pj_b], **kw)
                        if carry:
                            S.op("pool", lambda e: e.tensor_copy(out=KT_t[:, :, 0:128], in_=pKT_t[:, :, nq:nq + 128]), reads=[pKT_b], writes=[KT_b])
                            S.op("pool", lambda e: e.tensor_copy(out=VA_t[:, 0, :, :], in_=pVA_t[:, nch - 1, :, :]), reads=[pVA_b], writes=[VA_b])
                            kspans = [(128, nq)]
                        else:
                            kspans = [(0, Wd // 2), (Wd // 2, Wd // 2)]
                        for h in range(8):
                            for (k0, hw) in kspans:
                                pj_t, pj_b = pj.next()

                                def f_k(e, pj_t=pj_t, h=h, k0=k0, hw=hw):
                                    for k in range(8):
                                        ins = e.matmul(pj_t[:, 0:hw], Wg[:, k, 1024 + h * 128:1024 + (h + 1) * 128],
                                                       xT_t[:, k, k0:k0 + hw], start=(k == 0), stop=(k == 7))
                                    return ins
                                S.op("pe", f_k, reads=[bWg, xT_b], writes=[pj_b])
                                kw = dict(writes=[KT_b]) if (h == 0 and k0 == 0) else dict(appends=[KT_b])
                                S.op("dve", lambda e, pj_t=pj_t, h=h, k0=k0, hw=hw: e.tensor_copy(out=KT_t[:, h, k0:k0 + hw], in_=pj_t[:, 0:hw]),
                                     reads=[pj_b], **kw)
                        prev["t"] = (xT_t, xT_b, KT_t, KT_b, VA_t, VA_b)
                        for c in range(1 if carry else 0, nch):
                            for j in range(2):
                                pj_t, pj_b = pj.next()

                                def f_v(e, pj_t=pj_t, c=c, j=j):
                                    for k in range(8):
                                        ins = e.matmul(pj_t[:, :], xT_t[:, k, c * 128:(c + 1) * 128], Wg[:, k, 2048 + j * 512:2048 + (j + 1) * 512],
                                                       start=(k == 0), stop=(k == 7))
                                    return ins
                                S.op("pe", f_v, reads=[bWg, xT_b], writes=[pj_b])
                                kw = dict(writes=[VA_b]) if (c == 0 and j == 0) else dict(appends=[VA_b])
                                if (c * 2 + j) % 2:
                                    S.op("act", lambda e, pj_t=pj_t, c=c, j=j: e.copy(out=VA_t[:, c, 4 * j:4 * j + 4, 0:128], in_=pj_t[:, :].rearrange("p (h f) -> p h f", h=4)),
                                         reads=[pj_b], **kw)
                                else:
                                    S.op("dve", lambda e, pj_t=pj_t, c=c, j=j: e.tensor_copy(out=VA_t[:, c, 4 * j:4 * j + 4, 0:128], in_=pj_t[:, :].rearrange("p (h f) -> p h f", h=4)),
                                         reads=[pj_b], **kw)
                        OS_t, OS_b = OS.next()
                        items = [(h, c) for h in range(8) for c in range(nch)]
                        LAG = 2
                        pend = {}
                        pOcur = {}
                        first_os = [True]

                        def emit_qk(h, c):
                            pT_t, pT_b = PT.next()
                            sc_t, sc_b = SC.next()
                            ps_t, ps_b = pj.next()
                            if c == 0:
                                q0, q1, b0, b1 = 0, 128, 128, 256
                            elif c == nch - 1:
                                q0, q1, b0, b1 = (c - 1) * 128, c * 128, 0, 128
                            else:
                                q0, q1, b0, b1 = (c - 1) * 128, (c + 1) * 128, 0, 256
                            S.op("pe", lambda e: e.matmul(ps_t[:, b0:b1], KT_t[:, h, c * 128:(c + 1) * 128], QT_t[:, h, q0:q1], start=True, stop=True),
                                 reads=[KT_b, QT_b], writes=[ps_b])
                            S.op("dve", lambda e: e.tensor_tensor(out=sc_t[:, b0:b1], in0=ps_t[:, b0:b1], in1=bt[:, h, b0:b1], op=ALU.add),
                                 reads=[ps_b, bbt], writes=[sc_b])
                            S.op("act", lambda e: e.activation(out=pT_t[:, b0:b1], in_=sc_t[:, b0:b1], func=AF.Exp), reads=[sc_b], writes=[pT_b])
                            p0, p1 = vparts[c]
                            if (p0, p1) != (0, 128):
                                i0, i1 = (0, p0) if p0 > 0 else (p1, 128)
                                S.op("pool", lambda e: e.memset(pT_t[i0:i1, b0:b1], 0.0), reads=[pT_b], writes=[pT_b], cost=0.2)
                            pend[(h, c)] = (pT_t, pT_b)

                        def emit_pv(h, c):
                            pT_t, pT_b = pend.pop((h, c))
                            if c == 0:
                                pOcur[h] = pO.next()
                            pO_t, pO_b = pOcur[h]
                            reg = lambda i: pO_t[:, i // 2, (i % 2) * 129:(i % 2) * 129 + 129]
                            p0, p1 = 0, 128

                            def f_pv(e):
                                ins = None
                                if c >= 1:
                                    i = c - 1
                                    ins = e.matmul(reg(i), pT_t[p0:p1, 0:128], VA_t[p0:p1, c, h, :], start=False, stop=True)
                                if c <= nch - 2:
                                    i = c
                                    ins = e.matmul(reg(i), pT_t[p0:p1, 128:256], VA_t[p0:p1, c, h, :], start=True, stop=False)
                                return ins
                            kw = dict(writes=[pO_b]) if c == 0 else dict(appends=[pO_b])
                            S.op("pe", f_pv, reads=[pT_b, VA_b, bones], **kw)
                            if c == nch - 1:
                                rc_t, rc_b = RC.next()
                                dn_t, dn_b = DN.next()
                                nbk, nj = (nsub + 1) // 2, min(nsub, 2)
                                pv4 = pO_t[:, 0:nbk, 0:258].rearrange("p b (j f) -> p b j f", f=129)[:, :, 0:nj, :]
                                v3 = lambda ap: ap.rearrange("p (b j) -> p b j", j=nj)
                                S.op("dve", lambda e: e.tensor_copy(out=v3(dn_t[:, 0:nsub]), in_=pv4[:, :, :, 128]),
                                     reads=[pO_b], writes=[dn_b])
                                S.op("dve", lambda e: e.reciprocal(out=rc_t[:, 0:nsub], in_=dn_t[:, 0:nsub]), reads=[dn_b], writes=[rc_b])
                                kw2 = dict(writes=[OS_b]) if first_os[0] else dict(appends=[OS_b])
                                first_os[0] = False
                                S.op("act", lambda e: e.activation(out=OS_t[:, 0:nsub, 1024 + h], in_=dn_t[:, 0:nsub], func=AF.Ln),
                                     reads=[dn_b], **kw2)
                                S.op("dve", lambda e: e.tensor_tensor(out=OS_t[:, 0:nsub, h * 128:(h + 1) * 128].rearrange("p (b j) f -> p b j f", j=nj),
                                                                      in0=pv4[:, :, :, 0:128],
                                                                      in1=bcast_last(v3(rc_t[:, 0:nsub]), 128), op=ALU.mult),
                                     reads=[pO_b, rc_b], appends=[OS_b])
                        for j in range(len(items) + LAG):
                            if j < len(items):
                                emit_qk(*items[j])
                            if j >= LAG:
                                emit_pv(*items[j - LAG])
                        for i in range(nsub):
                            self.dma(ogsub[r, l0 + 128 * i:l0 + 128 * (i + 1), :], OS_t[:, i, :], reads=[OS_b], appends=[self.bOG], key=OS_b.name)
                for r in range(d):
                    for T in range(L // nq):
                        do_tile(r, T)
            S.fence(self.bOG)
        S.barrier()

    def emit_ln(self, z_t, z_b, lnc, o_t, o_b, tmp):
        S = self.S
        g_t, g_b, b_t, b_b = lnc
        st_t, st_b, mv_t, mv_b, rs_t, rs_b, xn_t, xn_b = tmp

        def f_st(e):
            e.bn_stats(st_t[:, 0, :], z_t[:, 0:512])
            return e.bn_stats(st_t[:, 1, :], z_t[:, 512:1024])
        S.op("dve", f_st, reads=[z_b], writes=[st_b])
        S.op("dve", lambda e: e.bn_aggr(mv_t[:], st_t[:]), reads=[st_b], writes=[mv_b])
        S.op("dve", lambda e: e.tensor_scalar_add(out=rs_t[:], in0=mv_t[:, 1:2], scalar1=LN_EPS), reads=[mv_b], writes=[rs_b])
        S.op("act", lambda e: e.sqrt(out=rs_t[:], in_=rs_t[:]), reads=[rs_b], writes=[rs_b])
        S.op("dve", lambda e: e.reciprocal(out=rs_t[:], in_=rs_t[:]), reads=[rs_b], writes=[rs_b])
        S.op("dve", lambda e: e.tensor_scalar(out=xn_t[:], in0=z_t[:], scalar1=mv_t[:, 0:1], scalar2=rs_t[:, 0:1], op0=ALU.subtract, op1=ALU.mult),
             reads=[z_b, mv_b, rs_b], writes=[xn_b])
        S.op("pool", lambda e: e.tensor_tensor(out=xn_t[:], in0=xn_t[:], in1=g_t[:], op=ALU.mult), reads=[xn_b, g_b], writes=[xn_b])
        S.op("pool", lambda e: e.tensor_tensor(out=o_t[:], in0=xn_t[:], in1=b_t[:], op=ALU.add), reads=[xn_b, b_b], writes=[o_b])

    def ln_tmp(self):
        a = self.sb("st", [128, 2, 6], F32) + self.sb("mv", [128, 2], F32) + self.sb("rs", [128, 1], F32) + self.sb("xn", [128, DM], F32)
        return a

    def proj_ln_tile(self, W_t, W_b, nk, lhs_fn, lhs_bufs, res_t, res_b, lnc, tmps, pz, zr, o_t, o_b):
        S = self.S
        pz_t, pz_b = pz.next()

        def f_mm(e):
            for j in range(2):
                for k in range(nk):
                    ins = e.matmul(pz_t[:, j * 512:(j + 1) * 512], lhs_fn(k), W_t[:, k, j * 512:(j + 1) * 512], start=(k == 0), stop=(k == nk - 1))
            return ins
        S.op("pe", f_mm, reads=[W_b] + list(lhs_bufs), writes=[pz_b])
        z_t, z_b = zr.next()
        S.op("dve", lambda e: e.scalar_tensor_tensor(out=z_t[:], in0=res_t[:], scalar=ALPHA, in1=pz_t[:], op0=ALU.mult, op1=ALU.add),
             reads=[res_b, pz_b], writes=[z_b])
        self.emit_ln(z_t, z_b, lnc, o_t, o_b, tmps.next())

    def emit_xT(self, o_t, o_b, ob_ring, tp, xTg_t, xTg_b, col, first):
        S = self.S
        ob_t, ob_b = ob_ring.next()
        S.op("act", lambda e: e.copy(out=ob_t[:], in_=o_t[:]), reads=[o_b], writes=[ob_b])
        tp_t, tp_b = tp.next()
        ident = self.ident

        def f_tr(e):
            for k in range(8):
                ins = e.transpose(tp_t[:, k, :], ob_t[:, k * 128:(k + 1) * 128], ident[:])
            return ins
        S.op("pe", f_tr, reads=[ob_b, self.bident], writes=[tp_b])
        kw = dict(writes=[xTg_b]) if first else dict(appends=[xTg_b])
        S.op("act", lambda e: e.copy(out=xTg_t[:, :, col * 128:(col + 1) * 128], in_=tp_t[:]), reads=[tp_b], **kw)

    def phase_B(self):
        S = self.S
        with ExitStack() as es:
            self.es = es
            Wo, bWo = self.sb("Wo", [128, 8, DM], BF16)
            self.load_w(Wo, bWo, "Wo", [(Wo[:, k, :], self.w_o_a[0, k * 128:(k + 1) * 128, :]) for k in range(8)])
            lnc = self.ln_consts(0, 0)
            og = Ring([self.sb("og%d" % i, [128, 3, 1032], F32) for i in range(2)])
            xr = Ring([self.sb("xr%d" % i, [128, DM], F32) for i in range(2)])
            sm = Ring([self.sb("m%d" % i, [128, 8], F32) + self.sb("e%d" % i, [128, 3, 8], F32) + self.sb("ss%d" % i, [128, 8], F32) for i in range(2)])
            t1 = Ring([self.sb("t1_%d" % i, [128, DM], F32) for i in range(2)])
            t2 = Ring([self.sb("t2_%d" % i, [128, DM], F32) for i in range(2)])
            mg = Ring([self.sb("mg%d" % i, [128, DM], BF16) for i in range(2)])
            mT = Ring([self.sb("mT%d" % i, [128, 8, 128], BF16) for i in range(2)])
            zr = Ring([self.sb("z%d" % i, [128, DM], F32) for i in range(2)])
            o1 = Ring([self.sb("o1_%d" % i, [128, DM], F32) for i in range(2)])
            ob = Ring([self.sb("ob%d" % i, [128, DM], BF16) for i in range(2)])
            xTg = Ring([self.sb("xTg%d" % i, [128, 8, 512], BF16) for i in range(2)])
            tmps = Ring([self.ln_tmp() for i in range(2)])
            tp = Ring([self.ps("tp%d" % i, [128, 8, 128], BF16) for i in range(2)])
            pz = Ring([self.ps("pz%d" % i, [128, DM]) for i in range(2)])
            tiles = [(si, t) for si, Sq in enumerate(self.seqs) for t in range(Sq // 128)]

            def load(si, t):
                og_t, og_b = og.next()
                for g in range(3):
                    kw = dict(writes=[og_b]) if g == 0 else dict(appends=[og_b])
                    self.dma(og_t[:, g, :], self.OG[g][si][t * 128:(t + 1) * 128, :], reads=[self.bOG], key=og_b.name, **kw)
                xr_t, xr_b = xr.next()
                self.dma(xr_t[:], self.x[si][t * 128:(t + 1) * 128, :], writes=[xr_b], key=xr_b.name)
                return og_t, og_b, xr_t, xr_b
            st = {}

            def do_tile(si, t, og_t, og_b, xr_t, xr_b):
                m_t, m_b, e_t, e_b, ss_t, ss_b = sm.next()
                lse = og_t[:, :, 1024:1032]

                def f_m(e):
                    e.tensor_tensor(out=m_t[:], in0=og_t[:, 0, 1024:1032], in1=og_t[:, 1, 1024:1032], op=ALU.max)
                    return e.tensor_tensor(out=m_t[:], in0=m_t[:], in1=og_t[:, 2, 1024:1032], op=ALU.max)
                S.op("dve", f_m, reads=[og_b], writes=[m_b])
                S.op("dve", lambda e: e.tensor_tensor(out=e_t[:], in0=lse, in1=bcast_mid(m_t[:], 3), op=ALU.subtract), reads=[og_b, m_b], writes=[e_b])
                S.op("act", lambda e: e.activation(out=e_t[:], in_=e_t[:], func=AF.Exp), reads=[e_b], writes=[e_b])

                def f_s(e):
                    e.tensor_tensor(out=ss_t[:], in0=e_t[:, 0, :], in1=e_t[:, 1, :], op=ALU.add)
                    return e.tensor_tensor(out=ss_t[:], in0=ss_t[:], in1=e_t[:, 2, :], op=ALU.add)
                S.op("pool", f_s, reads=[e_b], writes=[ss_b])
                S.op("dve", lambda e: e.reciprocal(out=ss_t[:], in_=ss_t[:]), reads=[ss_b], writes=[ss_b])
                S.op("dve", lambda e: e.tensor_tensor(out=e_t[:], in0=e_t[:], in1=bcast_mid(ss_t[:], 3), op=ALU.mult), reads=[e_b, ss_b], writes=[e_b])
                t1_t, t1_b = t1.next()
                t2_t, t2_b = t2.next()
                mg_t, mg_b = mg.next()
                v3 = lambda ap: ap.rearrange("p (h f) -> p h f", h=8)
                def f_m0(e):
                    for h in range(8):
                        ins = e.activation(out=t1_t[:, h * 128:(h + 1) * 128], in_=og_t[:, 0, h * 128:(h + 1) * 128], func=AF.Copy, scale=e_t[:, 0, h:h + 1])
                    return ins
                S.op("act", f_m0, reads=[og_b, e_b], writes=[t1_b])

                def f_m1(e):
                    for h in range(8):
                        ins = e.scalar_tensor_tensor(out=t1_t[:, h * 128:(h + 1) * 128], in0=og_t[:, 1, h * 128:(h + 1) * 128], scalar=e_t[:, 1, h:h + 1],
                                                     in1=t1_t[:, h * 128:(h + 1) * 128], op0=ALU.mult, op1=ALU.add)
                    return ins
                S.op("dve", f_m1, reads=[og_b, e_b, t1_b], writes=[t1_b])

                def f_m2(e):
                    for h in range(8):
                        ins = e.scalar_tensor_tensor(out=mg_t[:, h * 128:(h + 1) * 128], in0=og_t[:, 2, h * 128:(h + 1) * 128], scalar=e_t[:, 2, h:h + 1],
                                                     in1=t1_t[:, h * 128:(h + 1) * 128], op0=ALU.mult, op1=ALU.add)
                    return ins
                S.op("dve", f_m2, reads=[og_b, e_b, t1_b], writes=[mg_b])
                tp_t, tp_b = tp.next()
                mT_t, mT_b = mT.next()

                def f_tr(e):
                    for k in range(8):
                        ins = e.transpose(tp_t[:, k, :], mg_t[:, k * 128:(k + 1) * 128], self.ident[:])
                    return ins
                S.op("pe", f_tr, reads=[mg_b, self.bident], writes=[tp_b])
                S.op("act", lambda e: e.copy(out=mT_t[:], in_=tp_t[:]), reads=[tp_b], writes=[mT_b])
                o_t, o_b = o1.next()
                self.proj_ln_tile(Wo, bWo, 8, lambda k: mT_t[:, k, :], [mT_b], xr_t, xr_b, lnc, tmps, pz, zr, o_t, o_b)
                self.dma(self.X1[si][t * 128:(t + 1) * 128, :], o_t[:], reads=[o_b], appends=[self.bX1], key=o_b.name)
                if t % 4 == 0:
                    st["x"] = xTg.next()
                xTg_t, xTg_b = st["x"]
                self.emit_xT(o_t, o_b, ob, tp, xTg_t, xTg_b, t % 4, t % 4 == 0)
                if t % 4 == 3:
                    self.dma(self.X1T[si].rearrange("k p s -> p k s")[:, :, (t - 3) * 128:(t + 1) * 128], xTg_t[:], reads=[xTg_b], appends=[self.bX1T], key=xTg_b.name)
            nxt = load(*tiles[0])
            for ti, (si, t) in enumerate(tiles):
                cur = nxt
                if ti + 1 < len(tiles):
                    nxt = load(*tiles[ti + 1])
                do_tile(si, t, *cur)
            S.fence(self.bX1)
            S.fence(self.bX1T)
        S.barrier()

    def phase_C1(self, li, XT, bXT):
        S = self.S
        with ExitStack() as es:
            self.es = es
            Wi, bWi = self.sb("Wi", [128, 8, 2 * DFF], BF16)
            self.load_w(Wi, bWi, "Wi", [(Wi[:, k, :], self.ffn_w_in[li, k * 128:(k + 1) * 128, :]) for k in range(8)])
            cp, bcp = self.sb("cp", [128, 44, 4], F32)
            for t in range(3):
                src = bass.AP(tensor=self.ffn_conv_w.tensor, offset=(li * 3 + t) * 2 * DFF, ap=[[1, 128], [128, 44], [1, 1]])
                kw = dict(writes=[bcp]) if t == 0 else dict(appends=[bcp])
                self.dma(cp[:, :, t:t + 1], src, key="cp", slow=True, **kw)
            src = bass.AP(tensor=self.ffn_conv_b.tensor, offset=li * 2 * DFF, ap=[[1, 128], [128, 44], [1, 1]])
            self.dma(cp[:, :, 3:4], src, appends=[bcp], key="cp", slow=True)
            xt = Ring([self.sb("xt%d" % i, [128, 8, 512], BF16) for i in range(2)])
            ca = Ring([self.sb("ca%d" % i, [128, 512], F32) for i in range(3)])
            cg = Ring([self.sb("cg%d" % i, [128, 512], F32) for i in range(3)])
            gg = Ring([self.sb("gg%d" % i, [128, 512], F32) for i in range(3)])
            uu = Ring([self.sb("uu%d" % i, [128, 512], BF16) for i in range(4)])
            pa = Ring([self.ps("pa%d" % i, [128, 512]) for i in range(3)])
            pg = Ring([self.ps("pg%d" % i, [128, 512]) for i in range(3)])
            tiles = []
            for si, Sq in enumerate(self.seqs):
                a = 0
                while a < Sq:
                    b = min(a + 510, Sq)
                    tiles.append((si, a, b))
                    a = b

            def load(si, a, b):
                Sq = self.seqs[si]
                xt_t, xt_b = xt.next()
                lo, hi = max(a - 1, 0), min(b + 1, Sq)
                c0 = lo - (a - 1)
                w = b - a + 2
                first = True
                if a == 0:
                    S.op("pool", lambda e: e.memset(xt_t[:, :, 0:2], 0.0), writes=[xt_b])
                    first = False
                if b == Sq:
                    kw = dict(writes=[xt_b]) if first else dict(appends=[xt_b])
                    e0 = (w - 2) if (w % 2 == 0) else (w - 1)
                    S.op("pool", lambda e: e.memset(xt_t[:, :, e0:e0 + 2], 0.0), **kw)
                    first = False
                self.dma(xt_t[:, :, c0:c0 + hi - lo], XT[si].rearrange("k p s -> p k s")[:, :, lo:hi], reads=[bXT], writes=[xt_b], key=xt_b.name)
                return xt_t, xt_b
            def do_tile(si, a, b, xt_t, xt_b):
                nb = b - a
                w = nb + 2
                for j in range(22):
                    pa_t, pa_b = pa.next()
                    pg_t, pg_b = pg.next()

                    def f_h(e, pa_t=pa_t, pg_t=pg_t, j=j):
                        for k in range(8):
                            e.matmul(pa_t[:, 0:w], Wi[:, k, j * 128:(j + 1) * 128], xt_t[:, k, 0:w], start=(k == 0), stop=(k == 7))
                        for k in range(8):
                            ins = e.matmul(pg_t[:, 0:w], Wi[:, k, DFF + j * 128:DFF + (j + 1) * 128], xt_t[:, k, 0:w], start=(k == 0), stop=(k == 7))
                        return ins
                    S.op("pe", f_h, reads=[bWi, xt_b], writes=[pa_b, pg_b])
                    ca_t, ca_b = ca.next()
                    cg_t, cg_b = cg.next()
                    gg_t, gg_b = gg.next()
                    uu_t, uu_b = uu.next()
                    for (p_t, p_b, c_t, c_b, jj) in ((pa_t, pa_b, ca_t, ca_b, j), (pg_t, pg_b, cg_t, cg_b, 22 + j)):
                        S.op("act", lambda e, p_t=p_t, c_t=c_t, jj=jj: e.activation(out=c_t[:, 0:nb], in_=p_t[:, 0:nb], func=AF.Identity,
                                                                                   scale=cp[:, jj, 0:1], bias=cp[:, jj, 3:4]),
                             reads=[p_b, bcp], writes=[c_b])
                        S.op("dve", lambda e, p_t=p_t, c_t=c_t, jj=jj: e.scalar_tensor_tensor(out=c_t[:, 0:nb], in0=p_t[:, 1:nb + 1], scalar=cp[:, jj, 1:2],
                                                                                             in1=c_t[:, 0:nb], op0=ALU.mult, op1=ALU.add),
                             reads=[p_b, bcp, c_b], writes=[c_b])
                        S.op("dve", lambda e, p_t=p_t, c_t=c_t, jj=jj: e.scalar_tensor_tensor(out=c_t[:, 0:nb], in0=p_t[:, 2:nb + 2], scalar=cp[:, jj, 2:3],
                                                                                             in1=c_t[:, 0:nb], op0=ALU.mult, op1=ALU.add),
                             reads=[p_b, bcp, c_b], writes=[c_b])
                    S.op("act", lambda e, cg_t=cg_t, gg_t=gg_t: e.activation(out=gg_t[:, 0:nb], in_=cg_t[:, 0:nb], func=AF.Gelu), reads=[cg_b], writes=[gg_b])
                    S.op("pool", lambda e, ca_t=ca_t, gg_t=gg_t, uu_t=uu_t: e.tensor_tensor(out=uu_t[:, 0:nb], in0=ca_t[:, 0:nb], in1=gg_t[:, 0:nb], op=ALU.mult),
                         reads=[ca_b, gg_b], writes=[uu_b])
                    self.dma(self.UT[si][j, :, a:b], uu_t[:, 0:nb], reads=[uu_b], appends=[self.bUT], key=uu_b.name)
            nxt = load(*tiles[0])
            for ti, (si, a, b) in enumerate(tiles):
                cur = nxt
                if ti + 1 < len(tiles):
                    nxt = load(*tiles[ti + 1])
                do_tile(si, a, b, *cur)
            S.fence(self.bUT)
        S.barrier()

    def phase_C2(self, li, RES, bRES, OUT, bOUT, OUTT, bOUTT):
        S = self.S
        with ExitStack() as es:
            self.es = es
            Wo, bWo = self.sb("Wout", [128, 22, DM], BF16)
            self.load_w(Wo, bWo, "Wout", [(Wo[:, k, :], self.ffn_w_out[li, k * 128:(k + 1) * 128, :]) for k in range(22)])
            lnc = self.ln_consts(li, 1)
            ut = Ring([self.sb("ut%d" % i, [128, 22, 512], BF16) for i in range(2)])
            xr = Ring([self.sb("xr%d" % i, [128, DM], F32) for i in range(3)])
            zr = Ring([self.sb("z%d" % i, [128, DM], F32) for i in range(2)])
            o1 = Ring([self.sb("o1_%d" % i, [128, DM], F32) for i in range(3)])
            ob = Ring([self.sb("ob%d" % i, [128, DM], BF16) for i in range(2)])
            xTg = Ring([self.sb("xTg%d" % i, [128, 8, 512], BF16) for i in range(2)])
            tmps = Ring([self.ln_tmp() for i in range(2)])
            tp = Ring([self.ps("tp%d" % i, [128, 8, 128], BF16) for i in range(2)])
            pz = Ring([self.ps("pz%d" % i, [128, DM]) for i in range(2)])
            tiles = [(si, t) for si, Sq in enumerate(self.seqs) for t in range(Sq // 512)]

            def load(si, t):
                ut_t, ut_b = ut.next()
                self.dma(ut_t[:], self.UT[si].rearrange("j p s -> p j s")[:, :, t * 512:(t + 1) * 512], reads=[self.bUT], writes=[ut_b], key=ut_b.name)
                return ut_t, ut_b
            def do_tile(si, t, ut_t, ut_b):
                if OUTT is not None:
                    xTg_t, xTg_b = xTg.next()
                for s4 in range(4):
                    r0 = t * 512 + s4 * 128
                    xr_t, xr_b = xr.next()
                    self.dma(xr_t[:], RES[si][r0:r0 + 128, :], reads=[bRES], writes=[xr_b], key=xr_b.name)
                    o_t, o_b = o1.next()
                    self.proj_ln_tile(Wo, bWo, 22, lambda k, s4=s4: ut_t[:, k, s4 * 128:(s4 + 1) * 128], [ut_b], xr_t, xr_b, lnc, tmps, pz, zr, o_t, o_b)
                    self.dma(OUT[si][r0:r0 + 128, :], o_t[:], reads=[o_b], appends=[bOUT], key=o_b.name)
                    if OUTT is not None:
                        self.emit_xT(o_t, o_b, ob, tp, xTg_t, xTg_b, s4, s4 == 0)
                if OUTT is not None:
                    self.dma(OUTT[si].rearrange("k p s -> p k s")[:, :, t * 512:(t + 1) * 512], xTg_t[:], reads=[xTg_b], appends=[bOUTT], key=xTg_b.name)
            nxt = load(*tiles[0])
            for ti, (si, t) in enumerate(tiles):
                cur = nxt
                if ti + 1 < len(tiles):
                    nxt = load(*tiles[ti + 1])
                do_tile(si, t, *cur)
            S.fence(bOUT)
            if OUTT is not None:
                S.fence(bOUTT)
        S.barrier()

    def phase_D(self):
        S = self.S
        with ExitStack() as es:
            self.es = es
            Wd, bWd = self.sb("Wd", [128, 8, 704], BF16)
            pieces = [(Wd[:, k, 0:672], self.w_dkv[0, k * 128:(k + 1) * 128, :]) for k in range(8)]
            self.load_w(Wd, bWd, "Wd", pieces)
            S.op("dve", lambda e: e.tensor_scalar(out=Wd[:, :, 672:688], in0=Wd[:, :, 656:672], scalar1=-1.0, scalar2=None, op0=ALU.mult), reads=[bWd], writes=[bWd])
            S.op("dve", lambda e: e.tensor_copy(out=Wd[:, :, 688:704], in_=Wd[:, :, 640:656]), reads=[bWd], writes=[bWd])
            wuq = self.w_uq[0].rearrange("(k p) (h c) -> k p h c", p=128, c=96)
            Wqn, bWqn = self.sb("Wqn", [128, 3, 16, 64], BF16)
            self.load_w(Wqn, bWqn, "Wqn", [(Wqn[:, k, :, :], wuq[k, :, :, 0:64]) for k in range(3)])
            Wqp, bWqp = self.sb("Wqp", [128, 3, 16, 32], BF16)
            self.load_w(Wqp, bWqp, "Wqp", [(Wqp[:, k, :, :], wuq[k, :, :, 64:96]) for k in range(3)])
            Wqr, bWqr = self.sb("Wqr", [128, 3, 16, 32], BF16)
            S.op("dve", lambda e: e.tensor_scalar(out=Wqr[:, :, :, 0:16], in0=Wqp[:, :, :, 16:32], scalar1=-1.0, scalar2=None, op0=ALU.mult), reads=[bWqp], writes=[bWqr])
            S.op("dve", lambda e: e.tensor_copy(out=Wqr[:, :, :, 16:32], in_=Wqp[:, :, :, 0:16]), reads=[bWqp, bWqr], writes=[bWqr])
            Wk, bWk = self.sb("Wk", [128, 2, 16, 64], BF16)
            wukv = self.w_ukv[0].rearrange("(k p) (h c) -> k p h c", p=128, c=128)
            self.load_w(Wk, bWk, "Wk", [(Wk[:, k, :, :], wukv[k, :, :, 0:64]) for k in range(2)])
            Wv, bWv = self.sb("Wv", [128, 2, 16, 64], BF16)
            self.load_w(Wv, bWv, "Wv", [(Wv[:, k, :, :], wukv[k, :, :, 64:128]) for k in range(2)])
            gq, bgq = self.sb("gq", [128, 384], F32)
            gkv, bgkv = self.sb("gkv", [128, 256], F32)
            self.dma(gq[:], bass.AP(tensor=self.g_q.tensor, offset=0, ap=[[0, 128], [1, 384]]), writes=[bgq], key="gq")
            self.dma(gkv[:], bass.AP(tensor=self.g_kv.tensor, offset=0, ap=[[0, 128], [1, 256]]), writes=[bgkv], key="gkv")
            xt = Ring([self.sb("xt%d" % i, [128, 8, 512], BF16) for i in range(2)])
            cs = Ring([self.sb("cs%d" % i, [128, 2, 512], F32) for i in range(2)])
            cn = Ring([self.sb("cn%d" % i, [128, 704], BF16) for i in range(2)])
            junk = Ring([self.sb("junk%d" % i, [128, 384], F32) for i in range(2)])
            ssr = Ring([self.sb("ssq%d" % i, [128, 4], F32) for i in range(3)])
            cT = Ring([self.sb("cT%d" % i, [128, 7, 512], BF16) for i in range(2)])
            qn = Ring([self.sb("qn%d" % i, [128, 8, 512], BF16) for i in range(2)])
            qr = Ring([self.sb("qr%d" % i, [128, 4, 512], BF16) for i in range(2)])
            kn = Ring([self.sb("kn%d" % i, [128, 8, 512], BF16) for i in range(2)])
            vo = Ring([self.sb("vo%d" % i, [128, 16, 65], BF16) for i in range(3)])
            for (t_, b_) in vo.slots:
                S.op("pool", lambda e, t_=t_: e.memset(t_[:], 1.0), writes=[b_])
            rt = Ring([self.sb("rt%d" % i, [128, 2, 512], F32) for i in range(3)])
            krf = Ring([self.sb("krf%d" % i, [96, 512], BF16) for i in range(2)])
            pc = Ring([self.ps("pc%d" % i, [128, DM]) for i in range(2)])
            tp = Ring([self.ps("tp%d" % i, [128, 8, 128], BF16) for i in range(1)])
            pq = Ring([self.ps("pq%d" % i, [128, 512]) for i in range(3)])
            tiles = [(si, t) for si, Sq in enumerate(self.seqs) for t in range(Sq // 512)]

            def load(si, t):
                xt_t, xt_b = xt.next()
                self.dma(xt_t[:], self.X2T[si].rearrange("k p s -> p k s")[:, :, t * 512:(t + 1) * 512], reads=[self.bX2T], writes=[xt_b], key=xt_b.name)
                cs_t, cs_b = cs.next()
                for q4 in range(4):
                    kw = dict(writes=[cs_b]) if q4 == 0 else dict(appends=[cs_b])
                    self.dma(cs_t[32 * q4:32 * q4 + 32, :, :], self.c_rope.rearrange("a r s -> r a s")[:, :, t * 512:(t + 1) * 512], key=cs_b.name, **kw)
                return xt_t, xt_b, cs_t, cs_b
            def do_tile(si, t, xt_t, xt_b, cs_t, cs_b):
                cT_t, cT_b = cT.next()
                for s4 in range(4):
                    pc_t, pc_b = pc.next()

                    def f_c(e, pc_t=pc_t, s4=s4):
                        for k in range(8):
                            e.matmul(pc_t[:, 0:512], xt_t[:, k, s4 * 128:(s4 + 1) * 128], Wd[:, k, 0:512], start=(k == 0), stop=(k == 7))
                        for k in range(8):
                            ins = e.matmul(pc_t[:, 512:704], xt_t[:, k, s4 * 128:(s4 + 1) * 128], Wd[:, k, 512:704], start=(k == 0), stop=(k == 7))
                        return ins
                    S.op("pe", f_c, reads=[bWd, xt_b], writes=[pc_b])
                    ss_t, ss_b = ssr.next()
                    jk_t, jk_b = junk.next()

                    def f_sq(e, pc_t=pc_t, ss_t=ss_t, jk_t=jk_t):
                        e.activation(out=jk_t[:, 0:384], in_=pc_t[:, 0:384], func=AF.Square, accum_out=ss_t[:, 0:1])
                        return e.activation(out=jk_t[:, 0:256], in_=pc_t[:, 384:640], func=AF.Square, accum_out=ss_t[:, 1:2])
                    S.op("act", f_sq, reads=[pc_b], writes=[jk_b, ss_b])

                    def f_rs(e, ss_t=ss_t):
                        e.tensor_scalar(out=ss_t[:, 2:3], in0=ss_t[:, 0:1], scalar1=1.0 / 384, scalar2=RMS_EPS, op0=ALU.mult, op1=ALU.add)
                        return e.tensor_scalar(out=ss_t[:, 3:4], in0=ss_t[:, 1:2], scalar1=1.0 / 256, scalar2=RMS_EPS, op0=ALU.mult, op1=ALU.add)
                    S.op("dve", f_rs, reads=[ss_b], writes=[ss_b])
                    S.op("act", lambda e, ss_t=ss_t: e.sqrt(out=ss_t[:, 2:4], in_=ss_t[:, 2:4]), reads=[ss_b], writes=[ss_b])
                    S.op("dve", lambda e, ss_t=ss_t: e.reciprocal(out=ss_t[:, 2:4], in_=ss_t[:, 2:4]), reads=[ss_b], writes=[ss_b])
                    cn_t, cn_b = cn.next()
                    S.op("dve", lambda e, pc_t=pc_t, ss_t=ss_t, cn_t=cn_t: e.scalar_tensor_tensor(out=cn_t[:, 0:384], in0=pc_t[:, 0:384], scalar=ss_t[:, 2:3], in1=gq[:],
                                                                                                   op0=ALU.mult, op1=ALU.mult), reads=[pc_b, ss_b, bgq], writes=[cn_b])
                    S.op("dve", lambda e, pc_t=pc_t, ss_t=ss_t, cn_t=cn_t: e.scalar_tensor_tensor(out=cn_t[:, 384:640], in0=pc_t[:, 384:640], scalar=ss_t[:, 3:4], in1=gkv[:],
                                                                                                   op0=ALU.mult, op1=ALU.mult), reads=[pc_b, ss_b, bgkv], appends=[cn_b])
                    S.op("dve", lambda e, pc_t=pc_t, cn_t=cn_t: e.tensor_copy(out=cn_t[:, 640:704], in_=pc_t[:, 640:704]), reads=[pc_b], appends=[cn_b])
                    tp_t, tp_b = tp.next()

                    def f_tr(e, tp_t=tp_t, cn_t=cn_t):
                        for k in range(5):
                            e.transpose(tp_t[:, k, :], cn_t[:, k * 128:(k + 1) * 128], self.ident[:])
                        e.transpose(tp_t[0:96, 5, :], cn_t[:, 576:672], self.ident[:])
                        return e.transpose(tp_t[0:96, 6, :], cn_t[:, 608:704], self.ident[:])
                    S.op("pe", f_tr, reads=[cn_b, self.bident], writes=[tp_b])
                    kw = dict(writes=[cT_b]) if s4 == 0 else dict(appends=[cT_b])

                    def f_cp(e, tp_t=tp_t, s4=s4):
                        e.copy(out=cT_t[:, 0:5, s4 * 128:(s4 + 1) * 128], in_=tp_t[:, 0:5, :])
                        return e.copy(out=cT_t[64:96, 5:7, s4 * 128:(s4 + 1) * 128], in_=tp_t[64:96, 5:7, :])
                    S.op("act", f_cp, reads=[tp_b], **kw)
                    pv_t, pv_b = pc.next()

                    def f_v(e, pv_t=pv_t, s4=s4):
                        for j in range(2):
                            for k in range(2):
                                ins = e.matmul(pv_t[:, j * 512:(j + 1) * 512], cT_t[:, 3 + k, s4 * 128:(s4 + 1) * 128], Wv[:, k, 8 * j:8 * j + 8, :],
                                               start=(k == 0), stop=(k == 1))
                        return ins
                    S.op("pe", f_v, reads=[cT_b, bWv], writes=[pv_b])
                    vo_t, vo_b = vo.next()
                    S.op("dve", lambda e, pv_t=pv_t, vo_t=vo_t: e.tensor_copy(out=vo_t[:, :, 0:64], in_=pv_t[:, :].rearrange("p (h c) -> p h c", c=64)),
                         reads=[pv_b], writes=[vo_b])
                    self.dma(self.V2[si][:, :, t * 4 + s4, :].rearrange("h p f -> p h f"), vo_t[:], reads=[vo_b], appends=[self.bQKV], key=vo_b.name)
                rt_t, rt_b = rt.next()
                kf_t, kf_b = krf.next()
                S.op("dve", lambda e: e.tensor_tensor(out=rt_t[64:96, 0, :], in0=cT_t[64:96, 5, :], in1=cs_t[64:96, 0, :], op=ALU.mult), reads=[cT_b, cs_b], writes=[rt_b])
                S.op("pool", lambda e: e.tensor_tensor(out=rt_t[64:96, 1, :], in0=cT_t[64:96, 6, :], in1=cs_t[64:96, 1, :], op=ALU.mult), reads=[cT_b, cs_b], appends=[rt_b])
                S.op("pool", lambda e: e.tensor_tensor(out=kf_t[64:96, :], in0=rt_t[64:96, 0, :], in1=rt_t[64:96, 1, :], op=ALU.add), reads=[rt_b], writes=[kf_b])
                ktd = self.KTd[si].rearrange("h p s -> p h s")
                qtd = self.QTd[si].rearrange("h p s -> p h s")
                cols = slice(t * 512, (t + 1) * 512)
                self.dma(ktd[64:96, :, cols], bcast_mid(kf_t[64:96, :], 16), reads=[kf_b], appends=[self.bQKV], key=kf_b.name)
                qn_t, qn_b = qn.next()
                kn_t, kn_b = kn.next()
                qr_t, qr_b = qr.next()
                for hp in range(8):
                    pq_t, pq_b = pq.next()

                    def f_qn(e, pq_t=pq_t, hp=hp):
                        for k in range(3):
                            ins = e.matmul(pq_t[:, :], Wqn[:, k, 2 * hp:2 * hp + 2, :].rearrange("p a b -> p (a b)"), cT_t[:, k, :], start=(k == 0), stop=(k == 2))
                        return ins
                    S.op("pe", f_qn, reads=[bWqn, cT_b], writes=[pq_b])
                    kw = dict(writes=[qn_b]) if hp == 0 else dict(appends=[qn_b])
                    S.op("act", lambda e, pq_t=pq_t, hp=hp: e.copy(out=qn_t[:, hp, :], in_=pq_t[:, :]), reads=[pq_b], **kw)
                    pk_t, pk_b = pq.next()

                    def f_kn(e, pk_t=pk_t, hp=hp):
                        for k in range(2):
                            ins = e.matmul(pk_t[:, :], Wk[:, k, 2 * hp:2 * hp + 2, :].rearrange("p a b -> p (a b)"), cT_t[:, 3 + k, :], start=(k == 0), stop=(k == 1))
                        return ins
                    S.op("pe", f_kn, reads=[bWk, cT_b], writes=[pk_b])
                    kw = dict(writes=[kn_b]) if hp == 0 else dict(appends=[kn_b])
                    if hp % 2:
                        S.op("dve", lambda e, pk_t=pk_t, hp=hp: e.tensor_copy(out=kn_t[:, hp, :], in_=pk_t[:, :]), reads=[pk_b], **kw)
                    else:
                        S.op("act", lambda e, pk_t=pk_t, hp=hp: e.copy(out=kn_t[:, hp, :], in_=pk_t[:, :]), reads=[pk_b], **kw)
                for hq in range(4):
                    pp_t, pp_b = pq.next()
                    pr_t, pr_b = pq.next()

                    def f_qp(e, pp_t=pp_t, pr_t=pr_t, hq=hq):
                        for k in range(3):
                            e.matmul(pp_t[:, :], Wqp[:, k, 4 * hq:4 * hq + 4, :].rearrange("p a b -> p (a b)"), cT_t[:, k, :], start=(k == 0), stop=(k == 2))
                        for k in range(3):
                            ins = e.matmul(pr_t[:, :], Wqr[:, k, 4 * hq:4 * hq + 4, :].rearrange("p a b -> p (a b)"), cT_t[:, k, :], start=(k == 0), stop=(k == 2))
                        return ins
                    S.op("pe", f_qp, reads=[bWqp, bWqr, cT_b], writes=[pp_b, pr_b])
                    rq_t, rq_b = rt.next()
                    S.op("dve", lambda e, pp_t=pp_t, rq_t=rq_t: e.tensor_tensor(out=rq_t[:, 0, :], in0=pp_t[:, :], in1=cs_t[:, 0, :], op=ALU.mult),
                         reads=[pp_b, cs_b], writes=[rq_b])
                    S.op("dve", lambda e, pr_t=pr_t, rq_t=rq_t: e.tensor_tensor(out=rq_t[:, 1, :], in0=pr_t[:, :], in1=cs_t[:, 1, :], op=ALU.mult),
                         reads=[pr_b, cs_b], appends=[rq_b])
                    kw = dict(writes=[qr_b]) if hq == 0 else dict(appends=[qr_b])
                    S.op("pool", lambda e, rq_t=rq_t, hq=hq: e.tensor_tensor(out=qr_t[:, hq, :], in0=rq_t[:, 0, :], in1=rq_t[:, 1, :], op=ALU.add), reads=[rq_b], **kw)
                qtd5 = self.QTd[si].rearrange("(hp two) p s -> two p hp s", two=2)
                ktd5 = self.KTd[si].rearrange("(hp two) p s -> two p hp s", two=2)
                for two in range(2):
                    self.dma(qtd5[two, 0:64, :, cols], qn_t[64 * two:64 * two + 64, :, :], reads=[qn_b], appends=[self.bQKV], key=qn_b.name)
                    self.dma(ktd5[two, 0:64, :, cols], kn_t[64 * two:64 * two + 64, :, :], reads=[kn_b], appends=[self.bQKV], key=kn_b.name)
                qtd4 = self.QTd[si].rearrange("(hq four) p s -> four p hq s", four=4)
                for j in range(4):
                    self.dma(qtd4[j, 64:96, :, cols], qr_t[32 * j:32 * j + 32, :, :], reads=[qr_b], appends=[self.bQKV], key=qr_b.name)
            nxt = load(*tiles[0])
            for ti, (si, t) in enumerate(tiles):
                cur = nxt
                if ti + 1 < len(tiles):
                    nxt = load(*tiles[ti + 1])
                do_tile(si, t, *cur)
            S.fence(self.bQKV)
        S.barrier()

    def phase_E(self):
        S = self.S
        scale = 96 ** -0.5
        with ExitStack() as es:
            self.es = es
            Smax = max(self.seqs)
            kt = Ring([self.sb("kt%d" % i, [96, Smax], BF16) for i in range(2)])
            vt = Ring([self.sb("vt%d" % i, [128, Smax // 128, 65], BF16) for i in range(2)])
            qt = Ring([self.sb("qt%d" % i, [96, 512], BF16) for i in range(3)])
            PT = Ring([self.sb("PT%d" % i, [128, 512], BF16) for i in range(4)])
            oT = Ring([self.sb("oT%d" % i, [65, 512], F32) for i in range(2)])
            rc = Ring([self.sb("rc%d" % i, [128, 4], F32) for i in range(2)])
            om = Ring([self.sb("om%d" % i, [128, 4, 64], BF16) for i in range(3)])
            psc = Ring([self.ps("psc%d" % i, [128, 512]) for i in range(3)])
            pO = Ring([self.ps("pO%d" % i, [128, 512]) for i in range(2)])
            ptr = Ring([self.ps("ptr%d" % i, [128, 4, 128]) for i in range(2)])
            for si, Sq in enumerate(self.seqs):
                nchk = Sq // 128
                heads = list(range(16))

                def loadh(h, si=si, Sq=Sq, nchk=nchk):
                    kt_t, kt_b = kt.next()
                    vt_t, vt_b = vt.next()
                    self.dma(kt_t[:, 0:Sq], self.KTd[si][h], reads=[self.bQKV], writes=[kt_b], key=kt_b.name)
                    self.dma(vt_t[:, 0:nchk, :], self.V2[si][h], reads=[self.bQKV], writes=[vt_b], key=vt_b.name)
                    return kt_t, kt_b, vt_t, vt_b
                nxh = loadh(0)
                for h in heads:
                    kt_t, kt_b, vt_t, vt_b = nxh
                    if h + 1 < 16:
                        nxh = loadh(h + 1)
                    def do_q(qi, si=si, h=h, kt_t=kt_t, kt_b=kt_b, vt_t=vt_t, vt_b=vt_b, nchk=nchk):
                        q_t, q_b = qt.next()
                        self.dma(q_t[:], self.QTd[si][h, :, qi * 512:(qi + 1) * 512], reads=[self.bQKV], writes=[q_b], key=q_b.name)
                        pO_t, pO_b = pO.next()
                        pend = {}
                        LAG = 2

                        def e_qk(c):
                            ps_t, ps_b = psc.next()
                            pT_t, pT_b = PT.next()
                            S.op("pe", lambda e: e.matmul(ps_t[:, :], kt_t[:, c * 128:(c + 1) * 128], q_t[:, :], start=True, stop=True), reads=[kt_b, q_b], writes=[ps_b])
                            S.op("act", lambda e: e.activation(out=pT_t[:], in_=ps_t[:], func=AF.Exp, scale=scale), reads=[ps_b], writes=[pT_b])
                            pend[c] = (pT_t, pT_b)

                        def e_pv(c):
                            pT_t, pT_b = pend.pop(c)
                            kw = dict(writes=[pO_b]) if c == 0 else dict(appends=[pO_b])
                            S.op("pe", lambda e: e.matmul(pO_t[0:65, :], vt_t[:, c, :], pT_t[:], start=(c == 0), stop=(c == nchk - 1)), reads=[vt_b, pT_b], **kw)
                        for j in range(nchk + LAG):
                            if j < nchk:
                                e_qk(j)
                            if j >= LAG:
                                e_pv(j - LAG)
                        oT_t, oT_b = oT.next()
                        S.op("dve", lambda e: e.tensor_copy(out=oT_t[:], in_=pO_t[0:65, :]), reads=[pO_b], writes=[oT_b])
                        ptr_t, ptr_b = ptr.next()

                        def f_tr(e):
                            for i in range(4):
                                ins = e.transpose(ptr_t[:, i, 0:65], oT_t[:, i * 128:(i + 1) * 128], self.identf[0:65, 0:65])
                            return ins
                        S.op("pe", f_tr, reads=[oT_b, self.bidentf], writes=[ptr_b])
                        rc_t, rc_b = rc.next()
                        om_t, om_b = om.next()
                        S.op("dve", lambda e: e.reciprocal(out=rc_t[:], in_=ptr_t[:, :, 64]), reads=[ptr_b], writes=[rc_b])
                        S.op("dve", lambda e: e.tensor_tensor(out=om_t[:], in0=ptr_t[:, :, 0:64], in1=bcast_last(rc_t[:], 64), op=ALU.mult),
                             reads=[ptr_b, rc_b], writes=[om_b])
                        self.dma(self.OM[si][qi * 512:(qi + 1) * 512, h * 64:(h + 1) * 64].rearrange("(i p) c -> p i c", p=128), om_t[:],
                                 reads=[om_b], appends=[self.bOM], key=om_b.name)
                    for qi in range(Sq // 512):
                        do_q(qi)
            S.fence(self.bOM)
        S.barrier()

    def phase_F(self):
        S = self.S
        with ExitStack() as es:
            self.es = es
            Wo, bWo = self.sb("Wob", [128, 8, DM], BF16)
            self.load_w(Wo, bWo, "Wob", [(Wo[:, k, :], self.w_o_b[0, k * 128:(k + 1) * 128, :]) for k in range(8)])
            lnc = self.ln_consts(1, 0)
            om = Ring([self.sb("om%d" % i, [128, DM], BF16) for i in range(2)])
            xr = Ring([self.sb("xr%d" % i, [128, DM], F32) for i in range(2)])
            mT = Ring([self.sb("mT%d" % i, [128, 8, 128], BF16) for i in range(2)])
            zr = Ring([self.sb("z%d" % i, [128, DM], F32) for i in range(2)])
            o1 = Ring([self.sb("o1_%d" % i, [128, DM], F32) for i in range(2)])
            ob = Ring([self.sb("ob%d" % i, [128, DM], BF16) for i in range(2)])
            xTg = Ring([self.sb("xTg%d" % i, [128, 8, 512], BF16) for i in range(2)])
            tmps = Ring([self.ln_tmp() for i in range(2)])
            tp = Ring([self.ps("tp%d" % i, [128, 8, 128], BF16) for i in range(2)])
            pz = Ring([self.ps("pz%d" % i, [128, DM]) for i in range(2)])
            tiles = [(si, t) for si, Sq in enumerate(self.seqs) for t in range(Sq // 128)]

            def load(si, t):
                om_t, om_b = om.next()
                self.dma(om_t[:], self.OM[si][t * 128:(t + 1) * 128, :], reads=[self.bOM], writes=[om_b], key=om_b.name)
                xr_t, xr_b = xr.next()
                self.dma(xr_t[:], self.X2[si][t * 128:(t + 1) * 128, :], reads=[self.bX2], writes=[xr_b], key=xr_b.name)
                return om_t, om_b, xr_t, xr_b
            st = {}

            def do_tile(si, t, om_t, om_b, xr_t, xr_b):
                tp_t, tp_b = tp.next()
                mT_t, mT_b = mT.next()

                def f_tr(e):
                    for k in range(8):
                        ins = e.transpose(tp_t[:, k, :], om_t[:, k * 128:(k + 1) * 128], self.ident[:])
                    return ins
                S.op("pe", f_tr, reads=[om_b, self.bident], writes=[tp_b])
                S.op("act", lambda e: e.copy(out=mT_t[:], in_=tp_t[:]), reads=[tp_b], writes=[mT_b])
                o_t, o_b = o1.next()
                self.proj_ln_tile(Wo, bWo, 8, lambda k: mT_t[:, k, :], [mT_b], xr_t, xr_b, lnc, tmps, pz, zr, o_t, o_b)
                self.dma(self.X3[si][t * 128:(t + 1) * 128, :], o_t[:], reads=[o_b], appends=[self.bX3], key=o_b.name)
                if t % 4 == 0:
                    st["x"] = xTg.next()
                xTg_t, xTg_b = st["x"]
                self.emit_xT(o_t, o_b, ob, tp, xTg_t, xTg_b, t % 4, t % 4 == 0)
                if t % 4 == 3:
                    self.dma(self.X3T[si].rearrange("k p s -> p k s")[:, :, (t - 3) * 128:(t + 1) * 128], xTg_t[:], reads=[xTg_b], appends=[self.bX3T], key=xTg_b.name)
            nxt = load(*tiles[0])
            for ti, (si, t) in enumerate(tiles):
                cur = nxt
                if ti + 1 < len(tiles):
                    nxt = load(*tiles[ti + 1])
                do_tile(si, t, *cur)
            S.fence(self.bX3)
            S.fence(self.bX3T)
        S.barrier()

    def build(self, stop_after=None):
        with ExitStack() as es0:
            self.es = es0
            self.consts()
            self.es0 = es0
            steps = [("bias", self.phase_bias), ("A0", lambda: self.phase_A(0)), ("A1", lambda: self.phase_A(1)), ("A2", lambda: self.phase_A(2)),
                     ("B", self.phase_B), ("C1a", lambda: self.phase_C1(0, self.X1T, self.bX1T)),
                     ("C2a", lambda: self.phase_C2(0, self.X1, self.bX1, self.X2, self.bX2, self.X2T, self.bX2T)),
                     ("D", self.phase_D), ("E", self.phase_E), ("F", self.phase_F),
                     ("C1b", lambda: self.phase_C1(1, self.X3T, self.bX3T)),
                     ("C2b", lambda: self.phase_C2(1, self.X3, self.bX3, self.y, self.bY, None, None))]
            LAT = {"A0": 0.3, "A1": 0.3, "A2": 0.3}
            WIN = {"A0": 192, "A1": 192, "A2": 192}
            for name, fn in steps:
                self.S.lat = LAT.get(name, 2.0)
                self.S.win = WIN.get(name, 64)
                fn()
                if stop_after == name:
                    break
            self.S.barrier()
            self.stats = self.S.emit()
        return self.nc


def _t5_bucket_np(rel):
    nb = 16
    max_exact = 8
    rel = np.asarray(rel, np.int64)
    ret = np.where(rel > 0, nb, 0)
    n = np.abs(rel)
    nf = np.maximum(n, 1).astype(np.float32)
    large = max_exact + (np.log(nf / np.float32(max_exact)) / np.float32(math.log(1024 / max_exact)) * np.float32(nb - max_exact)).astype(np.int32)
    large = np.minimum(large, nb - 1)
    return ret + np.where(n < max_exact, n, large)


def _host_consts():
    oh = np.zeros((3, 32, 384), np.float32)
    ng = np.full((3, 384), NEGV, np.float32)
    n = np.arange(384)
    m = n - 127
    valid = (m >= 0) & (m <= 128)
    rel = 64 - m
    for g, d in enumerate(DILS):
        b = _t5_bucket_np(rel * d)
        for i in range(384):
            if valid[i]:
                oh[g, b[i], i] = 1.0
                ng[g, i] = 0.0
    inv = (1.0 / (np.float32(10000.0) ** (np.arange(0, 32, 2, dtype=np.float32) / np.float32(32)))).astype(np.float32)
    ang = (np.arange(SMAX, dtype=np.float32)[:, None] * inv[None, :]).astype(np.float32)
    cos = np.cos(ang.astype(np.float64)).astype(np.float32).T
    sin = np.sin(ang.astype(np.float64)).astype(np.float32).T
    rope = np.stack([np.concatenate([cos, cos], 0), np.concatenate([sin, sin], 0)], 0)
    return oh, ng, np.ascontiguousarray(rope)


_CACHE = {}


def run(inputs, seqs, n_cores, xs):
    key = tuple(seqs)
    if key not in _CACHE:
        _CACHE[key] = MK(list(seqs)).build()
    nc = _CACHE[key]
    oh, ng, rope = _host_consts()
    shared = {k: np.ascontiguousarray(np.asarray(inputs[k], np.float32)) for k in
              ("rel_bias", "w_qkv_a", "w_o_a", "w_dkv_b", "g_q_b", "g_kv_b", "w_uq_b", "w_ukv_b", "w_o_b",
               "ffn_w_in", "ffn_conv_w", "ffn_conv_b", "ffn_w_out", "ln_g", "ln_b")}
    shared["c_onehot"] = oh
    shared["c_negv"] = ng
    shared["c_rope"] = rope
    in_maps = []
    for c in range(n_cores):
        m = dict(shared)
        for i in range(len(seqs)):
            m["x%d" % i] = np.ascontiguousarray(xs[c][i])
        in_maps.append(m)
    res = run_bass_kernel_spmd(nc, in_maps, core_ids=list(range(n_cores)))
    return [[np.asarray(r["y%d" % i]) for i in range(len(seqs))] for r in res.results]


def kernel(**inputs):
    xp = np.asarray(inputs["x_prompt"], np.float32)
    xs_ = np.asarray(inputs["x_sample"], np.float32)
    n = 8
    outs = run(inputs, (xp.shape[1], xs_.shape[1]), n, [[xp[c], xs_[c]] for c in range(n)])
    yp = np.stack([outs[c][0] for c in range(n)], 0).astype(np.float32)
    ys = np.stack([outs[c][1] for c in range(n)], 0).astype(np.float32)
    return (yp, ys)
```
